# Optimizing a Trainium2 kernel written in Bass

```python
import jax, jax.numpy as jnp
from jax import lax
import numpy as np

D_MODEL = 2048
BATCH = 4
SEQ = 4096
DEPTH = 2

CTX_LEN = 256
GRID_W = 64
HEAD_DIM = 128
N_HG = 4
N_NA = 8
N_CV = 4
HG_W = N_HG * HEAD_DIM
NA_W = N_NA * HEAD_DIM
CV_W = N_CV * HEAD_DIM
MIX_W = HG_W + NA_W + CV_W
IN_SPLITS = (HG_W, HG_W, HG_W, NA_W, NA_W, HG_W, HG_W, NA_W, CV_W, CV_W, CV_W)
IN_W = sum(IN_SPLITS)
CTX_PIECES = 5
HG_CHUNK = 64
NA_WIN_R = 8
NA_WIN_C = 16
NA_QBLK_C = 16
NA_KEY_C = 32
CONV_W = 3
D_FF = 5632
EPS = 1e-6
F_FLOOR = 1e-30
ATTN_SCALE = HEAD_DIM ** -0.5
NEG_INF = -1e30

kernel_name = 'hybrid_hgrn2_natten_shortconv_dit_block'


def _rms_norm(x, w):
    xf = x.astype(jnp.float32)
    y = xf * lax.rsqrt(jnp.mean(xf * xf, axis=-1, keepdims=True) + EPS)
    return (y * w.astype(jnp.float32)).astype(x.dtype)


def _modulate(h, shift, scale):
    return h * (1 + scale) + shift


def _heads(a):
    return a.reshape(a.shape[:-1] + (a.shape[-1] // HEAD_DIM, HEAD_DIM))


def _merge(a):
    return a.reshape(a.shape[:-2] + (a.shape[-2] * a.shape[-1],))


def _flip(a):
    return a[:, ::-1]


def _split_cols(u, n_pieces):
    cuts = [int(v) for v in np.cumsum(IN_SPLITS[:n_pieces])[:-1]]
    return jnp.split(u, cuts, axis=-1)


def _dwconv(x, w):
    l_ = x.shape[1]
    pad = CONV_W // 2
    xp = jnp.pad(x, ((0, 0), (pad, pad), (0, 0)))
    out = xp[:, :l_] * w[0]
    for t in range(1, CONV_W):
        out = out + xp[:, t:t + l_] * w[t]
    return out


def _hgrn2_forget(z, lb):
    zf = z.astype(jnp.float32)
    f = lb + (1.0 - lb) * jax.nn.sigmoid(zf)
    log_f = jnp.log(jnp.maximum(f, F_FLOOR))
    k = (1.0 - lb) * jax.nn.sigmoid(-zf)
    return _heads(log_f), _heads(k)


def _hgrn2_scan(log_f, k, v, s0, q=None):
    b_, l_, h_, _ = log_f.shape
    n_chunks = l_ // HG_CHUNK

    def chunks(a):
        return a.reshape(b_, n_chunks, HG_CHUNK, h_, a.shape[-1]).transpose(1, 0, 3, 2, 4)

    tri = jnp.tril(jnp.ones((HG_CHUNK, HG_CHUNK), dtype=bool))[:, :, None]
    with_out = q is not None
    xs = (chunks(log_f), chunks(k), chunks(v)) + ((chunks(q),) if with_out else ())

    def step(s, inp):
        g, kc, vc = inp[0], inp[1], inp[2]
        cum = jnp.cumsum(g, axis=2)
        tot = cum[:, :, -1]
        s_new = jnp.exp(tot)[..., None] * s + jnp.einsum('bhcd,bhce->bhde', kc * jnp.exp(tot[:, :, None] - cum), vc)
        if not with_out:
            return s_new, None
        qc = inp[3]
        o_inter = jnp.einsum('bhcd,bhde->bhce', qc * jnp.exp(cum), s)
        diff = cum[:, :, :, None, :] - cum[:, :, None, :, :]
        decay = jnp.exp(jnp.where(tri, diff, NEG_INF))
        att = jnp.einsum('bhid,bhjd,bhijd->bhij', qc, kc, decay)
        return s_new, o_inter + jnp.einsum('bhij,bhje->bhie', att, vc)

    s_fin, o = lax.scan(step, s0, xs)
    if not with_out:
        return s_fin, None
    return s_fin, o.transpose(1, 0, 3, 2, 4).reshape(b_, l_, h_, v.shape[-1])


def _hgrn2_readout(o, g, w):
    return _merge(_rms_norm(o, w)).astype(g.dtype) * jax.nn.silu(g)


def _neighbourhood_attention(q, k, v, k_ctx, v_ctx, rpb, rows):
    b_, n_, h_, d_ = q.shape
    win_r = min(NA_WIN_R, rows)
    n_cb = GRID_W // NA_QBLK_C
    qcol = np.arange(GRID_W).reshape(n_cb, NA_QBLK_C)
    kstart = np.clip(np.arange(n_cb) * NA_QBLK_C - NA_WIN_C // 2, 0, GRID_W - NA_KEY_C)
    kcol = kstart[:, None] + np.arange(NA_KEY_C)[None]
    wstart = np.clip(qcol - NA_WIN_C // 2, 0, GRID_W - NA_WIN_C)
    col_ok = (kcol[:, None, :] >= wstart[:, :, None]) & (kcol[:, None, :] < wstart[:, :, None] + NA_WIN_C)
    mask = jnp.asarray(col_ok[:, :, None, :])
    idx_c = jnp.asarray(np.clip(kcol[:, None, :] - qcol[:, :, None] + NA_WIN_C - 1, 0, 2 * NA_WIN_C - 2)[:, :, None, :])
    qg = q.reshape(b_, rows, GRID_W, h_, d_)
    kg = k.reshape(b_, rows, GRID_W, h_, d_)
    vg = v.reshape(b_, rows, GRID_W, h_, d_)
    n_loc = win_r * NA_KEY_C

    def row_block(r):
        rs = jnp.clip(r - win_r // 2, 0, rows - win_r)
        qr = lax.dynamic_index_in_dim(qg, r, axis=1, keepdims=False).reshape(b_, n_cb, NA_QBLK_C, h_, d_)
        kr = lax.dynamic_slice_in_dim(kg, rs, win_r, axis=1)[:, :, kcol]
        vr = lax.dynamic_slice_in_dim(vg, rs, win_r, axis=1)[:, :, kcol]
        idx_r = (rs + jnp.arange(win_r) - r + NA_WIN_R - 1)[None, None, :, None]
        bias = rpb[:, idx_r, idx_c].astype(jnp.float32)
        s_loc = jnp.einsum('bmqhd,brmkhd->bhmqrk', qr, kr).astype(jnp.float32) * ATTN_SCALE
        s_loc = jnp.where(mask, s_loc + bias, NEG_INF)
        s_ctx = jnp.einsum('bmqhd,bkhd->bhmqk', qr, k_ctx).astype(jnp.float32) * ATTN_SCALE
        logits = jnp.concatenate([s_loc.reshape(s_loc.shape[:4] + (n_loc,)), s_ctx], axis=-1)
        p = jax.nn.softmax(logits, axis=-1).astype(v.dtype)
        p_loc = p[..., :n_loc].reshape(s_loc.shape)
        o = jnp.einsum('bhmqrk,brmkhd->bmqhd', p_loc, vr) + jnp.einsum('bhmqk,bkhd->bmqhd', p[..., n_loc:], v_ctx)
        return o.reshape(b_, GRID_W, h_, d_)

    o = lax.map(row_block, jnp.arange(rows))
    return jnp.moveaxis(o, 0, 1).reshape(b_, n_, h_, d_)


def _context_attention(q, k, v):
    s = jnp.einsum('bqhd,bkhd->bhqk', q, k).astype(jnp.float32) * ATTN_SCALE
    p = jax.nn.softmax(s, axis=-1).astype(v.dtype)
    return jnp.einsum('bhqk,bkhd->bqhd', p, v)


def _short_conv(b_gate, c_gate, v, w):
    return b_gate * _dwconv(c_gate * v, w)


def _conv_ffn(h, w_up, cw, cb, w_down):
    u = _dwconv(h @ w_up, cw) + cb
    gate, val = jnp.split(u, 2, axis=-1)
    return (jax.nn.silu(gate) * val) @ w_down


def _layer(x, ctx, ada, ada_ctx, lb_fw, lb_bw, ln1, ln2, w_in, hg_norm, q_norm, k_norm, rpb, na_onorm,
           cv_w, cv_onorm, w_out, w_up, f_cw, f_cb, w_down, rows, last):
    sh1, sc1, g1, sh2, sc2, g2 = jnp.split(ada[:, None, :], 6, axis=-1)
    cm = jnp.split(ada_ctx, 2 if last else 6)
    h = _modulate(_rms_norm(x, ln1), sh1, sc1)
    hc = _modulate(_rms_norm(ctx, ln1), cm[0], cm[1])
    (f_fw, f_bw, i_v, na_k, na_v, hg_q, hg_g, na_q, cv_b, cv_c, cv_v) = _split_cols(h @ w_in, len(IN_SPLITS))
    n_c = CTX_PIECES if last else len(IN_SPLITS)
    uc = _split_cols(hc @ w_in[:, :sum(IN_SPLITS[:n_c])], n_c)

    s0 = jnp.zeros((x.shape[0], N_HG, HEAD_DIM, HEAD_DIM), jnp.float32)
    cf_fw, ck_fw = _hgrn2_forget(uc[0], lb_fw)
    cf_bw, ck_bw = _hgrn2_forget(uc[1], lb_bw)
    c_val = _heads(uc[2].astype(jnp.float32))
    c_q = None if last else _heads(jax.nn.silu(uc[5]).astype(jnp.float32))
    s_fw, co_fw = _hgrn2_scan(cf_fw, ck_fw, c_val, s0, c_q)
    s_bw, co_bw = _hgrn2_scan(_flip(cf_bw), _flip(ck_bw), _flip(c_val), s0, None if last else _flip(c_q))
    lf_fw, lk_fw = _hgrn2_forget(f_fw, lb_fw)
    lf_bw, lk_bw = _hgrn2_forget(f_bw, lb_bw)
    l_val = _heads(i_v.astype(jnp.float32))
    l_q = _heads(jax.nn.silu(hg_q).astype(jnp.float32))
    _, o_fw = _hgrn2_scan(lf_fw, lk_fw, l_val, s_fw, l_q)
    _, o_bw = _hgrn2_scan(_flip(lf_bw), _flip(lk_bw), _flip(l_val), s_bw, _flip(l_q))
    hg_out = _hgrn2_readout(o_fw + _flip(o_bw), hg_g, hg_norm)

    k_c = _rms_norm(_heads(uc[3]), k_norm)
    v_c = _heads(uc[4])
    na_o = _neighbourhood_attention(_rms_norm(_heads(na_q), q_norm), _rms_norm(_heads(na_k), k_norm),
                                    _heads(na_v), k_c, v_c, rpb, rows)
    na_onorm_h = na_onorm.reshape(N_NA, HEAD_DIM)
    na_out = _merge(_rms_norm(na_o, na_onorm_h))

    cv_onorm_h = cv_onorm.reshape(N_CV, HEAD_DIM)
    cv_out = _merge(_rms_norm(_heads(_short_conv(cv_b, cv_c, cv_v, cv_w)), cv_onorm_h))

    x = x + g1 * (jnp.concatenate([hg_out, na_out, cv_out], axis=-1) @ w_out)
    x = x + g2 * _conv_ffn(_modulate(_rms_norm(x, ln2), sh2, sc2), w_up, f_cw, f_cb, w_down)
    if last:
        return x, None

    hg_c = _hgrn2_readout(co_fw + _flip(co_bw), uc[6], hg_norm)
    na_c = _merge(_rms_norm(_context_attention(_rms_norm(_heads(uc[7]), q_norm), k_c, v_c), na_onorm_h))
    cv_co = _merge(_rms_norm(_heads(_short_conv(uc[8], uc[9], uc[10], cv_w)), cv_onorm_h))
    ctx = ctx + cm[2] * (jnp.concatenate([hg_c, na_c, cv_co], axis=-1) @ w_out)
    ctx = ctx + cm[5] * _conv_ffn(_modulate(_rms_norm(ctx, ln2), cm[3], cm[4]), w_up, f_cw, f_cb, w_down)
    return x, ctx


def setup_inputs(seed: int = 0) -> dict:
    key = jax.random.key(seed)
    ks = jax.random.split(key, 24)
    d = D_MODEL

    def nrm(k, shape, scale):
        return jax.random.normal(k, shape, jnp.float32) * scale

    return {
        'x': nrm(ks[0], (BATCH, SEQ, d), 1.0),
        'c': nrm(ks[1], (BATCH, d), 1.0),
        'ctx': nrm(ks[2], (BATCH, CTX_LEN, d), 1.0),
        'c_ctx': nrm(ks[3], (d,), 1.0),
        'w_ada': nrm(ks[4], (DEPTH, d, 6 * d), 0.5 * d ** -0.5),
        'b_ada': nrm(ks[5], (DEPTH, 6 * d), 0.01),
        'ln1_w': 1.0 + nrm(ks[6], (DEPTH, d), 0.02),
        'ln2_w': 1.0 + nrm(ks[7], (DEPTH, d), 0.02),
        'w_in': nrm(ks[8], (DEPTH, d, IN_W), d ** -0.5),
        'hg_lb_logits': nrm(ks[9], (2, DEPTH, HG_W), 0.5),
        'hg_norm_w': 1.0 + nrm(ks[10], (DEPTH, HEAD_DIM), 0.02),
        'na_q_norm_w': 1.0 + nrm(ks[11], (DEPTH, HEAD_DIM), 0.02),
        'na_k_norm_w': 1.0 + nrm(ks[12], (DEPTH, HEAD_DIM), 0.02),
        'na_rpb': nrm(ks[13], (DEPTH, N_NA, 2 * NA_WIN_R - 1, 2 * NA_WIN_C - 1), 0.1),
        'na_out_norm_w': 1.0 + nrm(ks[14], (DEPTH, NA_W), 0.02),
        'cv_w': nrm(ks[15], (DEPTH, CONV_W, CV_W), CONV_W ** -0.5),
        'cv_out_norm_w': 1.0 + nrm(ks[16], (DEPTH, CV_W), 0.02),
        'w_out': nrm(ks[17], (DEPTH, MIX_W, d), MIX_W ** -0.5),
        'w_up': nrm(ks[18], (DEPTH, d, 2 * D_FF), d ** -0.5),
        'ffn_conv_w': nrm(ks[19], (DEPTH, CONV_W, 2 * D_FF), CONV_W ** -0.5),
        'ffn_conv_b': nrm(ks[20], (DEPTH, 2 * D_FF), 0.01),
        'w_down': nrm(ks[21], (DEPTH, D_FF, d), D_FF ** -0.5),
    }


def reference(x, c, ctx, c_ctx, w_ada, b_ada, ln1_w, ln2_w, w_in, hg_lb_logits, hg_norm_w, na_q_norm_w,
              na_k_norm_w, na_rpb, na_out_norm_w, cv_w, cv_out_norm_w, w_out, w_up, ffn_conv_w, ffn_conv_b, w_down):
    rows = x.shape[1] // GRID_W
    lb_sm = jax.nn.softmax(hg_lb_logits.astype(jnp.float32), axis=1)
    lb_all = jnp.cumsum(lb_sm, axis=1) - lb_sm[:, :1]
    silu_c = jax.nn.silu(c)
    silu_cc = jax.nn.silu(c_ctx)
    for l in range(DEPTH):
        last = l == DEPTH - 1
        ada = silu_c @ w_ada[l] + b_ada[l]
        n_ada = (2 if last else 6) * D_MODEL
        ada_ctx = silu_cc @ w_ada[l][:, :n_ada] + b_ada[l][:n_ada]
        x, ctx = _layer(x, ctx, ada, ada_ctx, lb_all[0, l], lb_all[1, l], ln1_w[l], ln2_w[l], w_in[l],
                        hg_norm_w[l], na_q_norm_w[l], na_k_norm_w[l], na_rpb[l], na_out_norm_w[l], cv_w[l],
                        cv_out_norm_w[l], w_out[l], w_up[l], ffn_conv_w[l], ffn_conv_b[l], w_down[l], rows, last)
    return x
```

```python
import contextlib
import numpy as np
import concourse.bass as bass
import concourse.mybir as mybir
from concourse.bass_utils import run_bass_kernel_spmd

F32 = mybir.dt.float32
BF16 = mybir.dt.bfloat16
AF = mybir.ActivationFunctionType
ALU = mybir.AluOpType

D = 2048
DEPTH = 2
CTX = 256
SEQ = 4096
T = CTX + SEQ
GW = 64
ROWS = SEQ // GW
HD = 128
N_HG, N_NA, N_CV = 4, 8, 4
IN_W = 7168
D_FF = 5632
EPS = 1e-6
CH = 64
NT = T // 128

OFF_FFW, OFF_FBW, OFF_HI, OFF_NK, OFF_NV, OFF_HQ, OFF_HGG, OFF_NQ, OFF_CB, OFF_CC, OFF_CVV = (
    0, 512, 1024, 1536, 2560, 3584, 4096, 4608, 5632, 6144, 6656)

V_BADA = 0
V_LN1 = V_BADA + DEPTH * 96
V_LN2 = V_LN1 + DEPTH * 16
V_HGN = V_LN2 + DEPTH * 16
V_QN = V_HGN + DEPTH
V_KN = V_QN + DEPTH
V_NAON = V_KN + DEPTH
V_CVW = V_NAON + DEPTH * 8
V_CVON = V_CVW + DEPTH * 12
V_FCW = V_CVON + DEPTH * 4
V_FCB = V_FCW + DEPTH * 264
V_CT = V_FCB + DEPTH * 88
NVEC = V_CT + 32

C_ID = 0
C_ONES = 128
C_MEXT_F = 256
C_MEXT_B = C_MEXT_F + 66
C_MC_F = C_MEXT_B + 66
C_MC_B = C_MC_F + 64
C_MK_F = C_MC_B + 64
C_MK_B = C_MK_F + 64
C_MCC_F = C_MK_B + 64
C_MCC_B = C_MCC_F + 128
C_MKR_F = C_MCC_B + 128
C_MKR_B = C_MKR_F + 256
NCONST = C_MKR_B + 256


class Sched:
    def __init__(self, nc, es):
        self.nc = nc
        self.eng = {'pe': nc.tensor, 'act': nc.scalar, 'dve': nc.vector, 'pool': nc.gpsimd, 'sp': nc.sync}
        self.sem = {k: es.enter_context(nc.semaphore("sem_" + k)) for k in ('pe', 'act', 'dve', 'pool')}
        self.count = {k: 0 for k in self.sem}
        self.NDS = 40
        self.dsem = [es.enter_context(nc.semaphore("dsem%d" % i)) for i in range(self.NDS)]
        self.dval = [0] * self.NDS
        self.dk = 0
        self.seen = {k: {} for k in self.eng}
        self.res = {}
        self.ninstr = 0
        self.csem = {}
        self.ccnt = {}
        self.es = es

    def _wait(self, en, tok):
        kind, a, v = tok
        if kind == 'e':
            if a == en and en == 'pe':
                return
            key = ('e', a)
            sem = self.sem[a]
        elif kind == 'c':
            key = ('c', a)
            sem = self.csem[a]
            v = 16 * self.ccnt[a]
        else:
            key = ('d', a)
            sem = self.dsem[a]
        if self.seen[en].get(key, 0) >= v:
            return
        self.eng[en].wait_ge(sem, v)
        self.seen[en][key] = v

    def _deps(self, en, reads, writes):
        deps = []
        for k in reads:
            r = self.res.get(k)
            if r and r['w']:
                deps.append(r['w'])
        for k in writes:
            r = self.res.get(k)
            if r:
                deps.extend(r['r'])
                if r['w']:
                    deps.append(r['w'])
        for tok in deps:
            self._wait(en, tok)

    def _update(self, tok, reads, writes):
        for k in reads:
            r = self.res.setdefault(k, {'w': None, 'r': []})
            if tok[0] == 'e':
                r['r'] = [t for t in r['r'] if not (t[0] == 'e' and t[1] == tok[1])]
            r['r'].append(tok)
        for k in writes:
            self.res[k] = {'w': tok, 'r': []}

    def op(self, en, fn, reads=(), writes=(), inc=True):
        self._deps(en, reads, writes)
        ins = fn(self.eng[en])
        self.ninstr += 1
        if inc:
            self.count[en] += 1
            ins.then_inc(self.sem[en], 1)
            tok = ('e', en, self.count[en])
        else:
            tok = ('e', en, self.count[en] + 1)
        self._update(tok, reads, writes)
        return tok

    def dma(self, en, out, in_, reads=(), writes=(), **kw):
        slot = self.dk % self.NDS
        self.dk += 1
        if self.dval[slot] > 0:
            self._wait(en, ('d', slot, self.dval[slot]))
        self._deps(en, reads, writes)
        self.dval[slot] += 16
        self.eng[en].dma_start(out=out, in_=in_, **kw).then_inc(self.dsem[slot], 16)
        self.ninstr += 1
        tok = ('d', slot, self.dval[slot])
        self._update(tok, reads, writes)
        return tok

    def cast_dma(self, en, group, out, in_):
        if group not in self.csem:
            self.csem[group] = self.es.enter_context(self.nc.semaphore("csem_" + group))
            self.ccnt[group] = 0
        self.ccnt[group] += 1
        self.eng[en].dma_start(out=out, in_=in_).then_inc(self.csem[group], 16)
        self.ninstr += 1

    def barrier(self):
        for en in self.eng:
            for a in self.sem:
                if a != en and self.count[a] > 0:
                    self._wait(en, ('e', a, self.count[a]))
            for s in range(self.NDS):
                if self.dval[s] > 0:
                    self._wait(en, ('d', s, self.dval[s]))
        self.res = {k: v for k, v in self.res.items() if k.startswith('WB_')}

    def final(self):
        for g in self.csem:
            self._wait('sp', ('c', g, 0))
        for s in range(self.NDS):
            if self.dval[s] > 0:
                self._wait('sp', ('d', s, self.dval[s]))
        for a in self.sem:
            if self.count[a] > 0:
                self._wait('sp', ('e', a, self.count[a]))


def _split(n, m):
    k = (n + m - 1) // m
    base = n // k
    rem = n - base * k
    out = []
    o = 0
    for i in range(k):
        s = base + (1 if i < rem else 0)
        out.append((o, s))
        o += s
    return out


def build(stage=99, dbg=None, phases=('hg', 'na', 'cv')):
    nc = bass.Bass("TRN2", target_bir_lowering=False)
    dbg = dbg or []
    _uid = [0]
    _sb, _pt = nc.sbuf_tensor, nc.psum_tensor

    def sbuf_u(name, shape, dt):
        _uid[0] += 1
        return _sb("%s_u%d" % (name, _uid[0]), shape, dt)

    def psum_u(name, shape, dt):
        _uid[0] += 1
        return _pt("%s_u%d" % (name, _uid[0]), shape, dt)
    outs = {}

    def dram_in(name, shape, dt=F32):
        return nc.dram_tensor(name, list(shape), dt, kind="ExternalInput").ap()

    def dram_tmp(name, shape, dt):
        kind = "ExternalOutput" if name in dbg else "Internal"
        return nc.dram_tensor(name, list(shape), dt, kind=kind).ap()

    xin = dram_in("xin", [T, D])
    vecs = dram_in("vecs", [128, NVEC])
    consts = dram_in("consts", [128, NCONST])
    lbl = dram_in("lbl", [64, 2 * DEPTH * 512])
    natab = dram_in("natab", [DEPTH, N_NA, 128, 14 * 64])
    w_ada = dram_in("w_ada", [DEPTH, D, 6 * D])
    w_in = dram_in("w_in", [DEPTH, D, IN_W])
    w_out = dram_in("w_out", [DEPTH, D, D])
    w_up = dram_in("w_up", [DEPTH, D, 2 * D_FF])
    w_down = dram_in("w_down", [DEPTH, D_FF, D])
    y = nc.dram_tensor("y", [SEQ, D], F32, kind="ExternalOutput").ap()

    wb_in = [dram_tmp("wb_in%d" % l, [D, IN_W], BF16) for l in range(DEPTH)]
    wb_out = [dram_tmp("wb_out%d" % l, [D, D], BF16) for l in range(DEPTH)]
    wb_up = [dram_tmp("wb_up%d" % l, [D, 2 * D_FF], BF16) for l in range(DEPTH)]
    wb_down = [dram_tmp("wb_down%d" % l, [D_FF, D], BF16) for l in range(DEPTH)]
    hF = dram_tmp("hF", [D, T], BF16)
    U_f = dram_tmp("U_f", [T, 1024], F32)
    U_i = dram_tmp("U_i", [T, 512], BF16)
    U_nv = dram_tmp("U_nv", [T + 64, 1024], BF16)
    U_F = dram_tmp("U_F", [IN_W, T], BF16)
    U_k = dram_tmp("U_k", [T, 1024], F32)
    U_qs = dram_tmp("U_qs", [512, T], F32)
    mixF = dram_tmp("mixF", [D, T], BF16)
    actF = dram_tmp("actF", [D_FF, T], BF16)
    X1 = dram_tmp("X1", [T, D], F32)
    X2 = dram_tmp("X2", [T, D], F32)

    es = contextlib.ExitStack()
    with es:
        S = Sched(nc, es)
        op, dma = S.op, S.dma

        vt = es.enter_context(sbuf_u("vt", [128, NVEC], F32))
        ct = es.enter_context(sbuf_u("ct", [128, NCONST], F32))
        ctb = es.enter_context(sbuf_u("ctb", [128, NCONST], BF16))
        adaT = es.enter_context(sbuf_u("adaT", [128, 96 * 2], F32))
        modv = es.enter_context(sbuf_u("modv", [128, 4 * 16 * 2], F32))
        grow = es.enter_context(sbuf_u("grow", [128, 4 * D], F32))
        silc = es.enter_context(sbuf_u("silc", [128, 32], F32))
        epst = es.enter_context(sbuf_u("epst", [128, 1], F32))
        onet = es.enter_context(sbuf_u("onet", [128, 1], F32))
        lbt = es.enter_context(sbuf_u("lbt", [128, 2 * 512], F32))
        omt = es.enter_context(sbuf_u("omt", [128, 2 * 512], F32))
        dma('sp', vt[:], vecs[:, :], writes=['vt'])
        dma('sp', ct[:], consts[:, :], writes=['ct'])
        op('dve', lambda e: e.tensor_copy(out=ctb[:], in_=ct[:]), reads=['ct'], writes=['ctb'])
        op('dve', lambda e: e.memset(epst[:], EPS), writes=['epst'])
        op('dve', lambda e: e.memset(onet[:], 1.0), writes=['onet'])
        op('act', lambda e: e.activation(out=silc[:], in_=vt[:, V_CT:V_CT + 32], func=AF.Silu), reads=['vt'], writes=['silc'])
        ident = ct[:, C_ID:C_ID + 128]
        identb = ctb[:, C_ID:C_ID + 128]
        onesb = ctb[:, C_ONES:C_ONES + 128]
        ones = ct[:, C_ONES:C_ONES + 128]

        def cast_weight(src, dst, rows, key):
            step = 512
            for r0 in range(0, rows, step):
                r1 = min(rows, r0 + step)
                S.cast_dma('pool', key, dst[r0:r1, :], src[r0:r1, :])
            S.res['WB_' + key] = {'w': ('c', key, 0), 'r': []}
            return ['WB_' + key]

        wkeys = {}
        for l in range(DEPTH):
            for nm in ('in', 'out', 'up', 'down'):
                wkeys[nm, l] = ['WB_wb%s%d_' % (nm if nm != 'down' else 'dn', l)]
        cast_plan = {
            'start': [('in', 0), ('out', 0)],
            'mix0': [('up', 0), ('down', 0), ('in', 1), ('out', 1)],
            'mix1': [('up', 1), ('down', 1)],
        }

        def issue_casts(tag):
            for (nm, l) in cast_plan[tag]:
                src = {'in': w_in, 'out': w_out, 'up': w_up, 'down': w_down}[nm][l]
                dst = {'in': wb_in, 'out': wb_out, 'up': wb_up, 'down': wb_down}[nm][l]
                rows = D_FF if nm == 'down' else D
                cast_weight(src, dst, rows, "wb%s%d_" % (nm if nm != 'down' else 'dn', l))

        issue_casts('start')

        def phase_ada(l):
            with contextlib.ExitStack() as ps:
                wsl = [ps.enter_context(sbuf_u("adaw%d" % i, [128, 16 * 512], F32)) for i in range(2)]
                pacc = [ps.enter_context(psum_u("adap%d" % i, [128, 512], F32)) for i in range(2)]
                wv = w_ada[l].rearrange("(kc p) n -> p kc n", p=128)
                for sl in range(24):
                    wt = wsl[sl % 2]
                    wk = "adaw%d" % (sl % 2)
                    dma('sp', wt[:].rearrange("p (kc n) -> p kc n", kc=16), wv[:, :, sl * 512:(sl + 1) * 512], writes=[wk])
                    pa = pacc[sl % 2]
                    pk = "adap%d" % (sl % 2)
                    for c4 in range(4):
                        for kc in range(16):
                            op('pe', lambda e, c4=c4, kc=kc: e.matmul(
                                pa[:, c4 * 2:c4 * 2 + 2], lhsT=wt[:, kc * 512 + c4 * 128: kc * 512 + c4 * 128 + 128],
                                rhs=silc[:, kc * 2:kc * 2 + 2], start=(kc == 0), stop=(kc == 15)),
                               reads=[wk, 'silc'], writes=[pk], inc=(kc == 15))
                    for c4 in range(4):
                        j = sl * 4 + c4
                        op('dve', lambda e, c4=c4, j=j: e.tensor_scalar(
                            out=adaT[:, j * 2:j * 2 + 2], in0=pa[:, c4 * 2:c4 * 2 + 2],
                            scalar1=vt[:, V_BADA + l * 96 + j: V_BADA + l * 96 + j + 1], scalar2=None, op0=ALU.add),
                           reads=[pk, 'vt'], writes=['adaT'])
                for n, (lnoff, shc, scc) in enumerate([(V_LN1, 0, 16), (V_LN2, 48, 64)]):
                    for fc in range(16):
                        a_out = modv[:, (n * 2) * 32 + fc * 2:(n * 2) * 32 + fc * 2 + 2]
                        b_out = modv[:, (n * 2 + 1) * 32 + fc * 2:(n * 2 + 1) * 32 + fc * 2 + 2]
                        op('dve', lambda e, a_out=a_out, fc=fc, scc=scc, lnoff=lnoff: e.tensor_scalar(
                            out=a_out, in0=adaT[:, (scc + fc) * 2:(scc + fc) * 2 + 2], scalar1=1.0,
                            scalar2=vt[:, lnoff + l * 16 + fc:lnoff + l * 16 + fc + 1], op0=ALU.add, op1=ALU.mult),
                           reads=['adaT', 'vt'], writes=['modv'])
                        op('dve', lambda e, b_out=b_out, fc=fc, shc=shc: e.tensor_copy(
                            out=b_out, in_=adaT[:, (shc + fc) * 2:(shc + fc) * 2 + 2]),
                           reads=['adaT'], writes=['modv'])
                dg = ps.enter_context(sbuf_u("dg", [128, 2 * 128], F32))
                pg = [ps.enter_context(psum_u("pg%d" % i, [128, 512], F32)) for i in range(2)]
                cnt = 0
                for gi, gch in enumerate([32, 80]):
                    for which in range(2):
                        for f4 in range(4):
                            pgt = pg[cnt % 2]
                            pgk = "pg%d" % (cnt % 2)
                            for q in range(4):
                                fc = f4 * 4 + q
                                dsl = dg[:, (q % 2) * 128:(q % 2) * 128 + 128]
                                dk_ = "dg%d" % (q % 2)
                                col = (gch + fc) * 2 + which
                                op('dve', lambda e, dsl=dsl, col=col: e.tensor_scalar(
                                    out=dsl, in0=ident, scalar1=adaT[:, col:col + 1], scalar2=None, op0=ALU.mult),
                                   reads=['adaT', 'ct'], writes=[dk_])
                                op('pe', lambda e, dsl=dsl, q=q, pgt=pgt: e.matmul(
                                    pgt[:, q * 128:(q + 1) * 128], lhsT=ones, rhs=dsl, start=True, stop=True),
                                   reads=[dk_, 'ct'], writes=[pgk])
                            base = (gi * 2 + which) * D + f4 * 512
                            op('act', lambda e, base=base, pgt=pgt: e.activation(
                                out=grow[:, base:base + 512], in_=pgt[:], func=AF.Copy),
                               reads=[pgk], writes=['grow'])
                            cnt += 1
            S.barrier()

        def phase_norm(src, n, groups):
            with contextlib.ExitStack() as ps:
                xt = [ps.enter_context(sbuf_u("nx%d" % i, [128, D], F32)) for i in range(2)]
                sq = ps.enter_context(sbuf_u("nsq", [128, D], BF16))
                st = [ps.enter_context(sbuf_u("nst%d" % i, [128, 4], F32)) for i in range(2)]
                hs = [ps.enter_context(sbuf_u("nh%d" % i, [128, 16 * 512], BF16)) for i in range(2)]
                pt = [ps.enter_context(psum_u("npt%d" % i, [128, 512], F32)) for i in range(4)]
                hFv = hF.rearrange("(fc p) t -> p fc t", p=128)
                ti = 0
                pi = 0
                for gi, (t0, ntl, which) in enumerate(groups):
                    hst = hs[gi % 2]
                    hk = "nh%d" % (gi % 2)
                    hview = hst[:].rearrange("p (fc t) -> p fc t", fc=16)
                    for s in range(ntl):
                        x_ = xt[ti % 2]
                        xk = "nx%d" % (ti % 2)
                        s_ = st[ti % 2]
                        sk = "nst%d" % (ti % 2)
                        ti += 1
                        tt = t0 + s * 128
                        dma('sp', x_[:], src[tt:tt + 128, :], writes=[xk])
                        op('act', lambda e, x_=x_, s_=s_: e.activation(out=sq[:], in_=x_[:], func=AF.Square, accum_out=s_[:, 0:1]),
                           reads=[xk], writes=['nsq', sk])
                        op('act', lambda e, s_=s_: e.activation(out=s_[:, 1:2], in_=s_[:, 0:1], func=AF.Ln, bias=epst[:, 0:1], scale=1.0 / D),
                           reads=[sk, 'epst'], writes=[sk])
                        op('act', lambda e, s_=s_: e.activation(out=s_[:, 2:3], in_=s_[:, 1:2], func=AF.Exp, scale=-0.5), reads=[sk], writes=[sk])
                        op('dve', lambda e, x_=x_, s_=s_: e.tensor_scalar(out=x_[:], in0=x_[:], scalar1=s_[:, 2:3], scalar2=None, op0=ALU.mult),
                           reads=[xk, sk], writes=[xk])
                        for f4 in range(4):
                            p_ = pt[pi % 4]
                            pk = "npt%d" % (pi % 4)
                            pi += 1
                            for q in range(4):
                                fc = f4 * 4 + q
                                op('pe', lambda e, p_=p_, q=q, fc=fc, x_=x_: e.transpose(
                                    out=p_[:, q * 128:(q + 1) * 128], in_=x_[:, fc * 128:(fc + 1) * 128], identity=ident),
                                   reads=[xk, 'ct'], writes=[pk], inc=(q == 3))
                            for q in range(4):
                                fc = f4 * 4 + q
                                acol = (n * 2) * 32 + fc * 2 + which
                                bcol = (n * 2 + 1) * 32 + fc * 2 + which
                                eng = 'act' if q % 2 == 0 else 'dve'
                                if eng == 'act':
                                    op('act', lambda e, p_=p_, q=q, fc=fc, s=s, acol=acol, bcol=bcol: e.activation(
                                        out=hview[:, fc, s * 128:(s + 1) * 128], in_=p_[:, q * 128:(q + 1) * 128], func=AF.Identity,
                                        scale=modv[:, acol:acol + 1], bias=modv[:, bcol:bcol + 1]),
                                       reads=[pk, 'modv'], writes=[hk])
                                else:
                                    op('dve', lambda e, p_=p_, q=q, fc=fc, s=s, acol=acol, bcol=bcol: e.tensor_scalar(
                                        out=hview[:, fc, s * 128:(s + 1) * 128], in0=p_[:, q * 128:(q + 1) * 128],
                                        scalar1=modv[:, acol:acol + 1], scalar2=modv[:, bcol:bcol + 1], op0=ALU.mult, op1=ALU.add),
                                       reads=[pk, 'modv'], writes=[hk])
                    dma('pool', hFv[:, :, t0:t0 + ntl * 128], hview[:, :, 0:ntl * 128], reads=[hk], writes=['hF'])
            S.barrier()

        def gemm(A, K, Wb, wkeylist, col0, ncols, tblocks, mode, epi, halo=False, name="g", abufs=2):
            KC = K // 128
            with contextlib.ExitStack() as ps:
                maxn = max(n for _, n in tblocks)
                ab = [ps.enter_context(sbuf_u("%sa%d" % (name, i), [128, KC * maxn], BF16)) for i in range(abufs)]
                wsb = [ps.enter_context(sbuf_u("%sw%d" % (name, i), [128, KC * 512], BF16)) for i in range(2)]
                pp = [ps.enter_context(psum_u("%sp%d" % (name, i), [128, 512], F32)) for i in range(8)]
                ctx = epi('alloc', ps)
                Av = A.rearrange("(kc p) t -> p kc t", p=128)
                Wv = Wb.rearrange("(kc p) n -> p kc n", p=128)
                wi = 0
                pidx = 0
                def load_A(bi):
                    t0, n = tblocks[bi]
                    a_ = ab[bi % abufs]
                    ak = "%sa%d" % (name, bi % abufs)
                    av = a_[:, 0:KC * n].rearrange("p (kc t) -> p kc t", kc=KC)
                    for k0 in range(0, KC, 8):
                        k1 = min(KC, k0 + 8)
                        dma('sp', av[:, k0:k1, :], Av[:, k0:k1, t0:t0 + n], reads=['A_' + name], writes=[ak + "_%d" % k0])

                load_A(0)
                for bi, (t0, n) in enumerate(tblocks):
                    a_ = ab[bi % abufs]
                    ak = "%sa%d" % (name, bi % abufs)
                    if abufs == 1 and bi > 0:
                        load_A(bi)
                    av = a_[:, 0:KC * n].rearrange("p (kc t) -> p kc t", kc=KC)
                    akeys = [ak + "_%d" % k0 for k0 in range(0, KC, 8)]
                    for c0 in range(col0, col0 + ncols, 512):
                        w_ = wsb[wi % 2]
                        wk = "%sw%d" % (name, wi % 2)
                        wi += 1
                        wv = w_[:].rearrange("p (kc n) -> p kc n", kc=KC)
                        for k0 in range(0, KC, 8):
                            k1 = min(KC, k0 + 8)
                            dma('sp', wv[:, k0:k1, :], Wv[:, k0:k1, c0:c0 + 512], reads=wkeylist, writes=[wk + "_%d" % k0])
                        wks = [wk + "_%d" % k0 for k0 in range(0, KC, 8)]
                        if abufs > 1 and c0 == col0 and bi + 1 < len(tblocks):
                            load_A(bi + 1)
                        if mode == 'F':
                            pieces = _split(n, 512)
                            for c4 in range(4):
                                banks = []
                                for (o, sz) in pieces:
                                    p_ = pp[pidx % 8]
                                    pk = "%sp%d" % (name, pidx % 8)
                                    pidx += 1
                                    for kc in range(KC):
                                        op('pe', lambda e, p_=p_, sz=sz, o=o, kc=kc, c4=c4, wv=wv, av=av: e.matmul(
                                            p_[:, 0:sz], lhsT=wv[:, kc, c4 * 128:(c4 + 1) * 128], rhs=av[:, kc, o:o + sz],
                                            start=(kc == 0), stop=(kc == KC - 1)),
                                           reads=[wks[kc // 8], akeys[kc // 8]], writes=[pk], inc=(kc == KC - 1))
                                    banks.append((p_, pk, o, sz))
                                epi('F', ctx, banks, c0 + c4 * 128, t0, n)
                        else:
                            for s in range(n // 128):
                                p_ = pp[pidx % 8]
                                pk = "%sp%d" % (name, pidx % 8)
                                pidx += 1
                                for kc in range(KC):
                                    op('pe', lambda e, p_=p_, kc=kc, s=s, wv=wv, av=av: e.matmul(
                                        p_[:, :], lhsT=av[:, kc, s * 128:(s + 1) * 128], rhs=wv[:, kc, :],
                                        start=(kc == 0), stop=(kc == KC - 1)),
                                       reads=[wks[kc // 8], akeys[kc // 8]], writes=[pk], inc=(kc == KC - 1))
                                epi('T', ctx, (p_, pk), c0, t0 + s * 128, 128)
            S.barrier()

        def lb_setup(l):
            if l == 0:
                op('dve', lambda e: e.memset(lbt[:], 0.0), writes=['lbt'])
                op('dve', lambda e: e.memset(omt[:], 1.0), writes=['omt'])
            else:
                with contextlib.ExitStack() as ps:
                    lg = ps.enter_context(sbuf_u("lg", [128, 2 * DEPTH * 512], F32))
                    dma('sp', lg[0:64, :], lbl[:, :], writes=['lg'])
                    dma('sp', lg[64:128, :], lbl[:, :], writes=['lg2'])
                    for dr in range(2):
                        op('dve', lambda e, dr=dr: e.tensor_tensor(out=lbt[:, dr * 512:(dr + 1) * 512], in0=lg[:, (dr * 2) * 512:(dr * 2 + 1) * 512],
                                                                 in1=lg[:, (dr * 2 + 1) * 512:(dr * 2 + 2) * 512], op=ALU.subtract), reads=['lg', 'lg2'], writes=['lbt'])
                    op('act', lambda e: e.activation(out=lbt[:], in_=lbt[:], func=AF.Exp), reads=['lbt'], writes=['lbt'])
                    op('dve', lambda e: e.tensor_scalar(out=lbt[:], in0=lbt[:], scalar1=1.0, scalar2=None, op0=ALU.add), reads=['lbt'], writes=['lbt'])
                    op('dve', lambda e: e.reciprocal(out=lbt[:], in_=lbt[:]), reads=['lbt'], writes=['lbt'])
                    op('dve', lambda e: e.tensor_scalar(out=omt[:], in0=lbt[:], scalar1=-1.0, scalar2=1.0, op0=ALU.mult, op1=ALU.add),
                       reads=['lbt'], writes=['omt'])
                    S.barrier()

        def epi_inproj(kind, *a):
            if kind == 'alloc':
                ps = a[0]
                c = {}
                c['sf'] = [ps.enter_context(sbuf_u("ipf%d" % i, [128, 1024], BF16)) for i in range(3)]
                c['st'] = [ps.enter_context(sbuf_u("ipt%d" % i, [128, 512], F32)) for i in range(3)]
                c['stb'] = [ps.enter_context(sbuf_u("iptb%d" % i, [128, 512], BF16)) for i in range(3)]
                c['st2'] = [ps.enter_context(sbuf_u("ipt2_%d" % i, [128, 512], F32)) for i in range(3)]
                c['sq'] = [ps.enter_context(sbuf_u("ipq%d" % i, [128, 1024], F32)) for i in range(2)]
                c['sq2'] = [ps.enter_context(sbuf_u("ipq2_%d" % i, [128, 1024], F32)) for i in range(2)]
                c['qi'] = 0
                c['i'] = 0
                return c
            if kind == 'F' and OFF_HQ <= a[2] < OFF_HQ + 512:
                c, banks, col, t0, n = a
                i = c['qi'] % 2
                c['qi'] += 1
                sq, sq2 = c['sq'][i], c['sq2'][i]
                sqk, sq2k = "ipq%d" % i, "ipq2_%d" % i
                for (p_, pk, o, sz) in banks:
                    op('act', lambda e, p_=p_, o=o, sz=sz: e.activation(out=sq[:, o:o + sz], in_=p_[:, 0:sz], func=AF.Exp, scale=-1.0), reads=[pk], writes=[sqk])
                    op('act', lambda e, o=o, sz=sz: e.activation(out=sq[:, o:o + sz], in_=sq[:, o:o + sz], func=AF.Ln, bias=onet[:, 0:1]), reads=[sqk, 'onet'], writes=[sqk])
                    op('act', lambda e, o=o, sz=sz: e.activation(out=sq[:, o:o + sz], in_=sq[:, o:o + sz], func=AF.Exp, scale=-1.0), reads=[sqk], writes=[sqk])
                    op('dve', lambda e, p_=p_, o=o, sz=sz: e.tensor_tensor(out=sq2[:, o:o + sz], in0=p_[:, 0:sz], in1=sq[:, o:o + sz], op=ALU.mult),
                       reads=[pk, sqk], writes=[sq2k])
                dma('pool', U_qs[col - OFF_HQ:col - OFF_HQ + 128, t0:t0 + n], sq2[:, 0:n], reads=[sq2k], writes=['U_qs'])
            elif kind == 'F':
                c, banks, col, t0, n = a
                i = c['i'] % 3
                c['i'] += 1
                sf = c['sf'][i]
                sk = "ipf%d" % i
                for bi_, (p_, pk, o, sz) in enumerate(banks):
                    if bi_ % 2 == 0:
                        op('act', lambda e, p_=p_, o=o, sz=sz: e.activation(out=sf[:, o:o + sz], in_=p_[:, 0:sz], func=AF.Copy),
                           reads=[pk], writes=[sk])
                    else:
                        op('dve', lambda e, p_=p_, o=o, sz=sz: e.tensor_copy(out=sf[:, o:o + sz], in_=p_[:, 0:sz]),
                           reads=[pk], writes=[sk])
                dma('pool', U_F[col:col + 128, t0:t0 + n], sf[:, 0:n], reads=[sk], writes=['U_F'])
            else:
                c, (p_, pk), c0, tt, _ = a
                i = c['i'] % 3
                c['i'] += 1
                if c0 < 1024:
                    s_ = c['st'][i]
                    s2 = c['st2'][i]
                    sk = "ipt%d" % i
                    s2k = "ipt2_%d" % i
                    lbs = lbt[:, c0:c0 + 512]
                    oms = omt[:, c0:c0 + 512]
                    op('act', lambda e: e.activation(out=s_[:], in_=p_[:], func=AF.Exp, scale=-1.0), reads=[pk], writes=[sk])
                    op('act', lambda e: e.activation(out=s_[:], in_=s_[:], func=AF.Ln, bias=onet[:, 0:1]), reads=[sk, 'onet'], writes=[sk])
                    op('act', lambda e: e.activation(out=s_[:], in_=s_[:], func=AF.Exp, scale=-1.0), reads=[sk], writes=[sk])
                    op('dve', lambda e: e.tensor_tensor(out=s2[:], in0=s_[:], in1=oms, op=ALU.mult), reads=[sk, 'omt'], writes=[s2k])
                    op('dve', lambda e: e.scalar_tensor_tensor(out=s_[:], in0=s2[:], scalar=1e-30, in1=lbs, op0=ALU.max, op1=ALU.add),
                       reads=[s2k, 'lbt'], writes=[sk])
                    op('act', lambda e: e.activation(out=s_[:], in_=s_[:], func=AF.Ln), reads=[sk], writes=[sk])
                    op('dve', lambda e: e.tensor_tensor(out=s2[:], in0=oms, in1=s2[:], op=ALU.subtract), reads=[s2k, 'omt'], writes=[s2k])
                    dma('pool', U_f[tt:tt + 128, c0:c0 + 512], s_[:], reads=[sk], writes=['U_f'])
                    dma('pool', U_k[tt:tt + 128, c0:c0 + 512], s2[:], reads=[s2k], writes=['U_k'])
                else:
                    s_ = c['stb'][i]
                    sk = "iptb%d" % i
                    op('dve', lambda e: e.tensor_copy(out=s_[:], in_=p_[:]), reads=[pk], writes=[sk])
                    if c0 == OFF_HI:
                        dma('pool', U_i[tt:tt + 128, :], s_[:], reads=[sk], writes=['U_i'])
                    else:
                        dma('pool', U_nv[tt:tt + 128, c0 - OFF_NV:c0 - OFF_NV + 512], s_[:], reads=[sk], writes=['U_nv'])


        def block_norm_store(ps_ctx, src_ap, n, wcol, dst_rows, t0, extra_mul=None, keys=()):
            c = ps_ctx
            i = c['i'] % 2
            c['i'] += 1
            sq, rs, ob, pn = c['sq'][i], c['rs'][i], c['ob'][i], c['pn'][i]
            sqk, rsk, obk, pnk = "bn_sq%d" % i, "bn_rs%d" % i, "bn_ob%d" % i, "bn_pn%d" % i
            op('act', lambda e: e.activation(out=sq[:, 0:n], in_=src_ap, func=AF.Square), reads=list(keys), writes=[sqk])
            op('pe', lambda e: e.matmul(pn[:, 0:n], lhsT=onesb, rhs=sq[:, 0:n], start=True, stop=True), reads=[sqk, 'ctb'], writes=[pnk])
            op('act', lambda e: e.activation(out=rs[:, 0:n], in_=pn[:, 0:n], func=AF.Ln, bias=epst[:, 0:1], scale=1.0 / 128),
               reads=[pnk, 'epst'], writes=[rsk])
            op('act', lambda e: e.activation(out=rs[:, 0:n], in_=rs[:, 0:n], func=AF.Exp, scale=-0.5), reads=[rsk], writes=[rsk])
            if extra_mul is None:
                op('dve', lambda e: e.scalar_tensor_tensor(out=ob[:, 0:n], in0=src_ap, scalar=vt[:, wcol:wcol + 1], in1=rs[:, 0:n],
                                                         op0=ALU.mult, op1=ALU.mult), reads=list(keys) + [rsk, 'vt'], writes=[obk])
            else:
                op('dve', lambda e: e.scalar_tensor_tensor(out=rs[:, 0:n], in0=src_ap, scalar=vt[:, wcol:wcol + 1], in1=rs[:, 0:n],
                                                         op0=ALU.mult, op1=ALU.mult), reads=list(keys) + [rsk, 'vt'], writes=[rsk])
                emul, ekeys = extra_mul
                op('dve', lambda e: e.tensor_tensor(out=ob[:, 0:n], in0=rs[:, 0:n], in1=emul, op=ALU.mult),
                   reads=[rsk] + list(ekeys), writes=[obk])
            dma('sp', mixF[dst_rows:dst_rows + 128, t0:t0 + n], ob[:, 0:n], reads=[obk], writes=['mixF'])

        def bn_alloc(ps):
            c = {'i': 0}
            c['sq'] = [ps.enter_context(sbuf_u("bnsq%d" % i, [128, 512], BF16)) for i in range(2)]
            c['rs'] = [ps.enter_context(sbuf_u("bnrs%d" % i, [128, 512], F32)) for i in range(2)]
            c['ob'] = [ps.enter_context(sbuf_u("bnob%d" % i, [128, 512], BF16)) for i in range(2)]
            c['pn'] = [ps.enter_context(psum_u("bnpn%d" % i, [128, 512], F32)) for i in range(2)]
            return c

        def phase_hgrn(l):
            NSC = 4
            W_ = NSC * 128
            NQ = NSC * 64
            with contextlib.ExitStack() as ps:
                bn = bn_alloc(ps)
                Obs = [ps.enter_context(sbuf_u("hgO%d" % i, [128, T], F32)) for i in range(2)]
                St = [ps.enter_context(sbuf_u("hgS%d" % i, [128, 128], F32)) for i in range(2)]
                gsb = [ps.enter_context(sbuf_u("hggs%d" % i, [128, 512], BF16)) for i in range(2)]
                gsf = [ps.enter_context(sbuf_u("hggf%d" % i, [128, 512], F32)) for i in range(2)]
                NBUF = 2
                names = [('A', [128, W_], F32), ('B', [128, W_], F32), ('C', [128, W_], F32), ('v', [128, W_], BF16), ('q', [128, NQ], BF16),
                         ('qe', [128, NQ], F32), ('qs', [128, NQ], F32), ('E', [128, NSC * 66], F32), ('En', [128, W_], F32), ('kt', [128, W_], BF16),
                         ('qt', [128, NQ], BF16), ('ktF', [128, NQ], BF16), ('amf', [64, NQ], F32), ('am', [64, NQ], BF16)]
                BUF = [{nm: ps.enter_context(sbuf_u("hg%s%d" % (nm, i), shp, dt)) for (nm, shp, dt) in names} for i in range(NBUF)]
                Sp = [ps.enter_context(sbuf_u("hgSp%d" % i, [128, 128], BF16)) for i in range(2)]
                pA = ps.enter_context(psum_u("hgpA", [128, 512], F32))
                pBB = ps.enter_context(psum_u("hgpBB", [128, 512], F32))
                pT = ps.enter_context(psum_u("hgpT", [128, 1024], BF16))
                pAtt = ps.enter_context(psum_u("hgpAtt", [128, 512], F32))
                pKv = ps.enter_context(psum_u("hgpKv", [128, 512], F32))
                pO = ps.enter_context(psum_u("hgpO", [128, 512], F32))
                nlat = SEQ // (NSC * 64)
                items = []
                for hd in range(4):
                    for dr in range(2):
                        scs = [0] + [CTX + i * NSC * 64 for i in (range(nlat) if dr == 0 else reversed(range(nlat)))]
                        for j, t0 in enumerate(scs):
                            items.append((hd, dr, j, t0, len(scs)))
                n = NSC * 64
                state = {'sidx': 0, 'spi': 0}

                def bufs(i):
                    b2 = i % NBUF
                    Bf = BUF[b2]
                    return [Bf[nm] for (nm, _, _) in names], ["hg%s%d" % (nm, b2) for (nm, _, _) in names]

                def prep(i, part):
                    hd, dr, j, t0, _ = items[i]
                    (A_, B_, C_, v_, q_, qe_, qs_, E_, En_, kt_, qt_, ktF_, amf_, am_), (Ak, Bk, Ck, vk, qk, qek, qsk, Ek, Enk, ktk, qtk, ktFk, amfk, amk) = bufs(i)
                    if part != 1:
                        return
                    fcol = dr * 512 + hd * 128
                    dma('sp', A_[0:64, :].rearrange("p (c d) -> p c d", c=NSC),
                        U_f[t0:t0 + n, fcol:fcol + 128].rearrange("(c p) d -> p c d", p=64), writes=[Ak + "_0"])
                    for half in range(2):
                        dma('sp', C_[half * 64:(half + 1) * 64, :].rearrange("p (c d) -> p c d", c=NSC),
                            U_k[t0:t0 + n, fcol:fcol + 128].rearrange("(c p) d -> p c d", p=64), writes=[Ck + "_%d" % half])
                        dma('sp', v_[half * 64:(half + 1) * 64, :].rearrange("p (c d) -> p c d", c=NSC),
                            U_i[t0:t0 + n, hd * 128:hd * 128 + 128].rearrange("(c p) d -> p c d", p=64), writes=[vk + "_%d" % half])
                    dma('sp', qs_[:], U_qs[hd * 128:hd * 128 + 128, t0:t0 + n], writes=[qsk])

                def partAB(i, stage_):
                    hd, dr, j, t0, nsc_chain = items[i]
                    Ob = Obs[hd % 2]
                    (A_, B_, C_, v_, q_, qe_, qs_, E_, En_, kt_, qt_, ktF_, amf_, am_), (Ak, Bk, Ck, vk, qk, qek, qsk, Ek, Enk, ktk, qtk, ktFk, amfk, amk) = bufs(i)
                    Aks = [Ak + "_0"]
                    Cks = [Ck + "_0", Ck + "_1"]
                    vks = [vk + "_0", vk + "_1"]
                    mext = ct[0:64, (C_MEXT_F if dr == 0 else C_MEXT_B):(C_MEXT_F if dr == 0 else C_MEXT_B) + 66]
                    mcc = ct[0:64, (C_MCC_F if dr == 0 else C_MCC_B):(C_MCC_F if dr == 0 else C_MCC_B) + 128]
                    mkr = ct[0:64, (C_MKR_F if dr == 0 else C_MKR_B):(C_MKR_F if dr == 0 else C_MKR_B) + NQ]
                    def stA1():
                        for c in range(NSC):
                            op('pe', lambda e, c=c: e.matmul(pA[:, c * 66:(c + 1) * 66], lhsT=A_[0:64, c * 128:(c + 1) * 128], rhs=mext, start=True, stop=True),
                               reads=Aks + ['ct'], writes=['hgpA'], inc=(c == NSC - 1))
                        for c in range(NSC):
                            op('pe', lambda e, c=c: e.matmul(pBB[:, c * 128:(c + 1) * 128], lhsT=mcc, rhs=A_[0:64, c * 128:(c + 1) * 128], start=True, stop=True),
                               reads=Aks + ['ct'], writes=['hgpBB'], inc=(c == NSC - 1))

                    def stA1act():
                        op('act', lambda e: e.activation(out=E_[:], in_=pA[:, 0:NSC * 66], func=AF.Exp), reads=['hgpA'], writes=[Ek])
                        op('act', lambda e: e.activation(out=En_[:], in_=pBB[:, :], func=AF.Exp, scale=-1.0), reads=['hgpBB'], writes=[Enk])

                    def stA2():
                        op('dve', lambda e: e.tensor_tensor(out=kt_[:], in0=C_[:], in1=En_[:], op=ALU.mult), reads=Cks + [Enk], writes=[ktk])
                        op('dve', lambda e: e.tensor_tensor(out=qt_[:].rearrange("p (c w) -> p c w", c=NSC), in0=qs_[:].rearrange("p (c w) -> p c w", c=NSC),
                                                           in1=E_[:].rearrange("p (c w) -> p c w", c=NSC)[:, :, 0:64], op=ALU.mult), reads=[qsk, Ek], writes=[qtk])
                        for c in range(NSC):
                            op('pe', lambda e, c=c: e.transpose(out=pT[:, c * 64:(c + 1) * 64], in_=kt_[0:64, c * 128:(c + 1) * 128], identity=identb[0:64, 0:64]),
                               reads=[ktk, 'ctb'], writes=['hgpT'], inc=(c == NSC - 1))
                        op('act', lambda e: e.activation(out=ktF_[:], in_=pT[:, 0:NQ], func=AF.Copy), reads=['hgpT'], writes=[ktFk])

                    def stA3():
                        for c in range(NSC):
                            op('pe', lambda e, c=c: e.matmul(pAtt[0:64, c * 64:(c + 1) * 64], lhsT=ktF_[:, c * 64:(c + 1) * 64], rhs=qt_[:, c * 64:(c + 1) * 64],
                                                            start=True, stop=True), reads=[ktFk, qtk], writes=['hgpAtt'], inc=(c == NSC - 1))
                        op('dve', lambda e: e.tensor_scalar(out=amf_[:], in0=pAtt[0:64, 0:NQ], scalar1=1e30, scalar2=-1e30, op0=ALU.min, op1=ALU.max),
                           reads=['hgpAtt'], writes=[amfk])
                        op('dve', lambda e: e.tensor_tensor(out=am_[:], in0=amf_[:], in1=mkr, op=ALU.mult), reads=[amfk, 'ct'], writes=[amk])
                        for c in range(NSC):
                            op('pe', lambda e, c=c: e.matmul(pKv[:, c * 128:(c + 1) * 128], lhsT=kt_[64:128, c * 128:(c + 1) * 128], rhs=v_[64:128, c * 128:(c + 1) * 128],
                                                            start=True, stop=True), reads=[ktk] + vks, writes=['hgpKv'], inc=(c == NSC - 1))

                    def stB():
                        if j == 0:
                            state['sidx'] = 0
                            op('dve', lambda e: e.memset(St[0][:], 0.0), writes=['hgS0'])
                        order = list(range(NSC)) if dr == 0 else list(reversed(range(NSC)))
                        for c in order:
                            sidx = state['sidx']
                            Sold, Snew = St[sidx % 2], St[(sidx + 1) % 2]
                            Soldk, Snewk = "hgS%d" % (sidx % 2), "hgS%d" % ((sidx + 1) % 2)
                            state['sidx'] += 1
                            Sp_ = Sp[state['spi'] % 2]
                            Spk = "hgSp%d" % (state['spi'] % 2)
                            state['spi'] += 1
                            op('act', lambda e, c=c, Sp_=Sp_, Sold=Sold: e.activation(out=Sp_[:], in_=Sold[:], func=AF.Identity, scale=E_[:, c * 66 + 65:c * 66 + 66]),
                               reads=[Soldk, Ek], writes=[Spk])
                            op('pe', lambda e, c=c, Sp_=Sp_: e.matmul(pO[:, c * 64:(c + 1) * 64], lhsT=Sp_[:], rhs=qt_[:, c * 64:(c + 1) * 64], start=True, stop=False),
                               reads=[Spk, qtk], writes=['hgpO'], inc=False)
                            op('pe', lambda e, c=c: e.matmul(pO[:, c * 64:(c + 1) * 64], lhsT=v_[0:64, c * 128:(c + 1) * 128], rhs=am_[:, c * 64:(c + 1) * 64], start=False, stop=True),
                               reads=vks + [amk], writes=['hgpO'])
                            op('dve', lambda e, c=c, Sold=Sold, Snew=Snew: e.scalar_tensor_tensor(
                                out=Snew[:], in0=Sold[:], scalar=E_[:, c * 66 + 64:c * 66 + 65], in1=pKv[:, c * 128:(c + 1) * 128], op0=ALU.mult, op1=ALU.add),
                               reads=[Soldk, Ek, 'hgpKv'], writes=[Snewk])
                        ok_ = "hgO%d_%d" % (hd % 2, t0 // 256)
                        if dr == 0:
                            op('act', lambda e: e.activation(out=Ob[:, t0:t0 + n], in_=pO[:, 0:n], func=AF.Copy), reads=['hgpO'], writes=[ok_])
                        else:
                            op('dve', lambda e: e.tensor_tensor(out=Ob[:, t0:t0 + n], in0=pO[:, 0:n], in1=Ob[:, t0:t0 + n], op=ALU.add),
                               reads=['hgpO', ok_], writes=[ok_])

                    {0: stA1, 1: stA2, 2: stA3, 3: stB, 4: stA1act}[stage_]()

                def readout(hd):
                    Ob = Obs[hd % 2]
                    for bi_, (t0, nb) in enumerate([(0, 256)] + [(256 + i * 512, 512) for i in range(8)]):
                        b2 = bi_ % 2
                        gk_, gfk_ = "hggs%d" % b2, "hggf%d" % b2
                        dma('sp', gsb[b2][:, 0:nb], U_F[OFF_HGG + hd * 128:OFF_HGG + hd * 128 + 128, t0:t0 + nb], writes=[gk_])
                        op('act', lambda e, b2=b2, nb=nb: e.activation(out=gsf[b2][:, 0:nb], in_=gsb[b2][:, 0:nb], func=AF.Exp, scale=-1.0), reads=[gk_], writes=[gfk_])
                        op('act', lambda e, b2=b2, nb=nb: e.activation(out=gsf[b2][:, 0:nb], in_=gsf[b2][:, 0:nb], func=AF.Ln, bias=onet[:, 0:1]), reads=[gfk_, 'onet'], writes=[gfk_])
                        op('act', lambda e, b2=b2, nb=nb: e.activation(out=gsf[b2][:, 0:nb], in_=gsf[b2][:, 0:nb], func=AF.Exp, scale=-1.0), reads=[gfk_], writes=[gfk_])
                        op('dve', lambda e, b2=b2, nb=nb: e.tensor_tensor(out=gsf[b2][:, 0:nb], in0=gsf[b2][:, 0:nb], in1=gsb[b2][:, 0:nb], op=ALU.mult), reads=[gfk_, gk_], writes=[gfk_])
                        okeys = ["hgO%d_%d" % (hd % 2, jj) for jj in range(t0 // 256, (t0 + nb) // 256)]
                        block_norm_store(bn, Ob[:, t0:t0 + nb], nb, V_HGN + l, hd * 128, t0, extra_mul=(gsf[b2][:, 0:nb], [gfk_]), keys=okeys)

                prep(0, 1)
                prep(0, 2)
                partAB(0, 0)
                for i in range(len(items)):
                    nxt = i + 1 < len(items)
                    partAB(i, 4)
                    partAB(i, 1)
                    if nxt:
                        prep(i + 1, 1)
                    partAB(i, 2)
                    if nxt:
                        prep(i + 1, 2)
                        partAB(i + 1, 0)
                    partAB(i, 3)
                    hd, dr, j, t0, nsc_chain = items[i]
                    if dr == 1 and j == nsc_chain - 1:
                        readout(hd)
            S.barrier()

        def phase_na(l, skip_ctx_out=False):
            with contextlib.ExitStack() as ps:
                bn = bn_alloc(ps)
                qn = ps.enter_context(sbuf_u("naq", [128, T], BF16))
                kn = ps.enter_context(sbuf_u("nak", [128, T], BF16))
                raw = [ps.enter_context(sbuf_u("naraw%d" % i, [128, 512], BF16)) for i in range(2)]
                nsq = [ps.enter_context(sbuf_u("nansq%d" % i, [128, 512], BF16)) for i in range(2)]
                nrs = [ps.enter_context(sbuf_u("nanrs%d" % i, [128, 512], F32)) for i in range(2)]
                v0 = ps.enter_context(sbuf_u("nav0", [128, NT * 128], BF16))
                v1 = ps.enter_context(sbuf_u("nav1", [128, 31 * 128], BF16))
                nbf = ps.enter_context(sbuf_u("nanbf", [128, 14 * 64], F32))
                nbb = ps.enter_context(sbuf_u("nanbb", [128, 14 * 64], BF16))
                wsc = ps.enter_context(sbuf_u("nawsc", [128, 2], F32))
                PT = [ps.enter_context(sbuf_u("naPT%d" % i, [128, 512], BF16)) for i in range(2)]
                rd = [ps.enter_context(sbuf_u("nard%d" % i, [128, 256], F32)) for i in range(2)]
                ob = [ps.enter_context(sbuf_u("naob%d" % i, [128, 512], F32)) for i in range(2)]
                pS = [ps.enter_context(psum_u("napS%d" % i, [128, 512], F32)) for i in range(2)]
                pO = [ps.enter_context(psum_u("napO%d" % i, [128, 512], F32)) for i in range(2)]
                pD = [ps.enter_context(psum_u("napD%d" % i, [128, 512], F32)) for i in range(2)]
                pN = bn['pn'][0]
                op('dve', lambda e: e.tensor_scalar(out=wsc[:, 0:1], in0=vt[:, V_QN + l:V_QN + l + 1], scalar1=float(HD ** -0.5), scalar2=None, op0=ALU.mult),
                   reads=['vt'], writes=['nawsc'])
                op('dve', lambda e: e.tensor_copy(out=wsc[:, 1:2], in_=vt[:, V_KN + l:V_KN + l + 1]), reads=['vt'], writes=['nawsc'])
                blocks = [(0, 256)] + [(256 + i * 512, 512) for i in range(8)]
                ri = 0
                rowi = 0
                for hd in range(N_NA):
                    dma('sp', v0[:].rearrange("p (j d) -> p j d", j=NT), U_nv[0:T, hd * 128:hd * 128 + 128].rearrange("(j p) d -> p j d", p=128), writes=['nav0'])
                    dma('sp', v1[:].rearrange("p (j d) -> p j d", j=31),
                        U_nv[CTX + 64:CTX + 64 + 31 * 128, hd * 128:hd * 128 + 128].rearrange("(j p) d -> p j d", p=128), writes=['nav1'])
                    dma('sp', nbf[:], natab[l, hd], writes=['nanbf'])
                    op('dve', lambda e: e.tensor_copy(out=nbb[:], in_=nbf[:]), reads=['nanbf'], writes=['nanbb'])
                    for which, (off, dst, dk_) in enumerate([(OFF_NQ, qn, 'naq'), (OFF_NK, kn, 'nak')]):
                        for (t0, n) in blocks:
                            b2 = ri % 2
                            ri += 1
                            r_, s_, rs_ = raw[b2], nsq[b2], nrs[b2]
                            rk, sk, rsk = "naraw%d" % b2, "nansq%d" % b2, "nanrs%d" % b2
                            dma('sp', r_[:, 0:n], U_F[off + hd * 128:off + hd * 128 + 128, t0:t0 + n], writes=[rk])
                            op('act', lambda e: e.activation(out=s_[:, 0:n], in_=r_[:, 0:n], func=AF.Square), reads=[rk], writes=[sk])
                            op('pe', lambda e: e.matmul(pN[:, 0:n], lhsT=onesb, rhs=s_[:, 0:n], start=True, stop=True), reads=[sk, 'ctb'], writes=['bn_pn0'])
                            op('act', lambda e: e.activation(out=rs_[:, 0:n], in_=pN[:, 0:n], func=AF.Ln, bias=epst[:, 0:1], scale=1.0 / 128),
                               reads=['bn_pn0', 'epst'], writes=[rsk])
                            op('act', lambda e: e.activation(out=rs_[:, 0:n], in_=rs_[:, 0:n], func=AF.Exp, scale=-0.5), reads=[rsk], writes=[rsk])
                            op('dve', lambda e: e.scalar_tensor_tensor(out=dst[:, t0:t0 + n], in0=r_[:, 0:n], scalar=wsc[:, which:which + 1], in1=rs_[:, 0:n],
                                                                     op0=ALU.mult, op1=ALU.mult), reads=[rk, rsk, 'nawsc'], writes=[dk_])

                    def att_scores(job):
                        nonlocal rowi
                        qlo, nq, tiles = job['qlo'], job['nq'], job['tiles']
                        b2 = rowi % 2
                        rowi += 1
                        job['b2'] = b2
                        pS_ = pS[b2]
                        pSk = "napS%d" % b2
                        nt_ = len(tiles)
                        for i, (klo, vap, bap) in enumerate(tiles):
                            last = (i == nt_ - 1)
                            op('pe', lambda e, i=i, klo=klo, bap=bap: e.matmul(pS_[:, i * nq:(i + 1) * nq], lhsT=kn[:, klo:klo + 128], rhs=qn[:, qlo:qlo + nq],
                                                                              start=True, stop=(bap is None)), reads=['nak', 'naq'], writes=[pSk], inc=(bap is None and last))
                            if bap is not None:
                                op('pe', lambda e, i=i, bap=bap: e.matmul(pS_[:, i * nq:(i + 1) * nq], lhsT=identb, rhs=bap, start=False, stop=True),
                                   reads=['nanbb', 'ctb'], writes=[pSk], inc=last)

                    def att_finish(job):
                        qlo, nq, tiles, obuf, ocol, obk, b2 = job['qlo'], job['nq'], job['tiles'], job['obuf'], job['ocol'], job['obk'], job['b2']
                        pS_, pO_, pD_, PT_, rd_ = pS[b2], pO[b2], pD[b2], PT[b2], rd[b2]
                        pSk, pOk, pDk, PTk, rdk = "napS%d" % b2, "napO%d" % b2, "napD%d" % b2, "naPT%d" % b2, "nard%d" % b2
                        nt_ = len(tiles)
                        W_ = nt_ * nq
                        op('act', lambda e: e.activation(out=PT_[:, 0:W_], in_=pS_[:, 0:W_], func=AF.Exp), reads=[pSk], writes=[PTk])
                        for i, (klo, vap, bap) in enumerate(tiles):
                            op('pe', lambda e, i=i, vap=vap: e.matmul(pO_[:, 0:nq], lhsT=vap, rhs=PT_[:, i * nq:(i + 1) * nq], start=(i == 0), stop=(i == nt_ - 1)),
                               reads=['nav0', 'nav1', PTk], writes=[pOk], inc=(i == nt_ - 1))
                        for i in range(nt_):
                            op('pe', lambda e, i=i: e.matmul(pD_[:, 0:nq], lhsT=onesb, rhs=PT_[:, i * nq:(i + 1) * nq], start=(i == 0), stop=(i == nt_ - 1)),
                               reads=['ctb', PTk], writes=[pDk], inc=(i == nt_ - 1))
                        op('dve', lambda e: e.reciprocal(out=rd_[:, 0:nq], in_=pD_[:, 0:nq]), reads=[pDk], writes=[rdk])
                        op('dve', lambda e: e.tensor_tensor(out=obuf[:, ocol:ocol + nq], in0=pO_[:, 0:nq], in1=rd_[:, 0:nq], op=ALU.mult),
                           reads=[pOk, rdk], writes=[obk])
                        if job.get('norm') is not None:
                            nn, t0n = job['norm']
                            block_norm_store(bn, obuf[:, 0:nn], nn, V_NAON + l * 8 + hd, 512 + hd * 128, t0n, keys=[obk])

                    ctiles = [(0, v0[:, 0:128], None), (128, v0[:, 128:256], None)]
                    jobs = []
                    oi = 0
                    ob_ = ob[oi % 2]
                    obk = "naob%d" % (oi % 2)
                    oi += 1
                    if not skip_ctx_out:
                        jobs.append({'qlo': 0, 'nq': 256, 'tiles': ctiles, 'obuf': ob_, 'ocol': 0, 'obk': obk, 'norm': (256, 0)})
                    for r in range(ROWS):
                        if r % 8 == 0:
                            ob_ = ob[oi % 2]
                            obk = "naob%d" % (oi % 2)
                            oi += 1
                        rs0 = min(max(r - 4, 0), ROWS - 8)
                        par = rs0 % 2
                        tiles = []
                        for i in range(4):
                            kr = rs0 + 2 * i
                            klo = CTX + kr * 64
                            if par == 0:
                                j = 2 + kr // 2
                                vap = v0[:, j * 128:(j + 1) * 128]
                            else:
                                j = (kr - 1) // 2
                                vap = v1[:, j * 128:(j + 1) * 128]
                            pi_ = kr - r + 7
                            tiles.append((klo, vap, nbb[:, pi_ * 64:(pi_ + 1) * 64]))
                        tiles += ctiles
                        jobs.append({'qlo': CTX + r * 64, 'nq': 64, 'tiles': tiles, 'obuf': ob_, 'ocol': (r % 8) * 64, 'obk': obk,
                                     'norm': (512, CTX + (r // 8) * 512) if r % 8 == 7 else None})
                    att_scores(jobs[0])
                    for ji in range(len(jobs)):
                        if ji + 1 < len(jobs):
                            att_scores(jobs[ji + 1])
                        att_finish(jobs[ji])
            S.barrier()

        def phase_cv(l):
            with contextlib.ExitStack() as ps:
                bn = bn_alloc(ps)
                Bt = ps.enter_context(sbuf_u("cvB", [128, T], BF16))
                Ct = ps.enter_context(sbuf_u("cvC", [128, T], BF16))
                Vt = ps.enter_context(sbuf_u("cvV", [128, T], BF16))
                cvv = ps.enter_context(sbuf_u("cvcv", [128, T], F32))
                acc = ps.enter_context(sbuf_u("cvacc", [128, T], F32))
                for g in range(N_CV):
                    for (tile_, off, k_) in ((Bt, OFF_CB, 'cvB'), (Ct, OFF_CC, 'cvC'), (Vt, OFF_CVV, 'cvV')):
                        dma('sp', tile_[:], U_F[off + g * 128:off + g * 128 + 128, :], writes=[k_])
                    wc = V_CVW + l * 12
                    op('dve', lambda e: e.tensor_tensor(out=cvv[:], in0=Ct[:], in1=Vt[:], op=ALU.mult), reads=['cvC', 'cvV'], writes=['cvcv'])
                    op('act', lambda e: e.activation(out=acc[:], in_=cvv[:], func=AF.Identity, scale=vt[:, wc + 4 + g:wc + 4 + g + 1]), reads=['cvcv', 'vt'], writes=['cvacc'])
                    for (lo, hi) in ((0, CTX), (CTX, T)):
                        op('dve', lambda e, lo=lo, hi=hi: e.scalar_tensor_tensor(out=acc[:, lo + 1:hi], in0=cvv[:, lo:hi - 1], scalar=vt[:, wc + g:wc + g + 1],
                                                                               in1=acc[:, lo + 1:hi], op0=ALU.mult, op1=ALU.add), reads=['cvcv', 'cvacc', 'vt'], writes=['cvacc'])
                        op('dve', lambda e, lo=lo, hi=hi: e.scalar_tensor_tensor(out=acc[:, lo:hi - 1], in0=cvv[:, lo + 1:hi], scalar=vt[:, wc + 8 + g:wc + 8 + g + 1],
                                                                               in1=acc[:, lo:hi - 1], op0=ALU.mult, op1=ALU.add), reads=['cvcv', 'cvacc', 'vt'], writes=['cvacc'])
                    op('dve', lambda e: e.tensor_tensor(out=acc[:], in0=acc[:], in1=Bt[:], op=ALU.mult), reads=['cvacc', 'cvB'], writes=['cvacc'])
                    for (t0, n) in [(0, 256)] + [(256 + i * 512, 512) for i in range(8)]:
                        block_norm_store(bn, acc[:, t0:t0 + n], n, V_CVON + l * 4 + g, 1536 + g * 128, t0, keys=['cvacc'])
            S.barrier()

        def make_epi_resid(Xsrc, Xdst, gi, lat_only_out=None):
            def epi(kind, *a):
                if kind == 'alloc':
                    ps = a[0]
                    c = {'i': 0}
                    c['xs'] = [ps.enter_context(sbuf_u("erx%d" % i, [128, 512], F32)) for i in range(3)]
                    c['tm'] = [ps.enter_context(sbuf_u("ert%d" % i, [128, 512], F32)) for i in range(3)]
                    return c
                c, (p_, pk), c0, tt, _ = a
                i = c['i'] % 3
                c['i'] += 1
                xs, tm = c['xs'][i], c['tm'][i]
                xk, tk = "erx%d" % i, "ert%d" % i
                which = 1 if tt < CTX else 0
                gb = (gi * 2 + which) * D + c0
                dma('sp', xs[:], Xsrc[tt:tt + 128, c0:c0 + 512], writes=[xk])
                op('dve', lambda e: e.tensor_tensor(out=tm[:], in0=p_[:], in1=grow[:, gb:gb + 512], op=ALU.mult), reads=[pk, 'grow'], writes=[tk])
                op('pool', lambda e: e.tensor_tensor(out=tm[:], in0=tm[:], in1=xs[:], op=ALU.add), reads=[tk, xk], writes=[tk])
                if lat_only_out is not None:
                    if tt >= CTX:
                        dma('pool', lat_only_out[tt - CTX:tt - CTX + 128, c0:c0 + 512], tm[:], reads=[tk], writes=['Xout'])
                else:
                    dma('pool', Xdst[tt:tt + 128, c0:c0 + 512], tm[:], reads=[tk], writes=['Xout'])
            return epi

        def phase_ffn_up(l, skip_ctx=False):
            with contextlib.ExitStack() as ps:
                blocks = ([] if skip_ctx else [(0, 256, 0, CTX)]) + [(CTX + o, sz, CTX, T) for (o, sz) in _split(SEQ, 1020)]
                maxn = max(b[1] for b in blocks) + 2
                ab = [ps.enter_context(sbuf_u("fua%d" % i, [128, 16 * maxn], BF16)) for i in range(2)]
                wg = [ps.enter_context(sbuf_u("fuwg%d" % i, [128, 16 * 512], BF16)) for i in range(2)]
                wvl = [ps.enter_context(sbuf_u("fuwv%d" % i, [128, 16 * 512], BF16)) for i in range(2)]
                accg = [ps.enter_context(sbuf_u("fuag%d" % i, [128, maxn], F32)) for i in range(2)]
                accv = [ps.enter_context(sbuf_u("fuav%d" % i, [128, maxn], F32)) for i in range(2)]
                stg = [ps.enter_context(sbuf_u("fust%d" % i, [128, maxn], BF16)) for i in range(2)]
                pp = [ps.enter_context(psum_u("fup%d" % i, [128, 512], F32)) for i in range(8)]
                Av = hF.rearrange("(kc p) t -> p kc t", p=128)
                Wv = wb_up[l].rearrange("(kc p) n -> p kc n", p=128)
                wkl = wkeys['up', l]
                wi = 0
                pidx = 0
                ei = 0
                fcw = V_FCW + l * 264
                fcb = V_FCB + l * 88
                for bi, (t0, n, lo, hi) in enumerate(blocks):
                    a_ = ab[bi % 2]
                    ak = "fua%d" % (bi % 2)
                    av = a_[:, 0:16 * (n + 2)].rearrange("p (kc t) -> p kc t", kc=16)
                    s0_ = max(t0 - 1, lo)
                    s1_ = min(t0 + n + 1, hi)
                    d0 = s0_ - (t0 - 1)
                    if d0 > 0 or s1_ < t0 + n + 1:
                        op('dve', lambda e: e.memset(a_[:, 0:16 * (n + 2)], 0.0), writes=[ak])
                    dma('sp', av[:, :, d0:d0 + (s1_ - s0_)], Av[:, :, s0_:s1_], writes=[ak])
                    for j in range(11):
                        wg_, wv_ = wg[wi % 2], wvl[wi % 2]
                        wgk, wvk = "fuwg%d" % (wi % 2), "fuwv%d" % (wi % 2)
                        wi += 1
                        wgv = wg_[:].rearrange("p (kc n) -> p kc n", kc=16)
                        wvv = wv_[:].rearrange("p (kc n) -> p kc n", kc=16)
                        dma('sp', wgv, Wv[:, :, j * 512:(j + 1) * 512], reads=wkl, writes=[wgk])
                        dma('sp', wvv, Wv[:, :, D_FF + j * 512:D_FF + (j + 1) * 512], reads=wkl, writes=[wvk])
                        for c4 in range(4):
                            gcx = j * 4 + c4
                            e2 = ei % 2
                            ei += 1
                            ag, avl, st_ = accg[e2], accv[e2], stg[e2]
                            agk, avk, stk = "fuag%d" % e2, "fuav%d" % e2, "fust%d" % e2
                            for (wview, wk_, acc_, acck, chn) in ((wgv, wgk, ag, agk, gcx), (wvv, wvk, avl, avk, 44 + gcx)):
                                for (o, sz) in _split(n, 510):
                                    p_ = pp[pidx % 8]
                                    pk = "fup%d" % (pidx % 8)
                                    pidx += 1
                                    for kc in range(16):
                                        op('pe', lambda e, p_=p_, kc=kc, o=o, sz=sz, wview=wview: e.matmul(
                                            p_[:, 0:sz + 2], lhsT=wview[:, kc, c4 * 128:(c4 + 1) * 128], rhs=av[:, kc, o:o + sz + 2],
                                            start=(kc == 0), stop=(kc == 15)), reads=[wk_, ak], writes=[pk], inc=(kc == 15))
                                    op('act', lambda e, p_=p_, o=o, sz=sz, acc_=acc_, chn=chn: e.activation(
                                        out=acc_[:, o:o + sz], in_=p_[:, 1:sz + 1], func=AF.Identity,
                                        scale=vt[:, fcw + 88 + chn:fcw + 88 + chn + 1], bias=vt[:, fcb + chn:fcb + chn + 1]),
                                       reads=[pk, 'vt'], writes=[acck])
                                    op('dve', lambda e, p_=p_, o=o, sz=sz, acc_=acc_, chn=chn: e.scalar_tensor_tensor(
                                        out=acc_[:, o:o + sz], in0=p_[:, 0:sz], scalar=vt[:, fcw + chn:fcw + chn + 1], in1=acc_[:, o:o + sz],
                                        op0=ALU.mult, op1=ALU.add), reads=[pk, 'vt', acck], writes=[acck])
                                    op('dve', lambda e, p_=p_, o=o, sz=sz, acc_=acc_, chn=chn: e.scalar_tensor_tensor(
                                        out=acc_[:, o:o + sz], in0=p_[:, 2:sz + 2], scalar=vt[:, fcw + 176 + chn:fcw + 176 + chn + 1], in1=acc_[:, o:o + sz],
                                        op0=ALU.mult, op1=ALU.add), reads=[pk, 'vt', acck], writes=[acck])
                            op('act', lambda e, ag=ag: e.activation(out=ag[:, 0:n], in_=ag[:, 0:n], func=AF.Silu), reads=[agk], writes=[agk])
                            op('pool', lambda e, ag=ag, avl=avl, st_=st_: e.tensor_tensor(out=st_[:, 0:n], in0=ag[:, 0:n], in1=avl[:, 0:n], op=ALU.mult),
                               reads=[agk, avk], writes=[stk])
                            dma('pool', actF[gcx * 128:(gcx + 1) * 128, t0:t0 + n], st_[:, 0:n], reads=[stk], writes=['actF'])
            S.barrier()

        tblocks_all = [(0, 256)] + [(256 + i * 1024, 1024) for i in range(4)]
        norm_groups = [(0, 2, 1)] + [(256 + i * 512, 4, 0) for i in range(8)]

        X0 = xin
        tb128 = [(0, 256)] + [(256 + i * 1024, 1024) for i in range(4)]
        tb_down = [(0, 256)] + [(256 + i * 512, 512) for i in range(8)]
        for l in range(DEPTH):
            last = (l == DEPTH - 1)
            phase_ada(l)
            if stage <= 0:
                break
            phase_norm(X0, 0, norm_groups)
            if stage <= 1:
                break
            lb_setup(l)
            gemm(hF, D, wb_in[l], wkeys['in', l], 0, 1536, tblocks_all, 'T', epi_inproj, name="ip")
            gemm(hF, D, wb_in[l], wkeys['in', l], OFF_NV, 1024, tblocks_all, 'T', epi_inproj, name="ip")
            gemm(hF, D, wb_in[l], wkeys['in', l], OFF_NK, 1024, tblocks_all, 'F', epi_inproj, name="ip")
            gemm(hF, D, wb_in[l], wkeys['in', l], OFF_HQ, IN_W - OFF_HQ, tblocks_all, 'F', epi_inproj, name="ip")
            if stage <= 2:
                break
            if 'hg' in phases:
                phase_hgrn(l)
            issue_casts('mix%d' % l)
            if 'na' in phases:
                phase_na(l, skip_ctx_out=last)
            if 'cv' in phases:
                phase_cv(l)
            if stage <= 3:
                break
            gemm(mixF, D, wb_out[l], wkeys['out', l], 0, D, tb128[1:] if last else tb128, 'T', make_epi_resid(X0, X1, 0), name="op")
            if stage <= 4:
                break
            phase_norm(X1, 1, norm_groups[1:] if last else norm_groups)
            phase_ffn_up(l, skip_ctx=last)
            if stage <= 5:
                break
            if last:
                gemm(actF, D_FF, wb_down[l], wkeys['down', l], 0, D, tb_down[1:], 'T', make_epi_resid(X1, None, 1, lat_only_out=y), name="dn", abufs=1)
            else:
                gemm(actF, D_FF, wb_down[l], wkeys['down', l], 0, D, tb_down, 'T', make_epi_resid(X1, X2, 1), name="dn", abufs=1)
            X0 = X2
            if stage <= 6 + l:
                break

        if 'dbg_ada' in dbg:
            dada = nc.dram_tensor("dbg_ada", [128, 192 + 128 + 4 * D], F32, kind="ExternalOutput").ap()
            dma('sp', dada[:, 0:192], adaT[:], reads=['adaT'])
            dma('sp', dada[:, 192:320], modv[:], reads=['modv'])
            dma('sp', dada[:, 320:], grow[:], reads=['grow'])
        S.barrier()
        S.final()
    return nc


def _flay(v):
    v = np.asarray(v, np.float32)
    return np.ascontiguousarray(v.reshape(-1, 128).T)


def _consts():
    c = np.zeros((128, NCONST), np.float32)
    c[:, C_ID:C_ID + 128] = np.eye(128, dtype=np.float32)
    c[:, C_ONES:C_ONES + 128] = 1.0
    tp = np.arange(64)[:, None]
    t = np.arange(64)[None, :]
    Mf = (tp <= t).astype(np.float32)
    Mb = (tp >= t).astype(np.float32)
    for (M, mid, oe, oc, om, occ, omr) in ((Mf, 31, C_MEXT_F, C_MC_F, C_MK_F, C_MCC_F, C_MKR_F), (Mb, 32, C_MEXT_B, C_MC_B, C_MK_B, C_MCC_B, C_MKR_B)):
        Mc = M - M[:, mid:mid + 1]
        c[:64, oe:oe + 64] = Mc
        c[:64, oe + 64] = 1.0
        c[:64, oe + 65] = M[:, mid]
        c[:64, oc:oc + 64] = Mc
        c[:64, om:om + 64] = M
        c[:64, occ:occ + 64] = Mc
        c[:64, occ + 64:occ + 128] = M - 1.0
        for rr in range(4):
            c[:64, omr + rr * 64:omr + (rr + 1) * 64] = M
    return c


def _natab(rpb):
    L, H = rpb.shape[0], rpb.shape[1]
    kc = np.arange(64)[:, None]
    c = np.arange(64)[None, :]
    ws = np.clip(c - 8, 0, 48)
    ok = (kc >= ws) & (kc < ws + 16)
    ic = np.clip(kc - c + 15, 0, 30)
    out = np.full((L, H, 2, 64, 14, 64), -1e30, np.float32)
    for pi in range(14):
        for rr in range(2):
            dr = pi - 7 + rr
            g = rpb[:, :, dr + 7, :][:, :, ic]
            out[:, :, rr, :, pi, :] = np.where(ok[None, None], g, np.float32(-1e30))
    return np.ascontiguousarray(out.reshape(L, H, 128, 14 * 64))


def _vecs(inp, b):
    v = np.zeros((128, NVEC), np.float32)
    for l in range(DEPTH):
        v[:, V_BADA + l * 96:V_BADA + (l + 1) * 96] = _flay(inp['b_ada'][l])
        v[:, V_LN1 + l * 16:V_LN1 + (l + 1) * 16] = _flay(inp['ln1_w'][l])
        v[:, V_LN2 + l * 16:V_LN2 + (l + 1) * 16] = _flay(inp['ln2_w'][l])
        v[:, V_HGN + l] = inp['hg_norm_w'][l]
        v[:, V_QN + l] = inp['na_q_norm_w'][l]
        v[:, V_KN + l] = inp['na_k_norm_w'][l]
        v[:, V_NAON + l * 8:V_NAON + (l + 1) * 8] = _flay(inp['na_out_norm_w'][l])
        for t in range(3):
            v[:, V_CVW + l * 12 + t * 4:V_CVW + l * 12 + (t + 1) * 4] = _flay(inp['cv_w'][l, t])
            v[:, V_FCW + l * 264 + t * 88:V_FCW + l * 264 + (t + 1) * 88] = _flay(inp['ffn_conv_w'][l, t])
        v[:, V_CVON + l * 4:V_CVON + (l + 1) * 4] = _flay(inp['cv_out_norm_w'][l])
        v[:, V_FCB + l * 88:V_FCB + (l + 1) * 88] = _flay(inp['ffn_conv_b'][l])
    cc = np.stack([_flay(inp['c'][b]), _flay(inp['c_ctx'])], axis=-1)
    v[:, V_CT:V_CT + 32] = cc.reshape(128, 32)
    return v


def make_in_maps(inp, cores):
    inp = {k: np.asarray(v) for k, v in inp.items()}
    consts = _consts()
    natab = _natab(inp['na_rpb'].astype(np.float32))
    lbl = np.ascontiguousarray(np.broadcast_to(inp['hg_lb_logits'].astype(np.float32).reshape(1, -1), (64, 2 * DEPTH * 512)))
    maps = []
    for i in cores:
        b = i % 4
        m = {
            'xin': np.ascontiguousarray(np.concatenate([inp['ctx'][b], inp['x'][b]], axis=0).astype(np.float32)),
            'vecs': _vecs(inp, b), 'consts': consts, 'lbl': lbl, 'natab': natab,
            'w_ada': inp['w_ada'], 'w_in': inp['w_in'], 'w_out': inp['w_out'], 'w_up': inp['w_up'], 'w_down': inp['w_down'],
        }
        maps.append(m)
    return maps


def kernel(**inputs):
    nc = build()
    cores = list(range(8))
    maps = make_in_maps(inputs, cores)
    res = run_bass_kernel_spmd(nc, maps, core_ids=cores)
    return np.stack([np.asarray(res.results[b]['y'], np.float32) for b in range(4)], axis=0)
```

```python
import contextlib
import numpy as np
import concourse.bass as bass
import concourse.mybir as mybir
from concourse.bass_utils import run_bass_kernel_spmd

F32 = mybir.dt.float32
BF16 = mybir.dt.bfloat16
AF = mybir.ActivationFunctionType
ALU = mybir.AluOpType

D = 2048
DEPTH = 2
CTX = 256
SEQ = 4096
T = CTX + SEQ
GW = 64
ROWS = SEQ // GW
HD = 128
N_HG, N_NA, N_CV = 4, 8, 4
IN_W = 7168
D_FF = 5632
EPS = 1e-6
CH = 64
NT = T // 128

OFF_FFW, OFF_FBW, OFF_HI, OFF_NK, OFF_NV, OFF_HQ, OFF_HGG, OFF_NQ, OFF_CB, OFF_CC, OFF_CVV = (
    0, 512, 1024, 1536, 2560, 3584, 4096, 4608, 5632, 6144, 6656)

V_BADA = 0
V_LN1 = V_BADA + DEPTH * 96
V_LN2 = V_LN1 + DEPTH * 16
V_HGN = V_LN2 + DEPTH * 16
V_QN = V_HGN + DEPTH
V_KN = V_QN + DEPTH
V_NAON = V_KN + DEPTH
V_CVW = V_NAON + DEPTH * 8
V_CVON = V_CVW + DEPTH * 12
V_FCW = V_CVON + DEPTH * 4
V_FCB = V_FCW + DEPTH * 264
V_CT = V_FCB + DEPTH * 88
NVEC = V_CT + 32

C_ID = 0
C_ONES = 128
C_MEXT_F = 256
C_MEXT_B = C_MEXT_F + 66
C_MC_F = C_MEXT_B + 66
C_MC_B = C_MC_F + 64
C_MK_F = C_MC_B + 64
C_MK_B = C_MK_F + 64
C_MCC_F = C_MK_B + 64
C_MCC_B = C_MCC_F + 128
C_MKR_F = C_MCC_B + 128
C_MKR_B = C_MKR_F + 256
NCONST = C_MKR_B + 256


class Sched:
    def __init__(self, nc, es):
        self.nc = nc
        self.eng = {'pe': nc.tensor, 'act': nc.scalar, 'dve': nc.vector, 'pool': nc.gpsimd, 'sp': nc.sync}
        self.sem = {k: es.enter_context(nc.semaphore("sem_" + k)) for k in ('pe', 'act', 'dve', 'pool')}
        self.count = {k: 0 for k in self.sem}
        self.NDS = 40
        self.dsem = [es.enter_context(nc.semaphore("dsem%d" % i)) for i in range(self.NDS)]
        self.dval = [0] * self.NDS
        self.dk = 0
        self.seen = {k: {} for k in self.eng}
        self.res = {}
        self.ninstr = 0
        self.csem = {}
        self.ccnt = {}
        self.es = es

    def _wait(self, en, tok):
        kind, a, v = tok
        if kind == 'e':
            if a == en and en == 'pe':
                return
            key = ('e', a)
            sem = self.sem[a]
        elif kind == 'c':
            key = ('c', a)
            sem = self.csem[a]
            v = 16 * self.ccnt[a]
        else:
            key = ('d', a)
            sem = self.dsem[a]
        if self.seen[en].get(key, 0) >= v:
            return
        self.eng[en].wait_ge(sem, v)
        self.seen[en][key] = v

    def _deps(self, en, reads, writes):
        deps = []
        for k in reads:
            r = self.res.get(k)
            if r and r['w']:
                deps.append(r['w'])
        for k in writes:
            r = self.res.get(k)
            if r:
                deps.extend(r['r'])
                if r['w']:
                    deps.append(r['w'])
        for tok in deps:
            self._wait(en, tok)

    def _update(self, tok, reads, writes):
        for k in reads:
            r = self.res.setdefault(k, {'w': None, 'r': []})
            if tok[0] == 'e':
                r['r'] = [t for t in r['r'] if not (t[0] == 'e' and t[1] == tok[1])]
            r['r'].append(tok)
        for k in writes:
            self.res[k] = {'w': tok, 'r': []}

    def op(self, en, fn, reads=(), writes=(), inc=True):
        self._deps(en, reads, writes)
        ins = fn(self.eng[en])
        self.ninstr += 1
        if inc:
            self.count[en] += 1
            ins.then_inc(self.sem[en], 1)
            tok = ('e', en, self.count[en])
        else:
            tok = ('e', en, self.count[en] + 1)
        self._update(tok, reads, writes)
        return tok

    def dma(self, en, out, in_, reads=(), writes=(), **kw):
        slot = self.dk % self.NDS
        self.dk += 1
        if self.dval[slot] > 0:
            self._wait(en, ('d', slot, self.dval[slot]))
        self._deps(en, reads, writes)
        self.dval[slot] += 16
        self.eng[en].dma_start(out=out, in_=in_, **kw).then_inc(self.dsem[slot], 16)
        self.ninstr += 1
        tok = ('d', slot, self.dval[slot])
        self._update(tok, reads, writes)
        return tok

    def cast_dma(self, en, group, out, in_):
        if group not in self.csem:
            self.csem[group] = self.es.enter_context(self.nc.semaphore("csem_" + group))
            self.ccnt[group] = 0
        self.ccnt[group] += 1
        self.eng[en].dma_start(out=out, in_=in_).then_inc(self.csem[group], 16)
        self.ninstr += 1

    def barrier(self):
        for en in self.eng:
            for a in self.sem:
                if a != en and self.count[a] > 0:
                    self._wait(en, ('e', a, self.count[a]))
            for s in range(self.NDS):
                if self.dval[s] > 0:
                    self._wait(en, ('d', s, self.dval[s]))
        self.res = {k: v for k, v in self.res.items() if k.startswith('WB_')}

    def final(self):
        for g in self.csem:
            self._wait('sp', ('c', g, 0))
        for s in range(self.NDS):
            if self.dval[s] > 0:
                self._wait('sp', ('d', s, self.dval[s]))
        for a in self.sem:
            if self.count[a] > 0:
                self._wait('sp', ('e', a, self.count[a]))


def _split(n, m):
    k = (n + m - 1) // m
    base = n // k
    rem = n - base * k
    out = []
    o = 0
    for i in range(k):
        s = base + (1 if i < rem else 0)
        out.append((o, s))
        o += s
    return out


def build(stage=99, dbg=None, phases=('hg', 'na', 'cv')):
    nc = bass.Bass("TRN2", target_bir_lowering=False)
    dbg = dbg or []
    _uid = [0]
    _sb, _pt = nc.sbuf_tensor, nc.psum_tensor

    def sbuf_u(name, shape, dt):
        _uid[0] += 1
        return _sb("%s_u%d" % (name, _uid[0]), shape, dt)

    def psum_u(name, shape, dt):
        _uid[0] += 1
        return _pt("%s_u%d" % (name, _uid[0]), shape, dt)
    outs = {}

    def dram_in(name, shape, dt=F32):
        return nc.dram_tensor(name, list(shape), dt, kind="ExternalInput").ap()

    def dram_tmp(name, shape, dt):
        kind = "ExternalOutput" if name in dbg else "Internal"
        return nc.dram_tensor(name, list(shape), dt, kind=kind).ap()

    xin = dram_in("xin", [T, D])
    vecs = dram_in("vecs", [128, NVEC])
    consts = dram_in("consts", [128, NCONST])
    lbl = dram_in("lbl", [64, 2 * DEPTH * 512])
    natab = dram_in("natab", [DEPTH, N_NA, 128, 14 * 64])
    w_ada = dram_in("w_ada", [DEPTH, D, 6 * D])
    w_in = dram_in("w_in", [DEPTH, D, IN_W])
    w_out = dram_in("w_out", [DEPTH, D, D])
    w_up = dram_in("w_up", [DEPTH, D, 2 * D_FF])
    w_down = dram_in("w_down", [DEPTH, D_FF, D])
    y = nc.dram_tensor("y", [SEQ, D], F32, kind="ExternalOutput").ap()

    wb_in = [dram_tmp("wb_in%d" % l, [D, IN_W], BF16) for l in range(DEPTH)]
    wb_out = [dram_tmp("wb_out%d" % l, [D, D], BF16) for l in range(DEPTH)]
    wb_up = [dram_tmp("wb_up%d" % l, [D, 2 * D_FF], BF16) for l in range(DEPTH)]
    wb_down = [dram_tmp("wb_down%d" % l, [D_FF, D], BF16) for l in range(DEPTH)]
    hF = dram_tmp("hF", [D, T], BF16)
    U_f = dram_tmp("U_f", [T, 1024], F32)
    U_i = dram_tmp("U_i", [T, 512], BF16)
    U_nv = dram_tmp("U_nv", [T + 64, 1024], BF16)
    U_F = dram_tmp("U_F", [IN_W, T], BF16)
    U_k = dram_tmp("U_k", [T, 1024], F32)
    U_qs = dram_tmp("U_qs", [512, T], F32)
    mixF = dram_tmp("mixF", [D, T], BF16)
    actF = dram_tmp("actF", [D_FF, T], BF16)
    X1 = dram_tmp("X1", [T, D], F32)
    X2 = dram_tmp("X2", [T, D], F32)

    es = contextlib.ExitStack()
    with es:
        S = Sched(nc, es)
        op, dma = S.op, S.dma

        vt = es.enter_context(sbuf_u("vt", [128, NVEC], F32))
        ct = es.enter_context(sbuf_u("ct", [128, NCONST], F32))
        ctb = es.enter_context(sbuf_u("ctb", [128, NCONST], BF16))
        adaT = es.enter_context(sbuf_u("adaT", [128, 96 * 2], F32))
        modv = es.enter_context(sbuf_u("modv", [128, 4 * 16 * 2], F32))
        grow = es.enter_context(sbuf_u("grow", [128, 4 * D], F32))
        silc = es.enter_context(sbuf_u("silc", [128, 32], F32))
        epst = es.enter_context(sbuf_u("epst", [128, 1], F32))
        onet = es.enter_context(sbuf_u("onet", [128, 1], F32))
        qkw = es.enter_context(sbuf_u("qkw", [128, 2], F32))
        lbt = es.enter_context(sbuf_u("lbt", [128, 2 * 512], F32))
        omt = es.enter_context(sbuf_u("omt", [128, 2 * 512], F32))
        dma('sp', vt[:], vecs[:, :], writes=['vt'])
        dma('sp', ct[:], consts[:, :], writes=['ct'])
        op('dve', lambda e: e.tensor_copy(out=ctb[:], in_=ct[:]), reads=['ct'], writes=['ctb'])
        op('dve', lambda e: e.memset(epst[:], EPS), writes=['epst'])
        op('dve', lambda e: e.memset(onet[:], 1.0), writes=['onet'])
        op('act', lambda e: e.activation(out=silc[:], in_=vt[:, V_CT:V_CT + 32], func=AF.Silu), reads=['vt'], writes=['silc'])
        ident = ct[:, C_ID:C_ID + 128]
        identb = ctb[:, C_ID:C_ID + 128]
        onesb = ctb[:, C_ONES:C_ONES + 128]
        ones = ct[:, C_ONES:C_ONES + 128]

        def cast_weight(src, dst, rows, key):
            step = 512
            for r0 in range(0, rows, step):
                r1 = min(rows, r0 + step)
                S.cast_dma('pool', key, dst[r0:r1, :], src[r0:r1, :])
            S.res['WB_' + key] = {'w': ('c', key, 0), 'r': []}
            return ['WB_' + key]

        wkeys = {}
        for l in range(DEPTH):
            for nm in ('in', 'out', 'up', 'down'):
                wkeys[nm, l] = ['WB_wb%s%d_' % (nm if nm != 'down' else 'dn', l)]
        cast_plan = {
            'start': [('in', 0), ('out', 0)],
            'mix0': [('up', 0), ('down', 0), ('in', 1), ('out', 1)],
            'mix1': [('up', 1), ('down', 1)],
        }

        def issue_casts(tag, idx=None):
            plan = cast_plan[tag] if idx is None else cast_plan[tag][idx:idx + 1]
            for (nm, l) in plan:
                src = {'in': w_in, 'out': w_out, 'up': w_up, 'down': w_down}[nm][l]
                dst = {'in': wb_in, 'out': wb_out, 'up': wb_up, 'down': wb_down}[nm][l]
                rows = D_FF if nm == 'down' else D
                cast_weight(src, dst, rows, "wb%s%d_" % (nm if nm != 'down' else 'dn', l))

        issue_casts('start')

        def phase_ada(l):
            with contextlib.ExitStack() as ps:
                wsl = [ps.enter_context(sbuf_u("adaw%d" % i, [128, 16 * 512], F32)) for i in range(2)]
                pacc = [ps.enter_context(psum_u("adap%d" % i, [128, 512], F32)) for i in range(2)]
                wv = w_ada[l].rearrange("(kc p) n -> p kc n", p=128)
                for sl in range(24):
                    wt = wsl[sl % 2]
                    wk = "adaw%d" % (sl % 2)
                    dma('sp', wt[:].rearrange("p (kc n) -> p kc n", kc=16), wv[:, :, sl * 512:(sl + 1) * 512], writes=[wk])
                    pa = pacc[sl % 2]
                    pk = "adap%d" % (sl % 2)
                    for c4 in range(4):
                        for kc in range(16):
                            op('pe', lambda e, c4=c4, kc=kc: e.matmul(
                                pa[:, c4 * 2:c4 * 2 + 2], lhsT=wt[:, kc * 512 + c4 * 128: kc * 512 + c4 * 128 + 128],
                                rhs=silc[:, kc * 2:kc * 2 + 2], start=(kc == 0), stop=(kc == 15)),
                               reads=[wk, 'silc'], writes=[pk], inc=(kc == 15))
                    for c4 in range(4):
                        j = sl * 4 + c4
                        op('dve', lambda e, c4=c4, j=j: e.tensor_scalar(
                            out=adaT[:, j * 2:j * 2 + 2], in0=pa[:, c4 * 2:c4 * 2 + 2],
                            scalar1=vt[:, V_BADA + l * 96 + j: V_BADA + l * 96 + j + 1], scalar2=None, op0=ALU.add),
                           reads=[pk, 'vt'], writes=['adaT'])
                for n, (lnoff, shc, scc) in enumerate([(V_LN1, 0, 16), (V_LN2, 48, 64)]):
                    for fc in range(16):
                        a_out = modv[:, (n * 2) * 32 + fc * 2:(n * 2) * 32 + fc * 2 + 2]
                        b_out = modv[:, (n * 2 + 1) * 32 + fc * 2:(n * 2 + 1) * 32 + fc * 2 + 2]
                        op('dve', lambda e, a_out=a_out, fc=fc, scc=scc, lnoff=lnoff: e.tensor_scalar(
                            out=a_out, in0=adaT[:, (scc + fc) * 2:(scc + fc) * 2 + 2], scalar1=1.0,
                            scalar2=vt[:, lnoff + l * 16 + fc:lnoff + l * 16 + fc + 1], op0=ALU.add, op1=ALU.mult),
                           reads=['adaT', 'vt'], writes=['modv'])
                        op('dve', lambda e, b_out=b_out, fc=fc, shc=shc: e.tensor_copy(
                            out=b_out, in_=adaT[:, (shc + fc) * 2:(shc + fc) * 2 + 2]),
                           reads=['adaT'], writes=['modv'])
                dg = ps.enter_context(sbuf_u("dg", [128, 2 * 128], F32))
                pg = [ps.enter_context(psum_u("pg%d" % i, [128, 512], F32)) for i in range(2)]
                cnt = 0
                for gi, gch in enumerate([32, 80]):
                    for which in range(2):
                        for f4 in range(4):
                            pgt = pg[cnt % 2]
                            pgk = "pg%d" % (cnt % 2)
                            for q in range(4):
                                fc = f4 * 4 + q
                                dsl = dg[:, (q % 2) * 128:(q % 2) * 128 + 128]
                                dk_ = "dg%d" % (q % 2)
                                col = (gch + fc) * 2 + which
                                op('dve', lambda e, dsl=dsl, col=col: e.tensor_scalar(
                                    out=dsl, in0=ident, scalar1=adaT[:, col:col + 1], scalar2=None, op0=ALU.mult),
                                   reads=['adaT', 'ct'], writes=[dk_])
                                op('pe', lambda e, dsl=dsl, q=q, pgt=pgt: e.matmul(
                                    pgt[:, q * 128:(q + 1) * 128], lhsT=ones, rhs=dsl, start=True, stop=True),
                                   reads=[dk_, 'ct'], writes=[pgk])
                            base = (gi * 2 + which) * D + f4 * 512
                            op('act', lambda e, base=base, pgt=pgt: e.activation(
                                out=grow[:, base:base + 512], in_=pgt[:], func=AF.Copy),
                               reads=[pgk], writes=['grow'])
                            cnt += 1
            S.barrier()

        def phase_norm(src, n, groups):
            with contextlib.ExitStack() as ps:
                xt = [ps.enter_context(sbuf_u("nx%d" % i, [128, D], F32)) for i in range(2)]
                sq = ps.enter_context(sbuf_u("nsq", [128, D], BF16))
                st = [ps.enter_context(sbuf_u("nst%d" % i, [128, 4], F32)) for i in range(2)]
                hs = [ps.enter_context(sbuf_u("nh%d" % i, [128, 16 * 512], BF16)) for i in range(2)]
                pt = [ps.enter_context(psum_u("npt%d" % i, [128, 512], F32)) for i in range(4)]
                hFv = hF.rearrange("(fc p) t -> p fc t", p=128)
                ti = 0
                pi = 0
                for gi, (t0, ntl, which) in enumerate(groups):
                    hst = hs[gi % 2]
                    hk = "nh%d" % (gi % 2)
                    hview = hst[:].rearrange("p (fc t) -> p fc t", fc=16)
                    for s in range(ntl):
                        x_ = xt[ti % 2]
                        xk = "nx%d" % (ti % 2)
                        s_ = st[ti % 2]
                        sk = "nst%d" % (ti % 2)
                        ti += 1
                        tt = t0 + s * 128
                        dma('sp', x_[:], src[tt:tt + 128, :], writes=[xk])
                        op('act', lambda e, x_=x_, s_=s_: e.activation(out=sq[:], in_=x_[:], func=AF.Square, accum_out=s_[:, 0:1]),
                           reads=[xk], writes=['nsq', sk])
                        op('act', lambda e, s_=s_: e.activation(out=s_[:, 1:2], in_=s_[:, 0:1], func=AF.Ln, bias=epst[:, 0:1], scale=1.0 / D),
                           reads=[sk, 'epst'], writes=[sk])
                        op('act', lambda e, s_=s_: e.activation(out=s_[:, 2:3], in_=s_[:, 1:2], func=AF.Exp, scale=-0.5), reads=[sk], writes=[sk])
                        op('dve', lambda e, x_=x_, s_=s_: e.tensor_scalar(out=x_[:], in0=x_[:], scalar1=s_[:, 2:3], scalar2=None, op0=ALU.mult),
                           reads=[xk, sk], writes=[xk])
                        for f4 in range(4):
                            p_ = pt[pi % 4]
                            pk = "npt%d" % (pi % 4)
                            pi += 1
                            for q in range(4):
                                fc = f4 * 4 + q
                                op('pe', lambda e, p_=p_, q=q, fc=fc, x_=x_: e.transpose(
                                    out=p_[:, q * 128:(q + 1) * 128], in_=x_[:, fc * 128:(fc + 1) * 128], identity=ident),
                                   reads=[xk, 'ct'], writes=[pk], inc=(q == 3))
                            for q in range(4):
                                fc = f4 * 4 + q
                                acol = (n * 2) * 32 + fc * 2 + which
                                bcol = (n * 2 + 1) * 32 + fc * 2 + which
                                eng = 'act' if q % 2 == 0 else 'dve'
                                if eng == 'act':
                                    op('act', lambda e, p_=p_, q=q, fc=fc, s=s, acol=acol, bcol=bcol: e.activation(
                                        out=hview[:, fc, s * 128:(s + 1) * 128], in_=p_[:, q * 128:(q + 1) * 128], func=AF.Identity,
                                        scale=modv[:, acol:acol + 1], bias=modv[:, bcol:bcol + 1]),
                                       reads=[pk, 'modv'], writes=[hk])
                                else:
                                    op('dve', lambda e, p_=p_, q=q, fc=fc, s=s, acol=acol, bcol=bcol: e.tensor_scalar(
                                        out=hview[:, fc, s * 128:(s + 1) * 128], in0=p_[:, q * 128:(q + 1) * 128],
                                        scalar1=modv[:, acol:acol + 1], scalar2=modv[:, bcol:bcol + 1], op0=ALU.mult, op1=ALU.add),
                                       reads=[pk, 'modv'], writes=[hk])
                    dma('pool', hFv[:, :, t0:t0 + ntl * 128], hview[:, :, 0:ntl * 128], reads=[hk], writes=['hF'])
            S.barrier()

        def gemm(A, K, Wb, wkeylist, col0, ncols, tblocks, mode, epi, halo=False, name="g", abufs=2, npp=8):
            KC = K // 128
            with contextlib.ExitStack() as ps:
                maxn = max(n for _, n in tblocks)
                ab = [ps.enter_context(sbuf_u("%sa%d" % (name, i), [128, KC * maxn], BF16)) for i in range(abufs)]
                wsb = [ps.enter_context(sbuf_u("%sw%d" % (name, i), [128, KC * 512], BF16)) for i in range(2)]
                pp = [ps.enter_context(psum_u("%sp%d" % (name, i), [128, 512], F32)) for i in range(npp)]
                ctx = epi('alloc', ps)
                Av = A.rearrange("(kc p) t -> p kc t", p=128)
                Wv = Wb.rearrange("(kc p) n -> p kc n", p=128)
                wi = 0
                pidx = 0
                def load_A(bi):
                    t0, n = tblocks[bi]
                    a_ = ab[bi % abufs]
                    ak = "%sa%d" % (name, bi % abufs)
                    av = a_[:, 0:KC * n].rearrange("p (kc t) -> p kc t", kc=KC)
                    for k0 in range(0, KC, 8):
                        k1 = min(KC, k0 + 8)
                        dma('sp', av[:, k0:k1, :], Av[:, k0:k1, t0:t0 + n], reads=['A_' + name], writes=[ak + "_%d" % k0])

                load_A(0)
                for bi, (t0, n) in enumerate(tblocks):
                    a_ = ab[bi % abufs]
                    ak = "%sa%d" % (name, bi % abufs)
                    if abufs == 1 and bi > 0:
                        load_A(bi)
                    av = a_[:, 0:KC * n].rearrange("p (kc t) -> p kc t", kc=KC)
                    akeys = [ak + "_%d" % k0 for k0 in range(0, KC, 8)]
                    for c0 in range(col0, col0 + ncols, 512):
                        w_ = wsb[wi % 2]
                        wk = "%sw%d" % (name, wi % 2)
                        wi += 1
                        wv = w_[:].rearrange("p (kc n) -> p kc n", kc=KC)
                        for k0 in range(0, KC, 8):
                            k1 = min(KC, k0 + 8)
                            dma('sp', wv[:, k0:k1, :], Wv[:, k0:k1, c0:c0 + 512], reads=wkeylist, writes=[wk + "_%d" % k0])
                        wks = [wk + "_%d" % k0 for k0 in range(0, KC, 8)]
                        if abufs > 1 and c0 == col0 and bi + 1 < len(tblocks):
                            load_A(bi + 1)
                        if mode == 'F':
                            pieces = _split(n, 512)
                            for c4 in range(4):
                                banks = []
                                for (o, sz) in pieces:
                                    p_ = pp[pidx % npp]
                                    pk = "%sp%d" % (name, pidx % npp)
                                    pidx += 1
                                    for kc in range(KC):
                                        op('pe', lambda e, p_=p_, sz=sz, o=o, kc=kc, c4=c4, wv=wv, av=av: e.matmul(
                                            p_[:, 0:sz], lhsT=wv[:, kc, c4 * 128:(c4 + 1) * 128], rhs=av[:, kc, o:o + sz],
                                            start=(kc == 0), stop=(kc == KC - 1)),
                                           reads=[wks[kc // 8], akeys[kc // 8]], writes=[pk], inc=(kc == KC - 1))
                                    banks.append((p_, pk, o, sz))
                                epi('F', ctx, banks, c0 + c4 * 128, t0, n)
                        else:
                            for s in range(n // 128):
                                p_ = pp[pidx % npp]
                                pk = "%sp%d" % (name, pidx % npp)
                                pidx += 1
                                for kc in range(KC):
                                    op('pe', lambda e, p_=p_, kc=kc, s=s, wv=wv, av=av: e.matmul(
                                        p_[:, :], lhsT=av[:, kc, s * 128:(s + 1) * 128], rhs=wv[:, kc, :],
                                        start=(kc == 0), stop=(kc == KC - 1)),
                                       reads=[wks[kc // 8], akeys[kc // 8]], writes=[pk], inc=(kc == KC - 1))
                                epi('T', ctx, (p_, pk), c0, t0 + s * 128, 128)
                epi('end', ctx)
            S.barrier()

        def lb_setup(l):
            op('dve', lambda e: e.tensor_scalar(out=qkw[:, 0:1], in0=vt[:, V_QN + l:V_QN + l + 1], scalar1=float(HD ** -0.5), scalar2=None, op0=ALU.mult),
               reads=['vt'], writes=['qkw'])
            op('dve', lambda e: e.tensor_copy(out=qkw[:, 1:2], in_=vt[:, V_KN + l:V_KN + l + 1]), reads=['vt'], writes=['qkw'])
            _lb_setup(l)

        def _lb_setup(l):
            if l == 0:
                op('dve', lambda e: e.memset(lbt[:], 0.0), writes=['lbt'])
                op('dve', lambda e: e.memset(omt[:], 1.0), writes=['omt'])
            else:
                with contextlib.ExitStack() as ps:
                    lg = ps.enter_context(sbuf_u("lg", [128, 2 * DEPTH * 512], F32))
                    dma('sp', lg[0:64, :], lbl[:, :], writes=['lg'])
                    dma('sp', lg[64:128, :], lbl[:, :], writes=['lg2'])
                    for dr in range(2):
                        op('dve', lambda e, dr=dr: e.tensor_tensor(out=lbt[:, dr * 512:(dr + 1) * 512], in0=lg[:, (dr * 2) * 512:(dr * 2 + 1) * 512],
                                                                 in1=lg[:, (dr * 2 + 1) * 512:(dr * 2 + 2) * 512], op=ALU.subtract), reads=['lg', 'lg2'], writes=['lbt'])
                    op('act', lambda e: e.activation(out=lbt[:], in_=lbt[:], func=AF.Exp), reads=['lbt'], writes=['lbt'])
                    op('dve', lambda e: e.tensor_scalar(out=lbt[:], in0=lbt[:], scalar1=1.0, scalar2=None, op0=ALU.add), reads=['lbt'], writes=['lbt'])
                    op('dve', lambda e: e.reciprocal(out=lbt[:], in_=lbt[:]), reads=['lbt'], writes=['lbt'])
                    op('dve', lambda e: e.tensor_scalar(out=omt[:], in0=lbt[:], scalar1=-1.0, scalar2=1.0, op0=ALU.mult, op1=ALU.add),
                       reads=['lbt'], writes=['omt'])
                    S.barrier()

        def epi_inproj_qk(kind, *a):
            if kind == 'alloc':
                ps = a[0]
                c = {'i': 0, 'pend': None, 'si': 0}
                c['raw'] = [ps.enter_context(sbuf_u("qkraw%d" % i, [128, 512], F32)) for i in range(3)]
                c['sq'] = [ps.enter_context(sbuf_u("qksq%d" % i, [128, 512], BF16)) for i in range(3)]
                c['rs'] = [ps.enter_context(sbuf_u("qkrs%d" % i, [128, 512], F32)) for i in range(3)]
                c['sf'] = [ps.enter_context(sbuf_u("qksf%d" % i, [128, 1024], BF16)) for i in range(2)]
                c['pn'] = ps.enter_context(psum_u("qkpn", [128, 512], F32))
                return c

            def flush(c):
                job = c['pend']
                if job is None:
                    return
                c['pend'] = None
                (i, sz, o, which, sf, sfk, last, col, t0, n) = job
                raw, sq, rs = c['raw'][i], c['sq'][i], c['rs'][i]
                rawk, sqk, rsk = "qkraw%d" % i, "qksq%d" % i, "qkrs%d" % i
                pn = c['pn']
                op('pe', lambda e: e.matmul(pn[:, 0:sz], lhsT=onesb, rhs=sq[:, 0:sz], start=True, stop=True), reads=[sqk, 'ctb'], writes=['qkpn'])
                op('act', lambda e: e.activation(out=rs[:, 0:sz], in_=pn[:, 0:sz], func=AF.Ln, bias=epst[:, 0:1], scale=1.0 / 128), reads=['qkpn', 'epst'], writes=[rsk])
                op('act', lambda e: e.activation(out=rs[:, 0:sz], in_=rs[:, 0:sz], func=AF.Exp, scale=-0.5), reads=[rsk], writes=[rsk])
                op('dve', lambda e: e.scalar_tensor_tensor(out=sf[:, o:o + sz], in0=raw[:, 0:sz], scalar=qkw[:, which:which + 1], in1=rs[:, 0:sz],
                                                         op0=ALU.mult, op1=ALU.mult), reads=[rawk, rsk, 'qkw'], writes=[sfk])
                if last:
                    dma('pool', U_F[col:col + 128, t0:t0 + n], sf[:, 0:n], reads=[sfk], writes=['U_F'])

            if kind == 'end':
                flush(a[0])
                return
            c, banks, col, t0, n = a
            which = 0 if col >= OFF_NQ else 1
            si = c['si'] % 2
            c['si'] += 1
            sf = c['sf'][si]
            sfk = "qksf%d" % si
            for bi_, (p_, pk, o, sz) in enumerate(banks):
                i = c['i'] % 3
                c['i'] += 1
                raw, sq = c['raw'][i], c['sq'][i]
                rawk, sqk = "qkraw%d" % i, "qksq%d" % i
                op('act', lambda e, p_=p_, sz=sz, sq=sq: e.activation(out=sq[:, 0:sz], in_=p_[:, 0:sz], func=AF.Square), reads=[pk], writes=[sqk])
                op('dve', lambda e, p_=p_, sz=sz, raw=raw: e.tensor_copy(out=raw[:, 0:sz], in_=p_[:, 0:sz]), reads=[pk, sqk], writes=[rawk])
                flush(c)
                c['pend'] = (i, sz, o, which, sf, sfk, bi_ == len(banks) - 1, col, t0, n)

        def epi_inproj(kind, *a):
            if kind == 'end':
                return
            if kind == 'alloc':
                ps = a[0]
                c = {}
                c['sf'] = [ps.enter_context(sbuf_u("ipf%d" % i, [128, 1024], BF16)) for i in range(3)]
                c['st'] = [ps.enter_context(sbuf_u("ipt%d" % i, [128, 512], F32)) for i in range(3)]
                c['stb'] = [ps.enter_context(sbuf_u("iptb%d" % i, [128, 512], BF16)) for i in range(3)]
                c['st2'] = [ps.enter_context(sbuf_u("ipt2_%d" % i, [128, 512], F32)) for i in range(3)]
                c['sq'] = [ps.enter_context(sbuf_u("ipq%d" % i, [128, 1024], F32)) for i in range(2)]
                c['sq2'] = [ps.enter_context(sbuf_u("ipq2_%d" % i, [128, 1024], F32)) for i in range(2)]
                c['qi'] = 0
                c['i'] = 0
                return c
            if kind == 'F' and OFF_HQ <= a[2] < OFF_HQ + 512:
                c, banks, col, t0, n = a
                i = c['qi'] % 2
                c['qi'] += 1
                sq, sq2 = c['sq'][i], c['sq2'][i]
                sqk, sq2k = "ipq%d" % i, "ipq2_%d" % i
                for (p_, pk, o, sz) in banks:
                    op('act', lambda e, p_=p_, o=o, sz=sz: e.activation(out=sq[:, o:o + sz], in_=p_[:, 0:sz], func=AF.Exp, scale=-1.0), reads=[pk], writes=[sqk])
                    op('act', lambda e, o=o, sz=sz: e.activation(out=sq[:, o:o + sz], in_=sq[:, o:o + sz], func=AF.Ln, bias=onet[:, 0:1]), reads=[sqk, 'onet'], writes=[sqk])
                    op('act', lambda e, o=o, sz=sz: e.activation(out=sq[:, o:o + sz], in_=sq[:, o:o + sz], func=AF.Exp, scale=-1.0), reads=[sqk], writes=[sqk])
                    op('dve', lambda e, p_=p_, o=o, sz=sz: e.tensor_tensor(out=sq2[:, o:o + sz], in0=p_[:, 0:sz], in1=sq[:, o:o + sz], op=ALU.mult),
                       reads=[pk, sqk], writes=[sq2k])
                dma('pool', U_qs[col - OFF_HQ:col - OFF_HQ + 128, t0:t0 + n], sq2[:, 0:n], reads=[sq2k], writes=['U_qs'])
            elif kind == 'F':
                c, banks, col, t0, n = a
                i = c['i'] % 3
                c['i'] += 1
                sf = c['sf'][i]
                sk = "ipf%d" % i
                for bi_, (p_, pk, o, sz) in enumerate(banks):
                    if bi_ % 2 == 0:
                        op('act', lambda e, p_=p_, o=o, sz=sz: e.activation(out=sf[:, o:o + sz], in_=p_[:, 0:sz], func=AF.Copy),
                           reads=[pk], writes=[sk])
                    else:
                        op('dve', lambda e, p_=p_, o=o, sz=sz: e.tensor_copy(out=sf[:, o:o + sz], in_=p_[:, 0:sz]),
                           reads=[pk], writes=[sk])
                dma('pool', U_F[col:col + 128, t0:t0 + n], sf[:, 0:n], reads=[sk], writes=['U_F'])
            else:
                c, (p_, pk), c0, tt, _ = a
                i = c['i'] % 3
                c['i'] += 1
                if c0 < 1024:
                    s_ = c['st'][i]
                    s2 = c['st2'][i]
                    sk = "ipt%d" % i
                    s2k = "ipt2_%d" % i
                    lbs = lbt[:, c0:c0 + 512]
                    oms = omt[:, c0:c0 + 512]
                    op('act', lambda e: e.activation(out=s_[:], in_=p_[:], func=AF.Exp, scale=-1.0), reads=[pk], writes=[sk])
                    op('act', lambda e: e.activation(out=s_[:], in_=s_[:], func=AF.Ln, bias=onet[:, 0:1]), reads=[sk, 'onet'], writes=[sk])
                    op('act', lambda e: e.activation(out=s_[:], in_=s_[:], func=AF.Exp, scale=-1.0), reads=[sk], writes=[sk])
                    op('dve', lambda e: e.tensor_tensor(out=s2[:], in0=s_[:], in1=oms, op=ALU.mult), reads=[sk, 'omt'], writes=[s2k])
                    op('dve', lambda e: e.scalar_tensor_tensor(out=s_[:], in0=s2[:], scalar=1e-30, in1=lbs, op0=ALU.max, op1=ALU.add),
                       reads=[s2k, 'lbt'], writes=[sk])
                    op('act', lambda e: e.activation(out=s_[:], in_=s_[:], func=AF.Ln), reads=[sk], writes=[sk])
                    op('dve', lambda e: e.tensor_tensor(out=s2[:], in0=oms, in1=s2[:], op=ALU.subtract), reads=[s2k, 'omt'], writes=[s2k])
                    dma('pool', U_f[tt:tt + 128, c0:c0 + 512], s_[:], reads=[sk], writes=['U_f'])
                    dma('pool', U_k[tt:tt + 128, c0:c0 + 512], s2[:], reads=[s2k], writes=['U_k'])
                else:
                    s_ = c['stb'][i]
                    sk = "iptb%d" % i
                    op('dve', lambda e: e.tensor_copy(out=s_[:], in_=p_[:]), reads=[pk], writes=[sk])
                    if c0 == OFF_HI:
                        dma('pool', U_i[tt:tt + 128, :], s_[:], reads=[sk], writes=['U_i'])
                    else:
                        dma('pool', U_nv[tt:tt + 128, c0 - OFF_NV:c0 - OFF_NV + 512], s_[:], reads=[sk], writes=['U_nv'])


        def block_norm_store(ps_ctx, src_ap, n, wcol, dst_rows, t0, extra_mul=None, keys=()):
            c = ps_ctx
            i = c['i'] % 2
            c['i'] += 1
            sq, rs, ob, pn = c['sq'][i], c['rs'][i], c['ob'][i], c['pn'][i]
            sqk, rsk, obk, pnk = "bn_sq%d" % i, "bn_rs%d" % i, "bn_ob%d" % i, "bn_pn%d" % i
            op('act', lambda e: e.activation(out=sq[:, 0:n], in_=src_ap, func=AF.Square), reads=list(keys), writes=[sqk])
            op('pe', lambda e: e.matmul(pn[:, 0:n], lhsT=onesb, rhs=sq[:, 0:n], start=True, stop=True), reads=[sqk, 'ctb'], writes=[pnk])
            op('act', lambda e: e.activation(out=rs[:, 0:n], in_=pn[:, 0:n], func=AF.Ln, bias=epst[:, 0:1], scale=1.0 / 128),
               reads=[pnk, 'epst'], writes=[rsk])
            op('act', lambda e: e.activation(out=rs[:, 0:n], in_=rs[:, 0:n], func=AF.Exp, scale=-0.5), reads=[rsk], writes=[rsk])
            if extra_mul is None:
                op('dve', lambda e: e.scalar_tensor_tensor(out=ob[:, 0:n], in0=src_ap, scalar=vt[:, wcol:wcol + 1], in1=rs[:, 0:n],
                                                         op0=ALU.mult, op1=ALU.mult), reads=list(keys) + [rsk, 'vt'], writes=[obk])
            else:
                op('dve', lambda e: e.scalar_tensor_tensor(out=rs[:, 0:n], in0=src_ap, scalar=vt[:, wcol:wcol + 1], in1=rs[:, 0:n],
                                                         op0=ALU.mult, op1=ALU.mult), reads=list(keys) + [rsk, 'vt'], writes=[rsk])
                emul, ekeys = extra_mul
                op('dve', lambda e: e.tensor_tensor(out=ob[:, 0:n], in0=rs[:, 0:n], in1=emul, op=ALU.mult),
                   reads=[rsk] + list(ekeys), writes=[obk])
            dma('sp', mixF[dst_rows:dst_rows + 128, t0:t0 + n], ob[:, 0:n], reads=[obk], writes=['mixF'])

        def bn_alloc(ps):
            c = {'i': 0}
            c['sq'] = [ps.enter_context(sbuf_u("bnsq%d" % i, [128, 512], BF16)) for i in range(2)]
            c['rs'] = [ps.enter_context(sbuf_u("bnrs%d" % i, [128, 512], F32)) for i in range(2)]
            c['ob'] = [ps.enter_context(sbuf_u("bnob%d" % i, [128, 512], BF16)) for i in range(2)]
            c['pn'] = [ps.enter_context(psum_u("bnpn%d" % i, [128, 512], F32)) for i in range(2)]
            return c

        def phase_hgrn(l):
            NSC = 4
            W_ = NSC * 128
            NQ = NSC * 64
            with contextlib.ExitStack() as ps:
                bn = bn_alloc(ps)
                Obs = [ps.enter_context(sbuf_u("hgO%d" % i, [128, T], F32)) for i in range(2)]
                St = [ps.enter_context(sbuf_u("hgS%d" % i, [128, 128], F32)) for i in range(2)]
                gsb = [ps.enter_context(sbuf_u("hggs%d" % i, [128, 512], BF16)) for i in range(2)]
                gsf = [ps.enter_context(sbuf_u("hggf%d" % i, [128, 512], F32)) for i in range(2)]
                NBUF = 2
                names = [('A', [128, W_], F32), ('B', [128, W_], F32), ('C', [128, W_], F32), ('v', [128, W_], BF16), ('q', [128, NQ], BF16),
                         ('qe', [128, NQ], F32), ('qs', [128, NQ], F32), ('E', [128, NSC * 66], F32), ('En', [128, W_], F32), ('kt', [128, W_], BF16),
                         ('qt', [128, NQ], BF16), ('ktF', [128, NQ], BF16), ('amf', [64, NQ], F32), ('am', [64, NQ], BF16)]
                BUF = [{nm: ps.enter_context(sbuf_u("hg%s%d" % (nm, i), shp, dt)) for (nm, shp, dt) in names} for i in range(NBUF)]
                Sp = [ps.enter_context(sbuf_u("hgSp%d" % i, [128, 128], BF16)) for i in range(2)]
                pA = ps.enter_context(psum_u("hgpA", [128, 512], F32))
                pBB = ps.enter_context(psum_u("hgpBB", [128, 512], F32))
                pT = ps.enter_context(psum_u("hgpT", [128, 1024], BF16))
                pAtt = ps.enter_context(psum_u("hgpAtt", [128, 512], F32))
                pKv = ps.enter_context(psum_u("hgpKv", [128, 512], F32))
                pO = ps.enter_context(psum_u("hgpO", [128, 512], F32))
                nlat = SEQ // (NSC * 64)
                items = []
                for hd in range(4):
                    for dr in range(2):
                        scs = [0] + [CTX + i * NSC * 64 for i in (range(nlat) if dr == 0 else reversed(range(nlat)))]
                        for j, t0 in enumerate(scs):
                            items.append((hd, dr, j, t0, len(scs)))
                n = NSC * 64
                state = {'sidx': 0, 'spi': 0}

                def bufs(i):
                    b2 = i % NBUF
                    Bf = BUF[b2]
                    return [Bf[nm] for (nm, _, _) in names], ["hg%s%d" % (nm, b2) for (nm, _, _) in names]

                def prep(i, part):
                    hd, dr, j, t0, _ = items[i]
                    (A_, B_, C_, v_, q_, qe_, qs_, E_, En_, kt_, qt_, ktF_, amf_, am_), (Ak, Bk, Ck, vk, qk, qek, qsk, Ek, Enk, ktk, qtk, ktFk, amfk, amk) = bufs(i)
                    if part != 1:
                        return
                    fcol = dr * 512 + hd * 128
                    dma('sp', A_[0:64, :].rearrange("p (c d) -> p c d", c=NSC),
                        U_f[t0:t0 + n, fcol:fcol + 128].rearrange("(c p) d -> p c d", p=64), writes=[Ak + "_0"])
                    for half in range(2):
                        dma('sp', C_[half * 64:(half + 1) * 64, :].rearrange("p (c d) -> p c d", c=NSC),
                            U_k[t0:t0 + n, fcol:fcol + 128].rearrange("(c p) d -> p c d", p=64), writes=[Ck + "_%d" % half])
                        dma('sp', v_[half * 64:(half + 1) * 64, :].rearrange("p (c d) -> p c d", c=NSC),
                            U_i[t0:t0 + n, hd * 128:hd * 128 + 128].rearrange("(c p) d -> p c d", p=64), writes=[vk + "_%d" % half])
                    dma('sp', qs_[:], U_qs[hd * 128:hd * 128 + 128, t0:t0 + n], writes=[qsk])

                def partAB(i, stage_):
                    hd, dr, j, t0, nsc_chain = items[i]
                    Ob = Obs[hd % 2]
                    (A_, B_, C_, v_, q_, qe_, qs_, E_, En_, kt_, qt_, ktF_, amf_, am_), (Ak, Bk, Ck, vk, qk, qek, qsk, Ek, Enk, ktk, qtk, ktFk, amfk, amk) = bufs(i)
                    Aks = [Ak + "_0"]
                    Cks = [Ck + "_0", Ck + "_1"]
                    vks = [vk + "_0", vk + "_1"]
                    mext = ct[0:64, (C_MEXT_F if dr == 0 else C_MEXT_B):(C_MEXT_F if dr == 0 else C_MEXT_B) + 66]
                    mcc = ct[0:64, (C_MCC_F if dr == 0 else C_MCC_B):(C_MCC_F if dr == 0 else C_MCC_B) + 128]
                    mkr = ct[0:64, (C_MKR_F if dr == 0 else C_MKR_B):(C_MKR_F if dr == 0 else C_MKR_B) + NQ]
                    def stA1():
                        for c in range(NSC):
                            op('pe', lambda e, c=c: e.matmul(pA[:, c * 66:(c + 1) * 66], lhsT=A_[0:64, c * 128:(c + 1) * 128], rhs=mext, start=True, stop=True),
                               reads=Aks + ['ct'], writes=['hgpA'], inc=(c == NSC - 1))
                        for c in range(NSC):
                            op('pe', lambda e, c=c: e.matmul(pBB[:, c * 128:(c + 1) * 128], lhsT=mcc, rhs=A_[0:64, c * 128:(c + 1) * 128], start=True, stop=True),
                               reads=Aks + ['ct'], writes=['hgpBB'], inc=(c == NSC - 1))

                    def stA1act():
                        op('act', lambda e: e.activation(out=E_[:], in_=pA[:, 0:NSC * 66], func=AF.Exp), reads=['hgpA'], writes=[Ek])
                        op('act', lambda e: e.activation(out=En_[:], in_=pBB[:, :], func=AF.Exp, scale=-1.0), reads=['hgpBB'], writes=[Enk])

                    def stA2():
                        op('dve', lambda e: e.tensor_tensor(out=kt_[:], in0=C_[:], in1=En_[:], op=ALU.mult), reads=Cks + [Enk], writes=[ktk])
                        op('dve', lambda e: e.tensor_tensor(out=qt_[:].rearrange("p (c w) -> p c w", c=NSC), in0=qs_[:].rearrange("p (c w) -> p c w", c=NSC),
                                                           in1=E_[:].rearrange("p (c w) -> p c w", c=NSC)[:, :, 0:64], op=ALU.mult), reads=[qsk, Ek], writes=[qtk])
                        for c in range(NSC):
                            op('pe', lambda e, c=c: e.transpose(out=pT[:, c * 64:(c + 1) * 64], in_=kt_[0:64, c * 128:(c + 1) * 128], identity=identb[0:64, 0:64]),
                               reads=[ktk, 'ctb'], writes=['hgpT'], inc=(c == NSC - 1))
                        op('act', lambda e: e.activation(out=ktF_[:], in_=pT[:, 0:NQ], func=AF.Copy), reads=['hgpT'], writes=[ktFk])

                    def stA3():
                        for c in range(NSC):
                            op('pe', lambda e, c=c: e.matmul(pAtt[0:64, c * 64:(c + 1) * 64], lhsT=ktF_[:, c * 64:(c + 1) * 64], rhs=qt_[:, c * 64:(c + 1) * 64],
                                                            start=True, stop=True), reads=[ktFk, qtk], writes=['hgpAtt'], inc=(c == NSC - 1))
                        op('dve', lambda e: e.tensor_scalar(out=amf_[:], in0=pAtt[0:64, 0:NQ], scalar1=1e30, scalar2=-1e30, op0=ALU.min, op1=ALU.max),
                           reads=['hgpAtt'], writes=[amfk])
                        op('dve', lambda e: e.tensor_tensor(out=am_[:], in0=amf_[:], in1=mkr, op=ALU.mult), reads=[amfk, 'ct'], writes=[amk])
                        for c in range(NSC):
                            op('pe', lambda e, c=c: e.matmul(pKv[:, c * 128:(c + 1) * 128], lhsT=kt_[64:128, c * 128:(c + 1) * 128], rhs=v_[64:128, c * 128:(c + 1) * 128],
                                                            start=True, stop=True), reads=[ktk] + vks, writes=['hgpKv'], inc=(c == NSC - 1))

                    def stB():
                        if j == 0:
                            state['sidx'] = 0
                            op('dve', lambda e: e.memset(St[0][:], 0.0), writes=['hgS0'])
                        order = list(range(NSC)) if dr == 0 else list(reversed(range(NSC)))
                        for c in order:
                            sidx = state['sidx']
                            Sold, Snew = St[sidx % 2], St[(sidx + 1) % 2]
                            Soldk, Snewk = "hgS%d" % (sidx % 2), "hgS%d" % ((sidx + 1) % 2)
                            state['sidx'] += 1
                            Sp_ = Sp[state['spi'] % 2]
                            Spk = "hgSp%d" % (state['spi'] % 2)
                            state['spi'] += 1
                            op('act', lambda e, c=c, Sp_=Sp_, Sold=Sold: e.activation(out=Sp_[:], in_=Sold[:], func=AF.Identity, scale=E_[:, c * 66 + 65:c * 66 + 66]),
                               reads=[Soldk, Ek], writes=[Spk])
                            op('pe', lambda e, c=c, Sp_=Sp_: e.matmul(pO[:, c * 64:(c + 1) * 64], lhsT=Sp_[:], rhs=qt_[:, c * 64:(c + 1) * 64], start=True, stop=False),
                               reads=[Spk, qtk], writes=['hgpO'], inc=False)
                            op('pe', lambda e, c=c: e.matmul(pO[:, c * 64:(c + 1) * 64], lhsT=v_[0:64, c * 128:(c + 1) * 128], rhs=am_[:, c * 64:(c + 1) * 64], start=False, stop=True),
                               reads=vks + [amk], writes=['hgpO'])
                            op('dve', lambda e, c=c, Sold=Sold, Snew=Snew: e.scalar_tensor_tensor(
                                out=Snew[:], in0=Sold[:], scalar=E_[:, c * 66 + 64:c * 66 + 65], in1=pKv[:, c * 128:(c + 1) * 128], op0=ALU.mult, op1=ALU.add),
                               reads=[Soldk, Ek, 'hgpKv'], writes=[Snewk])
                        ok_ = "hgO%d_%d" % (hd % 2, t0 // 256)
                        if dr == 0:
                            op('act', lambda e: e.activation(out=Ob[:, t0:t0 + n], in_=pO[:, 0:n], func=AF.Copy), reads=['hgpO'], writes=[ok_])
                        else:
                            op('dve', lambda e: e.tensor_tensor(out=Ob[:, t0:t0 + n], in0=pO[:, 0:n], in1=Ob[:, t0:t0 + n], op=ALU.add),
                               reads=['hgpO', ok_], writes=[ok_])

                    {0: stA1, 1: stA2, 2: stA3, 3: stB, 4: stA1act}[stage_]()

                def readout(hd):
                    Ob = Obs[hd % 2]
                    for bi_, (t0, nb) in enumerate([(0, 256)] + [(256 + i * 512, 512) for i in range(8)]):
                        b2 = bi_ % 2
                        gk_, gfk_ = "hggs%d" % b2, "hggf%d" % b2
                        dma('sp', gsb[b2][:, 0:nb], U_F[OFF_HGG + hd * 128:OFF_HGG + hd * 128 + 128, t0:t0 + nb], writes=[gk_])
                        op('act', lambda e, b2=b2, nb=nb: e.activation(out=gsf[b2][:, 0:nb], in_=gsb[b2][:, 0:nb], func=AF.Exp, scale=-1.0), reads=[gk_], writes=[gfk_])
                        op('act', lambda e, b2=b2, nb=nb: e.activation(out=gsf[b2][:, 0:nb], in_=gsf[b2][:, 0:nb], func=AF.Ln, bias=onet[:, 0:1]), reads=[gfk_, 'onet'], writes=[gfk_])
                        op('act', lambda e, b2=b2, nb=nb: e.activation(out=gsf[b2][:, 0:nb], in_=gsf[b2][:, 0:nb], func=AF.Exp, scale=-1.0), reads=[gfk_], writes=[gfk_])
                        op('dve', lambda e, b2=b2, nb=nb: e.tensor_tensor(out=gsf[b2][:, 0:nb], in0=gsf[b2][:, 0:nb], in1=gsb[b2][:, 0:nb], op=ALU.mult), reads=[gfk_, gk_], writes=[gfk_])
                        okeys = ["hgO%d_%d" % (hd % 2, jj) for jj in range(t0 // 256, (t0 + nb) // 256)]
                        block_norm_store(bn, Ob[:, t0:t0 + nb], nb, V_HGN + l, hd * 128, t0, extra_mul=(gsf[b2][:, 0:nb], [gfk_]), keys=okeys)

                prep(0, 1)
                prep(0, 2)
                partAB(0, 0)
                for i in range(len(items)):
                    nxt = i + 1 < len(items)
                    partAB(i, 4)
                    partAB(i, 1)
                    if nxt:
                        prep(i + 1, 1)
                    partAB(i, 2)
                    if nxt:
                        prep(i + 1, 2)
                        partAB(i + 1, 0)
                    partAB(i, 3)
                    hd, dr, j, t0, nsc_chain = items[i]
                    if dr == 1 and j == nsc_chain - 1:
                        readout(hd)
            S.barrier()

        def phase_na(l, skip_ctx_out=False, cast_tag=None):
            with contextlib.ExitStack() as ps:
                bn = bn_alloc(ps)
                qn = ps.enter_context(sbuf_u("naq", [128, T], BF16))
                kn = ps.enter_context(sbuf_u("nak", [128, T], BF16))
                raw = [ps.enter_context(sbuf_u("naraw%d" % i, [128, 512], BF16)) for i in range(2)]
                nsq = [ps.enter_context(sbuf_u("nansq%d" % i, [128, 512], BF16)) for i in range(2)]
                nrs = [ps.enter_context(sbuf_u("nanrs%d" % i, [128, 512], F32)) for i in range(2)]
                v0 = ps.enter_context(sbuf_u("nav0", [128, NT * 128], BF16))
                v1 = ps.enter_context(sbuf_u("nav1", [128, 31 * 128], BF16))
                nbf = ps.enter_context(sbuf_u("nanbf", [128, 14 * 64], F32))
                nbb = ps.enter_context(sbuf_u("nanbb", [128, 14 * 64], BF16))
                wsc = ps.enter_context(sbuf_u("nawsc", [128, 2], F32))
                PT = [ps.enter_context(sbuf_u("naPT%d" % i, [128, 512], BF16)) for i in range(2)]
                rd = [ps.enter_context(sbuf_u("nard%d" % i, [128, 256], F32)) for i in range(2)]
                ob = [ps.enter_context(sbuf_u("naob%d" % i, [128, 512], F32)) for i in range(2)]
                pS = [ps.enter_context(psum_u("napS%d" % i, [128, 512], F32)) for i in range(2)]
                pO = [ps.enter_context(psum_u("napO%d" % i, [128, 512], F32)) for i in range(2)]
                pD = [ps.enter_context(psum_u("napD%d" % i, [128, 512], F32)) for i in range(2)]
                pN = bn['pn'][0]
                op('dve', lambda e: e.tensor_scalar(out=wsc[:, 0:1], in0=vt[:, V_QN + l:V_QN + l + 1], scalar1=float(HD ** -0.5), scalar2=None, op0=ALU.mult),
                   reads=['vt'], writes=['nawsc'])
                op('dve', lambda e: e.tensor_copy(out=wsc[:, 1:2], in_=vt[:, V_KN + l:V_KN + l + 1]), reads=['vt'], writes=['nawsc'])
                blocks = [(0, 256)] + [(256 + i * 512, 512) for i in range(8)]
                ri = 0
                rowi = 0
                for hd in range(N_NA):
                    if cast_tag is not None and hd % 2 == 0 and hd // 2 < len(cast_plan[cast_tag]):
                        issue_casts(cast_tag, hd // 2)
                    dma('sp', v0[:].rearrange("p (j d) -> p j d", j=NT), U_nv[0:T, hd * 128:hd * 128 + 128].rearrange("(j p) d -> p j d", p=128), writes=['nav0'])
                    dma('sp', v1[:].rearrange("p (j d) -> p j d", j=31),
                        U_nv[CTX + 64:CTX + 64 + 31 * 128, hd * 128:hd * 128 + 128].rearrange("(j p) d -> p j d", p=128), writes=['nav1'])
                    dma('sp', nbf[:], natab[l, hd], writes=['nanbf'])
                    op('dve', lambda e: e.tensor_copy(out=nbb[:], in_=nbf[:]), reads=['nanbf'], writes=['nanbb'])
                    dma('sp', qn[:], U_F[OFF_NQ + hd * 128:OFF_NQ + hd * 128 + 128, :], writes=['naq'])
                    dma('sp', kn[:], U_F[OFF_NK + hd * 128:OFF_NK + hd * 128 + 128, :], writes=['nak'])

                    def att_scores(job):
                        nonlocal rowi
                        qlo, nq, tiles = job['qlo'], job['nq'], job['tiles']
                        b2 = rowi % 2
                        rowi += 1
                        job['b2'] = b2
                        pS_ = pS[b2]
                        pSk = "napS%d" % b2
                        nt_ = len(tiles)
                        for i, (klo, vap, bap) in enumerate(tiles):
                            last = (i == nt_ - 1)
                            op('pe', lambda e, i=i, klo=klo, bap=bap: e.matmul(pS_[:, i * nq:(i + 1) * nq], lhsT=kn[:, klo:klo + 128], rhs=qn[:, qlo:qlo + nq],
                                                                              start=True, stop=(bap is None)), reads=['nak', 'naq'], writes=[pSk], inc=(bap is None and last))
                            if bap is not None:
                                op('pe', lambda e, i=i, bap=bap: e.matmul(pS_[:, i * nq:(i + 1) * nq], lhsT=identb, rhs=bap, start=False, stop=True),
                                   reads=['nanbb', 'ctb'], writes=[pSk], inc=last)

                    def att_finish(job):
                        qlo, nq, tiles, obuf, ocol, obk, b2 = job['qlo'], job['nq'], job['tiles'], job['obuf'], job['ocol'], job['obk'], job['b2']
                        pS_, pO_, pD_, PT_, rd_ = pS[b2], pO[b2], pD[b2], PT[b2], rd[b2]
                        pSk, pOk, pDk, PTk, rdk = "napS%d" % b2, "napO%d" % b2, "napD%d" % b2, "naPT%d" % b2, "nard%d" % b2
                        nt_ = len(tiles)
                        W_ = nt_ * nq
                        op('act', lambda e: e.activation(out=PT_[:, 0:W_], in_=pS_[:, 0:W_], func=AF.Exp), reads=[pSk], writes=[PTk])
                        for i, (klo, vap, bap) in enumerate(tiles):
                            op('pe', lambda e, i=i, vap=vap: e.matmul(pO_[:, 0:nq], lhsT=vap, rhs=PT_[:, i * nq:(i + 1) * nq], start=(i == 0), stop=(i == nt_ - 1)),
                               reads=['nav0', 'nav1', PTk], writes=[pOk], inc=(i == nt_ - 1))
                        for i in range(nt_):
                            op('pe', lambda e, i=i: e.matmul(pD_[:, 0:nq], lhsT=onesb, rhs=PT_[:, i * nq:(i + 1) * nq], start=(i == 0), stop=(i == nt_ - 1)),
                               reads=['ctb', PTk], writes=[pDk], inc=(i == nt_ - 1))
                        op('dve', lambda e: e.reciprocal(out=rd_[:, 0:nq], in_=pD_[:, 0:nq]), reads=[pDk], writes=[rdk])
                        op('dve', lambda e: e.tensor_tensor(out=obuf[:, ocol:ocol + nq], in0=pO_[:, 0:nq], in1=rd_[:, 0:nq], op=ALU.mult),
                           reads=[pOk, rdk], writes=[obk])
                        if job.get('norm') is not None:
                            nn, t0n = job['norm']
                            block_norm_store(bn, obuf[:, 0:nn], nn, V_NAON + l * 8 + hd, 512 + hd * 128, t0n, keys=[obk])

                    ctiles = [(0, v0[:, 0:128], None), (128, v0[:, 128:256], None)]
                    jobs = []
                    oi = 0
                    ob_ = ob[oi % 2]
                    obk = "naob%d" % (oi % 2)
                    oi += 1
                    if not skip_ctx_out:
                        jobs.append({'qlo': 0, 'nq': 256, 'tiles': ctiles, 'obuf': ob_, 'ocol': 0, 'obk': obk, 'norm': (256, 0)})
                    for r in range(ROWS):
                        if r % 8 == 0:
                            ob_ = ob[oi % 2]
                            obk = "naob%d" % (oi % 2)
                            oi += 1
                        rs0 = min(max(r - 4, 0), ROWS - 8)
                        par = rs0 % 2
                        tiles = []
                        for i in range(4):
                            kr = rs0 + 2 * i
                            klo = CTX + kr * 64
                            if par == 0:
                                j = 2 + kr // 2
                                vap = v0[:, j * 128:(j + 1) * 128]
                            else:
                                j = (kr - 1) // 2
                                vap = v1[:, j * 128:(j + 1) * 128]
                            pi_ = kr - r + 7
                            tiles.append((klo, vap, nbb[:, pi_ * 64:(pi_ + 1) * 64]))
                        tiles += ctiles
                        jobs.append({'qlo': CTX + r * 64, 'nq': 64, 'tiles': tiles, 'obuf': ob_, 'ocol': (r % 8) * 64, 'obk': obk,
                                     'norm': (512, CTX + (r // 8) * 512) if r % 8 == 7 else None})
                    att_scores(jobs[0])
                    for ji in range(len(jobs)):
                        if ji + 1 < len(jobs):
                            att_scores(jobs[ji + 1])
                        att_finish(jobs[ji])
            S.barrier()

        def phase_cv(l):
            with contextlib.ExitStack() as ps:
                bn = bn_alloc(ps)
                Bt = ps.enter_context(sbuf_u("cvB", [128, T], BF16))
                Ct = ps.enter_context(sbuf_u("cvC", [128, T], BF16))
                Vt = ps.enter_context(sbuf_u("cvV", [128, T], BF16))
                cvv = ps.enter_context(sbuf_u("cvcv", [128, T], F32))
                acc = ps.enter_context(sbuf_u("cvacc", [128, T], F32))
                for g in range(N_CV):
                    for (tile_, off, k_) in ((Bt, OFF_CB, 'cvB'), (Ct, OFF_CC, 'cvC'), (Vt, OFF_CVV, 'cvV')):
                        dma('sp', tile_[:], U_F[off + g * 128:off + g * 128 + 128, :], writes=[k_])
                    wc = V_CVW + l * 12
                    op('dve', lambda e: e.tensor_tensor(out=cvv[:], in0=Ct[:], in1=Vt[:], op=ALU.mult), reads=['cvC', 'cvV'], writes=['cvcv'])
                    op('act', lambda e: e.activation(out=acc[:], in_=cvv[:], func=AF.Identity, scale=vt[:, wc + 4 + g:wc + 4 + g + 1]), reads=['cvcv', 'vt'], writes=['cvacc'])
                    for (lo, hi) in ((0, CTX), (CTX, T)):
                        op('dve', lambda e, lo=lo, hi=hi: e.scalar_tensor_tensor(out=acc[:, lo + 1:hi], in0=cvv[:, lo:hi - 1], scalar=vt[:, wc + g:wc + g + 1],
                                                                               in1=acc[:, lo + 1:hi], op0=ALU.mult, op1=ALU.add), reads=['cvcv', 'cvacc', 'vt'], writes=['cvacc'])
                        op('dve', lambda e, lo=lo, hi=hi: e.scalar_tensor_tensor(out=acc[:, lo:hi - 1], in0=cvv[:, lo + 1:hi], scalar=vt[:, wc + 8 + g:wc + 8 + g + 1],
                                                                               in1=acc[:, lo:hi - 1], op0=ALU.mult, op1=ALU.add), reads=['cvcv', 'cvacc', 'vt'], writes=['cvacc'])
                    op('dve', lambda e: e.tensor_tensor(out=acc[:], in0=acc[:], in1=Bt[:], op=ALU.mult), reads=['cvacc', 'cvB'], writes=['cvacc'])
                    for (t0, n) in [(0, 256)] + [(256 + i * 512, 512) for i in range(8)]:
                        block_norm_store(bn, acc[:, t0:t0 + n], n, V_CVON + l * 4 + g, 1536 + g * 128, t0, keys=['cvacc'])
            S.barrier()

        def make_epi_resid(Xsrc, Xdst, gi, lat_only_out=None):
            def epi(kind, *a):
                if kind == 'end':
                    return
                if kind == 'alloc':
                    ps = a[0]
                    c = {'i': 0}
                    c['xs'] = [ps.enter_context(sbuf_u("erx%d" % i, [128, 512], F32)) for i in range(3)]
                    c['tm'] = [ps.enter_context(sbuf_u("ert%d" % i, [128, 512], F32)) for i in range(3)]
                    return c
                c, (p_, pk), c0, tt, _ = a
                i = c['i'] % 3
                c['i'] += 1
                xs, tm = c['xs'][i], c['tm'][i]
                xk, tk = "erx%d" % i, "ert%d" % i
                which = 1 if tt < CTX else 0
                gb = (gi * 2 + which) * D + c0
                dma('sp', xs[:], Xsrc[tt:tt + 128, c0:c0 + 512], writes=[xk])
                op('dve', lambda e: e.tensor_tensor(out=tm[:], in0=p_[:], in1=grow[:, gb:gb + 512], op=ALU.mult), reads=[pk, 'grow'], writes=[tk])
                op('pool', lambda e: e.tensor_tensor(out=tm[:], in0=tm[:], in1=xs[:], op=ALU.add), reads=[tk, xk], writes=[tk])
                if lat_only_out is not None:
                    if tt >= CTX:
                        dma('pool', lat_only_out[tt - CTX:tt - CTX + 128, c0:c0 + 512], tm[:], reads=[tk], writes=['Xout'])
                else:
                    dma('pool', Xdst[tt:tt + 128, c0:c0 + 512], tm[:], reads=[tk], writes=['Xout'])
            return epi

        def phase_ffn_up(l, skip_ctx=False):
            with contextlib.ExitStack() as ps:
                blocks = ([] if skip_ctx else [(0, 256, 0, CTX)]) + [(CTX + o, sz, CTX, T) for (o, sz) in _split(SEQ, 1020)]
                maxn = max(b[1] for b in blocks) + 2
                ab = [ps.enter_context(sbuf_u("fua%d" % i, [128, 16 * maxn], BF16)) for i in range(2)]
                wg = [ps.enter_context(sbuf_u("fuwg%d" % i, [128, 16 * 512], BF16)) for i in range(2)]
                wvl = [ps.enter_context(sbuf_u("fuwv%d" % i, [128, 16 * 512], BF16)) for i in range(2)]
                accg = [ps.enter_context(sbuf_u("fuag%d" % i, [128, maxn], F32)) for i in range(2)]
                accv = [ps.enter_context(sbuf_u("fuav%d" % i, [128, maxn], F32)) for i in range(2)]
                stg = [ps.enter_context(sbuf_u("fust%d" % i, [128, maxn], BF16)) for i in range(2)]
                pp = [ps.enter_context(psum_u("fup%d" % i, [128, 512], F32)) for i in range(8)]
                Av = hF.rearrange("(kc p) t -> p kc t", p=128)
                Wv = wb_up[l].rearrange("(kc p) n -> p kc n", p=128)
                wkl = wkeys['up', l]
                wi = 0
                pidx = 0
                ei = 0
                fcw = V_FCW + l * 264
                fcb = V_FCB + l * 88
                for bi, (t0, n, lo, hi) in enumerate(blocks):
                    a_ = ab[bi % 2]
                    ak = "fua%d" % (bi % 2)
                    av = a_[:, 0:16 * (n + 2)].rearrange("p (kc t) -> p kc t", kc=16)
                    s0_ = max(t0 - 1, lo)
                    s1_ = min(t0 + n + 1, hi)
                    d0 = s0_ - (t0 - 1)
                    if d0 > 0 or s1_ < t0 + n + 1:
                        op('dve', lambda e: e.memset(a_[:, 0:16 * (n + 2)], 0.0), writes=[ak])
                    dma('sp', av[:, :, d0:d0 + (s1_ - s0_)], Av[:, :, s0_:s1_], writes=[ak])
                    for j in range(11):
                        wg_, wv_ = wg[wi % 2], wvl[wi % 2]
                        wgk, wvk = "fuwg%d" % (wi % 2), "fuwv%d" % (wi % 2)
                        wi += 1
                        wgv = wg_[:].rearrange("p (kc n) -> p kc n", kc=16)
                        wvv = wv_[:].rearrange("p (kc n) -> p kc n", kc=16)
                        dma('sp', wgv, Wv[:, :, j * 512:(j + 1) * 512], reads=wkl, writes=[wgk])
                        dma('sp', wvv, Wv[:, :, D_FF + j * 512:D_FF + (j + 1) * 512], reads=wkl, writes=[wvk])
                        for c4 in range(4):
                            gcx = j * 4 + c4
                            e2 = ei % 2
                            ei += 1
                            ag, avl, st_ = accg[e2], accv[e2], stg[e2]
                            agk, avk, stk = "fuag%d" % e2, "fuav%d" % e2, "fust%d" % e2
                            for (wview, wk_, acc_, acck, chn) in ((wgv, wgk, ag, agk, gcx), (wvv, wvk, avl, avk, 44 + gcx)):
                                for (o, sz) in _split(n, 510):
                                    p_ = pp[pidx % 8]
                                    pk = "fup%d" % (pidx % 8)
                                    pidx += 1
                                    for kc in range(16):
                                        op('pe', lambda e, p_=p_, kc=kc, o=o, sz=sz, wview=wview: e.matmul(
                                            p_[:, 0:sz + 2], lhsT=wview[:, kc, c4 * 128:(c4 + 1) * 128], rhs=av[:, kc, o:o + sz + 2],
                                            start=(kc == 0), stop=(kc == 15)), reads=[wk_, ak], writes=[pk], inc=(kc == 15))
                                    op('act', lambda e, p_=p_, o=o, sz=sz, acc_=acc_, chn=chn: e.activation(
                                        out=acc_[:, o:o + sz], in_=p_[:, 1:sz + 1], func=AF.Identity,
                                        scale=vt[:, fcw + 88 + chn:fcw + 88 + chn + 1], bias=vt[:, fcb + chn:fcb + chn + 1]),
                                       reads=[pk, 'vt'], writes=[acck])
                                    op('dve', lambda e, p_=p_, o=o, sz=sz, acc_=acc_, chn=chn: e.scalar_tensor_tensor(
                                        out=acc_[:, o:o + sz], in0=p_[:, 0:sz], scalar=vt[:, fcw + chn:fcw + chn + 1], in1=acc_[:, o:o + sz],
                                        op0=ALU.mult, op1=ALU.add), reads=[pk, 'vt', acck], writes=[acck])
                                    op('dve', lambda e, p_=p_, o=o, sz=sz, acc_=acc_, chn=chn: e.scalar_tensor_tensor(
                                        out=acc_[:, o:o + sz], in0=p_[:, 2:sz + 2], scalar=vt[:, fcw + 176 + chn:fcw + 176 + chn + 1], in1=acc_[:, o:o + sz],
                                        op0=ALU.mult, op1=ALU.add), reads=[pk, 'vt', acck], writes=[acck])
                            op('act', lambda e, ag=ag: e.activation(out=ag[:, 0:n], in_=ag[:, 0:n], func=AF.Silu), reads=[agk], writes=[agk])
                            op('pool', lambda e, ag=ag, avl=avl, st_=st_: e.tensor_tensor(out=st_[:, 0:n], in0=ag[:, 0:n], in1=avl[:, 0:n], op=ALU.mult),
                               reads=[agk, avk], writes=[stk])
                            dma('pool', actF[gcx * 128:(gcx + 1) * 128, t0:t0 + n], st_[:, 0:n], reads=[stk], writes=['actF'])
            S.barrier()

        tblocks_all = [(0, 256)] + [(256 + i * 1024, 1024) for i in range(4)]
        norm_groups = [(0, 2, 1)] + [(256 + i * 512, 4, 0) for i in range(8)]

        X0 = xin
        tb128 = [(0, 256)] + [(256 + i * 1024, 1024) for i in range(4)]
        tb_down = [(0, 256)] + [(256 + i * 512, 512) for i in range(8)]
        for l in range(DEPTH):
            last = (l == DEPTH - 1)
            phase_ada(l)
            if stage <= 0:
                break
            phase_norm(X0, 0, norm_groups)
            if stage <= 1:
                break
            lb_setup(l)
            gemm(hF, D, wb_in[l], wkeys['in', l], 0, 1536, tblocks_all, 'T', epi_inproj, name="ip")
            gemm(hF, D, wb_in[l], wkeys['in', l], OFF_NV, 1024, tblocks_all, 'T', epi_inproj, name="ip")
            gemm(hF, D, wb_in[l], wkeys['in', l], OFF_NK, 1024, tblocks_all, 'F', epi_inproj_qk, name="ip", npp=7)
            gemm(hF, D, wb_in[l], wkeys['in', l], OFF_HQ, 1024, tblocks_all, 'F', epi_inproj, name="ip")
            gemm(hF, D, wb_in[l], wkeys['in', l], OFF_NQ, 1024, tblocks_all, 'F', epi_inproj_qk, name="ip", npp=7)
            gemm(hF, D, wb_in[l], wkeys['in', l], OFF_CB, 1536, tblocks_all, 'F', epi_inproj, name="ip")
            if stage <= 2:
                break
            if 'hg' in phases:
                phase_hgrn(l)
            if 'na' in phases:
                phase_na(l, skip_ctx_out=last, cast_tag='mix%d' % l)
            if 'cv' in phases:
                phase_cv(l)
            if stage <= 3:
                break
            gemm(mixF, D, wb_out[l], wkeys['out', l], 0, D, tb128[1:] if last else tb128, 'T', make_epi_resid(X0, X1, 0), name="op")
            if stage <= 4:
                break
            phase_norm(X1, 1, norm_groups[1:] if last else norm_groups)
            phase_ffn_up(l, skip_ctx=last)
            if stage <= 5:
                break
            if last:
                gemm(actF, D_FF, wb_down[l], wkeys['down', l], 0, D, tb_down[1:], 'T', make_epi_resid(X1, None, 1, lat_only_out=y), name="dn", abufs=1)
            else:
                gemm(actF, D_FF, wb_down[l], wkeys['down', l], 0, D, tb_down, 'T', make_epi_resid(X1, X2, 1), name="dn", abufs=1)
            X0 = X2
            if stage <= 6 + l:
                break

        if 'dbg_ada' in dbg:
            dada = nc.dram_tensor("dbg_ada", [128, 192 + 128 + 4 * D], F32, kind="ExternalOutput").ap()
            dma('sp', dada[:, 0:192], adaT[:], reads=['adaT'])
            dma('sp', dada[:, 192:320], modv[:], reads=['modv'])
            dma('sp', dada[:, 320:], grow[:], reads=['grow'])
        S.barrier()
        S.final()
    return nc


def _flay(v):
    v = np.asarray(v, np.float32)
    return np.ascontiguousarray(v.reshape(-1, 128).T)


def _consts():
    c = np.zeros((128, NCONST), np.float32)
    c[:, C_ID:C_ID + 128] = np.eye(128, dtype=np.float32)
    c[:, C_ONES:C_ONES + 128] = 1.0
    tp = np.arange(64)[:, None]
    t = np.arange(64)[None, :]
    Mf = (tp <= t).astype(np.float32)
    Mb = (tp >= t).astype(np.float32)
    for (M, mid, oe, oc, om, occ, omr) in ((Mf, 31, C_MEXT_F, C_MC_F, C_MK_F, C_MCC_F, C_MKR_F), (Mb, 32, C_MEXT_B, C_MC_B, C_MK_B, C_MCC_B, C_MKR_B)):
        Mc = M - M[:, mid:mid + 1]
        c[:64, oe:oe + 64] = Mc
        c[:64, oe + 64] = 1.0
        c[:64, oe + 65] = M[:, mid]
        c[:64, oc:oc + 64] = Mc
        c[:64, om:om + 64] = M
        c[:64, occ:occ + 64] = Mc
        c[:64, occ + 64:occ + 128] = M - 1.0
        for rr in range(4):
            c[:64, omr + rr * 64:omr + (rr + 1) * 64] = M
    return c


def _natab(rpb):
    L, H = rpb.shape[0], rpb.shape[1]
    kc = np.arange(64)[:, None]
    c = np.arange(64)[None, :]
    ws = np.clip(c - 8, 0, 48)
    ok = (kc >= ws) & (kc < ws + 16)
    ic = np.clip(kc - c + 15, 0, 30)
    out = np.full((L, H, 2, 64, 14, 64), -1e30, np.float32)
    for pi in range(14):
        for rr in range(2):
            dr = pi - 7 + rr
            g = rpb[:, :, dr + 7, :][:, :, ic]
            out[:, :, rr, :, pi, :] = np.where(ok[None, None], g, np.float32(-1e30))
    return np.ascontiguousarray(out.reshape(L, H, 128, 14 * 64))


def _vecs(inp, b):
    v = np.zeros((128, NVEC), np.float32)
    for l in range(DEPTH):
        v[:, V_BADA + l * 96:V_BADA + (l + 1) * 96] = _flay(inp['b_ada'][l])
        v[:, V_LN1 + l * 16:V_LN1 + (l + 1) * 16] = _flay(inp['ln1_w'][l])
        v[:, V_LN2 + l * 16:V_LN2 + (l + 1) * 16] = _flay(inp['ln2_w'][l])
        v[:, V_HGN + l] = inp['hg_norm_w'][l]
        v[:, V_QN + l] = inp['na_q_norm_w'][l]
        v[:, V_KN + l] = inp['na_k_norm_w'][l]
        v[:, V_NAON + l * 8:V_NAON + (l + 1) * 8] = _flay(inp['na_out_norm_w'][l])
        for t in range(3):
            v[:, V_CVW + l * 12 + t * 4:V_CVW + l * 12 + (t + 1) * 4] = _flay(inp['cv_w'][l, t])
            v[:, V_FCW + l * 264 + t * 88:V_FCW + l * 264 + (t + 1) * 88] = _flay(inp['ffn_conv_w'][l, t])
        v[:, V_CVON + l * 4:V_CVON + (l + 1) * 4] = _flay(inp['cv_out_norm_w'][l])
        v[:, V_FCB + l * 88:V_FCB + (l + 1) * 88] = _flay(inp['ffn_conv_b'][l])
    cc = np.stack([_flay(inp['c'][b]), _flay(inp['c_ctx'])], axis=-1)
    v[:, V_CT:V_CT + 32] = cc.reshape(128, 32)
    return v


def make_in_maps(inp, cores):
    inp = {k: np.asarray(v) for k, v in inp.items()}
    consts = _consts()
    natab = _natab(inp['na_rpb'].astype(np.float32))
    lbl = np.ascontiguousarray(np.broadcast_to(inp['hg_lb_logits'].astype(np.float32).reshape(1, -1), (64, 2 * DEPTH * 512)))
    maps = []
    for i in cores:
        b = i % 4
        m = {
            'xin': np.ascontiguousarray(np.concatenate([inp['ctx'][b], inp['x'][b]], axis=0).astype(np.float32)),
            'vecs': _vecs(inp, b), 'consts': consts, 'lbl': lbl, 'natab': natab,
            'w_ada': inp['w_ada'], 'w_in': inp['w_in'], 'w_out': inp['w_out'], 'w_up': inp['w_up'], 'w_down': inp['w_down'],
        }
        maps.append(m)
    return maps


def kernel(**inputs):
    nc = build()
    cores = list(range(8))
    maps = make_in_maps(inputs, cores)
    res = run_bass_kernel_spmd(nc, maps, core_ids=cores)
    return np.stack([np.asarray(res.results[b]['y'], np.float32) for b in range(4)], axis=0)
```

```python
import contextlib
import numpy as np
import concourse.bass as bass
import concourse.mybir as mybir
from concourse.bass_utils import run_bass_kernel_spmd

F32 = mybir.dt.float32
BF16 = mybir.dt.bfloat16
AF = mybir.ActivationFunctionType
ALU = mybir.AluOpType

D = 2048
DEPTH = 2
CTX = 256
SEQ = 4096
T = CTX + SEQ
GW = 64
ROWS = SEQ // GW
HD = 128
N_HG, N_NA, N_CV = 4, 8, 4
IN_W = 7168
D_FF = 5632
EPS = 1e-6
CH = 64
NT = T // 128

OFF_FFW, OFF_FBW, OFF_HI, OFF_NK, OFF_NV, OFF_HQ, OFF_HGG, OFF_NQ, OFF_CB, OFF_CC, OFF_CVV = (
    0, 512, 1024, 1536, 2560, 3584, 4096, 4608, 5632, 6144, 6656)

V_BADA = 0
V_LN1 = V_BADA + DEPTH * 96
V_LN2 = V_LN1 + DEPTH * 16
V_HGN = V_LN2 + DEPTH * 16
V_QN = V_HGN + DEPTH
V_KN = V_QN + DEPTH
V_NAON = V_KN + DEPTH
V_CVW = V_NAON + DEPTH * 8
V_CVON = V_CVW + DEPTH * 12
V_FCW = V_CVON + DEPTH * 4
V_FCB = V_FCW + DEPTH * 264
V_CT = V_FCB + DEPTH * 88
NVEC = V_CT + 32

C_ID = 0
C_ONES = 128
C_MEXT_F = 256
C_MEXT_B = C_MEXT_F + 66
C_MC_F = C_MEXT_B + 66
C_MC_B = C_MC_F + 64
C_MK_F = C_MC_B + 64
C_MK_B = C_MK_F + 64
C_MCC_F = C_MK_B + 64
C_MCC_B = C_MCC_F + 128
C_MKR_F = C_MCC_B + 128
C_MKR_B = C_MKR_F + 256
NCONST = C_MKR_B + 256


class Sched:
    def __init__(self, nc, es):
        self.nc = nc
        self.eng = {'pe': nc.tensor, 'act': nc.scalar, 'dve': nc.vector, 'pool': nc.gpsimd, 'sp': nc.sync}
        self.sem = {k: es.enter_context(nc.semaphore("sem_" + k)) for k in ('pe', 'act', 'dve', 'pool')}
        self.count = {k: 0 for k in self.sem}
        self.NDS = 40
        self.dsem = [es.enter_context(nc.semaphore("dsem%d" % i)) for i in range(self.NDS)]
        self.dval = [0] * self.NDS
        self.dk = 0
        self.seen = {k: {} for k in self.eng}
        self.res = {}
        self.ninstr = 0
        self.csem = {}
        self.ccnt = {}
        self.es = es

    def _wait(self, en, tok):
        kind, a, v = tok
        if kind == 'e':
            if a == en and en == 'pe':
                return
            key = ('e', a)
            sem = self.sem[a]
        elif kind == 'c':
            key = ('c', a)
            sem = self.csem[a]
            v = 16 * self.ccnt[a]
        else:
            key = ('d', a)
            sem = self.dsem[a]
        if self.seen[en].get(key, 0) >= v:
            return
        self.eng[en].wait_ge(sem, v)
        self.seen[en][key] = v

    def _deps(self, en, reads, writes):
        deps = []
        for k in reads:
            r = self.res.get(k)
            if r and r['w']:
                deps.append(r['w'])
        for k in writes:
            r = self.res.get(k)
            if r:
                deps.extend(r['r'])
                if r['w']:
                    deps.append(r['w'])
        for tok in deps:
            self._wait(en, tok)

    def _update(self, tok, reads, writes):
        for k in reads:
            r = self.res.setdefault(k, {'w': None, 'r': []})
            if tok[0] == 'e':
                r['r'] = [t for t in r['r'] if not (t[0] == 'e' and t[1] == tok[1])]
            r['r'].append(tok)
        for k in writes:
            self.res[k] = {'w': tok, 'r': []}

    def op(self, en, fn, reads=(), writes=(), inc=True):
        self._deps(en, reads, writes)
        ins = fn(self.eng[en])
        self.ninstr += 1
        if inc:
            self.count[en] += 1
            ins.then_inc(self.sem[en], 1)
            tok = ('e', en, self.count[en])
        else:
            tok = ('e', en, self.count[en] + 1)
        self._update(tok, reads, writes)
        return tok

    def dma(self, en, out, in_, reads=(), writes=(), **kw):
        slot = self.dk % self.NDS
        self.dk += 1
        if self.dval[slot] > 0:
            self._wait(en, ('d', slot, self.dval[slot]))
        self._deps(en, reads, writes)
        self.dval[slot] += 16
        self.eng[en].dma_start(out=out, in_=in_, **kw).then_inc(self.dsem[slot], 16)
        self.ninstr += 1
        tok = ('d', slot, self.dval[slot])
        self._update(tok, reads, writes)
        return tok

    def cast_dma(self, en, group, out, in_):
        if group not in self.csem:
            self.csem[group] = self.es.enter_context(self.nc.semaphore("csem_" + group))
            self.ccnt[group] = 0
        self.ccnt[group] += 1
        self.eng[en].dma_start(out=out, in_=in_).then_inc(self.csem[group], 16)
        self.ninstr += 1

    def barrier(self):
        for en in self.eng:
            for a in self.sem:
                if a != en and self.count[a] > 0:
                    self._wait(en, ('e', a, self.count[a]))
            for s in range(self.NDS):
                if self.dval[s] > 0:
                    self._wait(en, ('d', s, self.dval[s]))
        self.res = {k: v for k, v in self.res.items() if k.startswith('WB_')}

    def final(self):
        for g in self.csem:
            self._wait('sp', ('c', g, 0))
        for s in range(self.NDS):
            if self.dval[s] > 0:
                self._wait('sp', ('d', s, self.dval[s]))
        for a in self.sem:
            if self.count[a] > 0:
                self._wait('sp', ('e', a, self.count[a]))


def _split(n, m):
    k = (n + m - 1) // m
    base = n // k
    rem = n - base * k
    out = []
    o = 0
    for i in range(k):
        s = base + (1 if i < rem else 0)
        out.append((o, s))
        o += s
    return out


def build(stage=99, dbg=None, phases=('hg', 'na', 'cv')):
    nc = bass.Bass("TRN2", target_bir_lowering=False)
    dbg = dbg or []
    _uid = [0]
    _sb, _pt = nc.sbuf_tensor, nc.psum_tensor

    def sbuf_u(name, shape, dt):
        _uid[0] += 1
        return _sb("%s_u%d" % (name, _uid[0]), shape, dt)

    def psum_u(name, shape, dt):
        _uid[0] += 1
        return _pt("%s_u%d" % (name, _uid[0]), shape, dt)
    outs = {}

    def dram_in(name, shape, dt=F32):
        return nc.dram_tensor(name, list(shape), dt, kind="ExternalInput").ap()

    def dram_tmp(name, shape, dt):
        kind = "ExternalOutput" if name in dbg else "Internal"
        return nc.dram_tensor(name, list(shape), dt, kind=kind).ap()

    xin = dram_in("xin", [T, D])
    vecs = dram_in("vecs", [128, NVEC])
    consts = dram_in("consts", [128, NCONST])
    lbl = dram_in("lbl", [64, 2 * DEPTH * 512])
    natab = dram_in("natab", [DEPTH, N_NA, 128, 14 * 64])
    w_ada = dram_in("w_ada", [DEPTH, D, 6 * D])
    w_in = dram_in("w_in", [DEPTH, D, IN_W])
    w_out = dram_in("w_out", [DEPTH, D, D])
    w_up = dram_in("w_up", [DEPTH, D, 2 * D_FF])
    w_down = dram_in("w_down", [DEPTH, D_FF, D])
    y = nc.dram_tensor("y", [SEQ, D], F32, kind="ExternalOutput").ap()

    wb_in = [dram_tmp("wb_in%d" % l, [D, IN_W], BF16) for l in range(DEPTH)]
    wb_out = [dram_tmp("wb_out%d" % l, [D, D], BF16) for l in range(DEPTH)]
    wb_up = [dram_tmp("wb_up%d" % l, [D, 2 * D_FF], BF16) for l in range(DEPTH)]
    wb_down = [dram_tmp("wb_down%d" % l, [D_FF, D], BF16) for l in range(DEPTH)]
    hF = dram_tmp("hF", [D, T], BF16)
    U_f = dram_tmp("U_f", [T, 1024], F32)
    U_i = dram_tmp("U_i", [T, 512], BF16)
    U_nv = dram_tmp("U_nv", [T + 64, 1024], BF16)
    U_F = dram_tmp("U_F", [IN_W, T], BF16)
    U_k = dram_tmp("U_k", [T, 1024], F32)
    U_qs = dram_tmp("U_qs", [512, T], F32)
    mixF = dram_tmp("mixF", [D, T], BF16)
    actF = dram_tmp("actF", [D_FF, T], BF16)
    X1 = dram_tmp("X1", [T, D], F32)
    X2 = dram_tmp("X2", [T, D], F32)

    es = contextlib.ExitStack()
    with es:
        S = Sched(nc, es)
        op, dma = S.op, S.dma

        vt = es.enter_context(sbuf_u("vt", [128, NVEC], F32))
        ct = es.enter_context(sbuf_u("ct", [128, NCONST], F32))
        ctb = es.enter_context(sbuf_u("ctb", [128, NCONST], BF16))
        adaT = es.enter_context(sbuf_u("adaT", [128, 96 * 2], F32))
        modv = es.enter_context(sbuf_u("modv", [128, 4 * 16 * 2], F32))
        grow = es.enter_context(sbuf_u("grow", [128, 4 * D], F32))
        silc = es.enter_context(sbuf_u("silc", [128, 32], F32))
        epst = es.enter_context(sbuf_u("epst", [128, 1], F32))
        onet = es.enter_context(sbuf_u("onet", [128, 1], F32))
        qkw = es.enter_context(sbuf_u("qkw", [128, 2], F32))
        lbt = es.enter_context(sbuf_u("lbt", [128, 2 * 512], F32))
        omt = es.enter_context(sbuf_u("omt", [128, 2 * 512], F32))
        dma('sp', vt[:], vecs[:, :], writes=['vt'])
        dma('sp', ct[:], consts[:, :], writes=['ct'])
        op('dve', lambda e: e.tensor_copy(out=ctb[:], in_=ct[:]), reads=['ct'], writes=['ctb'])
        op('dve', lambda e: e.memset(epst[:], EPS), writes=['epst'])
        op('dve', lambda e: e.memset(onet[:], 1.0), writes=['onet'])
        op('act', lambda e: e.activation(out=silc[:], in_=vt[:, V_CT:V_CT + 32], func=AF.Silu), reads=['vt'], writes=['silc'])
        ident = ct[:, C_ID:C_ID + 128]
        identb = ctb[:, C_ID:C_ID + 128]
        onesb = ctb[:, C_ONES:C_ONES + 128]
        ones = ct[:, C_ONES:C_ONES + 128]

        def cast_weight(src, dst, rows, key):
            step = 512
            for r0 in range(0, rows, step):
                r1 = min(rows, r0 + step)
                S.cast_dma('pool', key, dst[r0:r1, :], src[r0:r1, :])
            S.res['WB_' + key] = {'w': ('c', key, 0), 'r': []}
            return ['WB_' + key]

        wkeys = {}
        for l in range(DEPTH):
            for nm in ('in', 'out', 'up', 'down'):
                wkeys[nm, l] = ['WB_wb%s%d_' % (nm if nm != 'down' else 'dn', l)]
        cast_plan = {
            'start': [('in', 0), ('out', 0)],
            'mix0': [('up', 0), ('down', 0), ('in', 1), ('out', 1)],
            'mix1': [('up', 1), ('down', 1)],
        }

        def issue_casts(tag, idx=None):
            plan = cast_plan[tag] if idx is None else cast_plan[tag][idx:idx + 1]
            for (nm, l) in plan:
                src = {'in': w_in, 'out': w_out, 'up': w_up, 'down': w_down}[nm][l]
                dst = {'in': wb_in, 'out': wb_out, 'up': wb_up, 'down': wb_down}[nm][l]
                rows = D_FF if nm == 'down' else D
                cast_weight(src, dst, rows, "wb%s%d_" % (nm if nm != 'down' else 'dn', l))

        issue_casts('start')

        def phase_ada(l):
            with contextlib.ExitStack() as ps:
                wsl = [ps.enter_context(sbuf_u("adaw%d" % i, [128, 16 * 512], F32)) for i in range(2)]
                pacc = [ps.enter_context(psum_u("adap%d" % i, [128, 512], F32)) for i in range(2)]
                wv = w_ada[l].rearrange("(kc p) n -> p kc n", p=128)
                for sl in range(24):
                    wt = wsl[sl % 2]
                    wk = "adaw%d" % (sl % 2)
                    dma('sp', wt[:].rearrange("p (kc n) -> p kc n", kc=16), wv[:, :, sl * 512:(sl + 1) * 512], writes=[wk])
                    pa = pacc[sl % 2]
                    pk = "adap%d" % (sl % 2)
                    for c4 in range(4):
                        for kc in range(16):
                            op('pe', lambda e, c4=c4, kc=kc: e.matmul(
                                pa[:, c4 * 2:c4 * 2 + 2], lhsT=wt[:, kc * 512 + c4 * 128: kc * 512 + c4 * 128 + 128],
                                rhs=silc[:, kc * 2:kc * 2 + 2], start=(kc == 0), stop=(kc == 15)),
                               reads=[wk, 'silc'], writes=[pk], inc=(kc == 15))
                    for c4 in range(4):
                        j = sl * 4 + c4
                        op('dve', lambda e, c4=c4, j=j: e.tensor_scalar(
                            out=adaT[:, j * 2:j * 2 + 2], in0=pa[:, c4 * 2:c4 * 2 + 2],
                            scalar1=vt[:, V_BADA + l * 96 + j: V_BADA + l * 96 + j + 1], scalar2=None, op0=ALU.add),
                           reads=[pk, 'vt'], writes=['adaT'])
                for n, (lnoff, shc, scc) in enumerate([(V_LN1, 0, 16), (V_LN2, 48, 64)]):
                    for fc in range(16):
                        a_out = modv[:, (n * 2) * 32 + fc * 2:(n * 2) * 32 + fc * 2 + 2]
                        b_out = modv[:, (n * 2 + 1) * 32 + fc * 2:(n * 2 + 1) * 32 + fc * 2 + 2]
                        op('dve', lambda e, a_out=a_out, fc=fc, scc=scc, lnoff=lnoff: e.tensor_scalar(
                            out=a_out, in0=adaT[:, (scc + fc) * 2:(scc + fc) * 2 + 2], scalar1=1.0,
                            scalar2=vt[:, lnoff + l * 16 + fc:lnoff + l * 16 + fc + 1], op0=ALU.add, op1=ALU.mult),
                           reads=['adaT', 'vt'], writes=['modv'])
                        op('dve', lambda e, b_out=b_out, fc=fc, shc=shc: e.tensor_copy(
                            out=b_out, in_=adaT[:, (shc + fc) * 2:(shc + fc) * 2 + 2]),
                           reads=['adaT'], writes=['modv'])
                dg = ps.enter_context(sbuf_u("dg", [128, 2 * 128], F32))
                pg = [ps.enter_context(psum_u("pg%d" % i, [128, 512], F32)) for i in range(2)]
                cnt = 0
                for gi, gch in enumerate([32, 80]):
                    for which in range(2):
                        for f4 in range(4):
                            pgt = pg[cnt % 2]
                            pgk = "pg%d" % (cnt % 2)
                            for q in range(4):
                                fc = f4 * 4 + q
                                dsl = dg[:, (q % 2) * 128:(q % 2) * 128 + 128]
                                dk_ = "dg%d" % (q % 2)
                                col = (gch + fc) * 2 + which
                                op('dve', lambda e, dsl=dsl, col=col: e.tensor_scalar(
                                    out=dsl, in0=ident, scalar1=adaT[:, col:col + 1], scalar2=None, op0=ALU.mult),
                                   reads=['adaT', 'ct'], writes=[dk_])
                                op('pe', lambda e, dsl=dsl, q=q, pgt=pgt: e.matmul(
                                    pgt[:, q * 128:(q + 1) * 128], lhsT=ones, rhs=dsl, start=True, stop=True),
                                   reads=[dk_, 'ct'], writes=[pgk])
                            base = (gi * 2 + which) * D + f4 * 512
                            op('act', lambda e, base=base, pgt=pgt: e.activation(
                                out=grow[:, base:base + 512], in_=pgt[:], func=AF.Copy),
                               reads=[pgk], writes=['grow'])
                            cnt += 1
            S.barrier()

        def phase_norm(src, n, groups):
            with contextlib.ExitStack() as ps:
                xt = [ps.enter_context(sbuf_u("nx%d" % i, [128, D], F32)) for i in range(2)]
                sq = ps.enter_context(sbuf_u("nsq", [128, D], BF16))
                st = [ps.enter_context(sbuf_u("nst%d" % i, [128, 4], F32)) for i in range(2)]
                hs = [ps.enter_context(sbuf_u("nh%d" % i, [128, 16 * 512], BF16)) for i in range(2)]
                pt = [ps.enter_context(psum_u("npt%d" % i, [128, 512], F32)) for i in range(4)]
                hFv = hF.rearrange("(fc p) t -> p fc t", p=128)
                ti = 0
                pi = 0
                for gi, (t0, ntl, which) in enumerate(groups):
                    hst = hs[gi % 2]
                    hk = "nh%d" % (gi % 2)
                    hview = hst[:].rearrange("p (fc t) -> p fc t", fc=16)
                    for s in range(ntl):
                        x_ = xt[ti % 2]
                        xk = "nx%d" % (ti % 2)
                        s_ = st[ti % 2]
                        sk = "nst%d" % (ti % 2)
                        ti += 1
                        tt = t0 + s * 128
                        dma('sp', x_[:], src[tt:tt + 128, :], writes=[xk])
                        op('act', lambda e, x_=x_, s_=s_: e.activation(out=sq[:], in_=x_[:], func=AF.Square, accum_out=s_[:, 0:1]),
                           reads=[xk], writes=['nsq', sk])
                        op('act', lambda e, s_=s_: e.activation(out=s_[:, 1:2], in_=s_[:, 0:1], func=AF.Ln, bias=epst[:, 0:1], scale=1.0 / D),
                           reads=[sk, 'epst'], writes=[sk])
                        op('act', lambda e, s_=s_: e.activation(out=s_[:, 2:3], in_=s_[:, 1:2], func=AF.Exp, scale=-0.5), reads=[sk], writes=[sk])
                        op('dve', lambda e, x_=x_, s_=s_: e.tensor_scalar(out=x_[:], in0=x_[:], scalar1=s_[:, 2:3], scalar2=None, op0=ALU.mult),
                           reads=[xk, sk], writes=[xk])
                        for f4 in range(4):
                            p_ = pt[pi % 4]
                            pk = "npt%d" % (pi % 4)
                            pi += 1
                            for q in range(4):
                                fc = f4 * 4 + q
                                op('pe', lambda e, p_=p_, q=q, fc=fc, x_=x_: e.transpose(
                                    out=p_[:, q * 128:(q + 1) * 128], in_=x_[:, fc * 128:(fc + 1) * 128], identity=ident),
                                   reads=[xk, 'ct'], writes=[pk], inc=(q == 3))
                            for q in range(4):
                                fc = f4 * 4 + q
                                acol = (n * 2) * 32 + fc * 2 + which
                                bcol = (n * 2 + 1) * 32 + fc * 2 + which
                                eng = 'act' if q % 2 == 0 else 'dve'
                                if eng == 'act':
                                    op('act', lambda e, p_=p_, q=q, fc=fc, s=s, acol=acol, bcol=bcol: e.activation(
                                        out=hview[:, fc, s * 128:(s + 1) * 128], in_=p_[:, q * 128:(q + 1) * 128], func=AF.Identity,
                                        scale=modv[:, acol:acol + 1], bias=modv[:, bcol:bcol + 1]),
                                       reads=[pk, 'modv'], writes=[hk])
                                else:
                                    op('dve', lambda e, p_=p_, q=q, fc=fc, s=s, acol=acol, bcol=bcol: e.tensor_scalar(
                                        out=hview[:, fc, s * 128:(s + 1) * 128], in0=p_[:, q * 128:(q + 1) * 128],
                                        scalar1=modv[:, acol:acol + 1], scalar2=modv[:, bcol:bcol + 1], op0=ALU.mult, op1=ALU.add),
                                       reads=[pk, 'modv'], writes=[hk])
                    dma('pool', hFv[:, :, t0:t0 + ntl * 128], hview[:, :, 0:ntl * 128], reads=[hk], writes=['hF'])
            S.barrier()

        def gemm(A, K, Wb, wkeylist, col0, ncols, tblocks, mode, epi, halo=False, name="g", abufs=2, npp=8):
            KC = K // 128
            with contextlib.ExitStack() as ps:
                maxn = max(n for _, n in tblocks)
                ab = [ps.enter_context(sbuf_u("%sa%d" % (name, i), [128, KC * maxn], BF16)) for i in range(abufs)]
                wsb = [ps.enter_context(sbuf_u("%sw%d" % (name, i), [128, KC * 512], BF16)) for i in range(2)]
                pp = [ps.enter_context(psum_u("%sp%d" % (name, i), [128, 512], F32)) for i in range(npp)]
                ctx = epi('alloc', ps)
                Av = A.rearrange("(kc p) t -> p kc t", p=128)
                Wv = Wb.rearrange("(kc p) n -> p kc n", p=128)
                wi = 0
                pidx = 0
                def load_A(bi):
                    t0, n = tblocks[bi]
                    a_ = ab[bi % abufs]
                    ak = "%sa%d" % (name, bi % abufs)
                    av = a_[:, 0:KC * n].rearrange("p (kc t) -> p kc t", kc=KC)
                    for k0 in range(0, KC, 8):
                        k1 = min(KC, k0 + 8)
                        dma('sp', av[:, k0:k1, :], Av[:, k0:k1, t0:t0 + n], reads=['A_' + name], writes=[ak + "_%d" % k0])

                load_A(0)
                for bi, (t0, n) in enumerate(tblocks):
                    a_ = ab[bi % abufs]
                    ak = "%sa%d" % (name, bi % abufs)
                    if abufs == 1 and bi > 0:
                        load_A(bi)
                    av = a_[:, 0:KC * n].rearrange("p (kc t) -> p kc t", kc=KC)
                    akeys = [ak + "_%d" % k0 for k0 in range(0, KC, 8)]
                    for c0 in range(col0, col0 + ncols, 512):
                        w_ = wsb[wi % 2]
                        wk = "%sw%d" % (name, wi % 2)
                        wi += 1
                        wv = w_[:].rearrange("p (kc n) -> p kc n", kc=KC)
                        for k0 in range(0, KC, 8):
                            k1 = min(KC, k0 + 8)
                            dma('sp', wv[:, k0:k1, :], Wv[:, k0:k1, c0:c0 + 512], reads=wkeylist, writes=[wk + "_%d" % k0])
                        wks = [wk + "_%d" % k0 for k0 in range(0, KC, 8)]
                        if abufs > 1 and c0 == col0 and bi + 1 < len(tblocks):
                            load_A(bi + 1)
                        if mode == 'F':
                            pieces = _split(n, 512)
                            for c4 in range(4):
                                banks = []
                                for (o, sz) in pieces:
                                    p_ = pp[pidx % npp]
                                    pk = "%sp%d" % (name, pidx % npp)
                                    pidx += 1
                                    for kc in range(KC):
                                        op('pe', lambda e, p_=p_, sz=sz, o=o, kc=kc, c4=c4, wv=wv, av=av: e.matmul(
                                            p_[:, 0:sz], lhsT=wv[:, kc, c4 * 128:(c4 + 1) * 128], rhs=av[:, kc, o:o + sz],
                                            start=(kc == 0), stop=(kc == KC - 1)),
                                           reads=[wks[kc // 8], akeys[kc // 8]], writes=[pk], inc=(kc == KC - 1))
                                    banks.append((p_, pk, o, sz))
                                epi('F', ctx, banks, c0 + c4 * 128, t0, n)
                        else:
                            for s in range(n // 128):
                                p_ = pp[pidx % npp]
                                pk = "%sp%d" % (name, pidx % npp)
                                pidx += 1
                                for kc in range(KC):
                                    op('pe', lambda e, p_=p_, kc=kc, s=s, wv=wv, av=av: e.matmul(
                                        p_[:, :], lhsT=av[:, kc, s * 128:(s + 1) * 128], rhs=wv[:, kc, :],
                                        start=(kc == 0), stop=(kc == KC - 1)),
                                       reads=[wks[kc // 8], akeys[kc // 8]], writes=[pk], inc=(kc == KC - 1))
                                epi('T', ctx, (p_, pk), c0, t0 + s * 128, 128)
                epi('end', ctx)
            S.barrier()

        def lb_setup(l):
            op('dve', lambda e: e.tensor_scalar(out=qkw[:, 0:1], in0=vt[:, V_QN + l:V_QN + l + 1], scalar1=float(HD ** -0.5), scalar2=None, op0=ALU.mult),
               reads=['vt'], writes=['qkw'])
            op('dve', lambda e: e.tensor_copy(out=qkw[:, 1:2], in_=vt[:, V_KN + l:V_KN + l + 1]), reads=['vt'], writes=['qkw'])
            _lb_setup(l)

        def _lb_setup(l):
            if l == 0:
                op('dve', lambda e: e.memset(lbt[:], 0.0), writes=['lbt'])
                op('dve', lambda e: e.memset(omt[:], 1.0), writes=['omt'])
            else:
                with contextlib.ExitStack() as ps:
                    lg = ps.enter_context(sbuf_u("lg", [128, 2 * DEPTH * 512], F32))
                    dma('sp', lg[0:64, :], lbl[:, :], writes=['lg'])
                    dma('sp', lg[64:128, :], lbl[:, :], writes=['lg2'])
                    for dr in range(2):
                        op('dve', lambda e, dr=dr: e.tensor_tensor(out=lbt[:, dr * 512:(dr + 1) * 512], in0=lg[:, (dr * 2) * 512:(dr * 2 + 1) * 512],
                                                                 in1=lg[:, (dr * 2 + 1) * 512:(dr * 2 + 2) * 512], op=ALU.subtract), reads=['lg', 'lg2'], writes=['lbt'])
                    op('act', lambda e: e.activation(out=lbt[:], in_=lbt[:], func=AF.Exp), reads=['lbt'], writes=['lbt'])
                    op('dve', lambda e: e.tensor_scalar(out=lbt[:], in0=lbt[:], scalar1=1.0, scalar2=None, op0=ALU.add), reads=['lbt'], writes=['lbt'])
                    op('dve', lambda e: e.reciprocal(out=lbt[:], in_=lbt[:]), reads=['lbt'], writes=['lbt'])
                    op('dve', lambda e: e.tensor_scalar(out=omt[:], in0=lbt[:], scalar1=-1.0, scalar2=1.0, op0=ALU.mult, op1=ALU.add),
                       reads=['lbt'], writes=['omt'])
                    S.barrier()

        def epi_inproj_qk(kind, *a):
            if kind == 'alloc':
                ps = a[0]
                c = {'i': 0, 'pend': None, 'si': 0}
                c['raw'] = [ps.enter_context(sbuf_u("qkraw%d" % i, [128, 512], F32)) for i in range(3)]
                c['sq'] = [ps.enter_context(sbuf_u("qksq%d" % i, [128, 512], BF16)) for i in range(3)]
                c['rs'] = [ps.enter_context(sbuf_u("qkrs%d" % i, [128, 512], F32)) for i in range(3)]
                c['sf'] = [ps.enter_context(sbuf_u("qksf%d" % i, [128, 1024], BF16)) for i in range(2)]
                c['pn'] = ps.enter_context(psum_u("qkpn", [128, 512], F32))
                return c

            def flush(c):
                job = c['pend']
                if job is None:
                    return
                c['pend'] = None
                (i, sz, o, which, sf, sfk, last, col, t0, n) = job
                raw, sq, rs = c['raw'][i], c['sq'][i], c['rs'][i]
                rawk, sqk, rsk = "qkraw%d" % i, "qksq%d" % i, "qkrs%d" % i
                pn = c['pn']
                op('pe', lambda e: e.matmul(pn[:, 0:sz], lhsT=onesb, rhs=sq[:, 0:sz], start=True, stop=True), reads=[sqk, 'ctb'], writes=['qkpn'])
                op('act', lambda e: e.activation(out=rs[:, 0:sz], in_=pn[:, 0:sz], func=AF.Ln, bias=epst[:, 0:1], scale=1.0 / 128), reads=['qkpn', 'epst'], writes=[rsk])
                op('act', lambda e: e.activation(out=rs[:, 0:sz], in_=rs[:, 0:sz], func=AF.Exp, scale=-0.5), reads=[rsk], writes=[rsk])
                op('dve', lambda e: e.scalar_tensor_tensor(out=sf[:, o:o + sz], in0=raw[:, 0:sz], scalar=qkw[:, which:which + 1], in1=rs[:, 0:sz],
                                                         op0=ALU.mult, op1=ALU.mult), reads=[rawk, rsk, 'qkw'], writes=[sfk])
                if last:
                    dma('pool', U_F[col:col + 128, t0:t0 + n], sf[:, 0:n], reads=[sfk], writes=['U_F'])

            if kind == 'end':
                flush(a[0])
                return
            c, banks, col, t0, n = a
            which = 0 if col >= OFF_NQ else 1
            si = c['si'] % 2
            c['si'] += 1
            sf = c['sf'][si]
            sfk = "qksf%d" % si
            for bi_, (p_, pk, o, sz) in enumerate(banks):
                i = c['i'] % 3
                c['i'] += 1
                raw, sq = c['raw'][i], c['sq'][i]
                rawk, sqk = "qkraw%d" % i, "qksq%d" % i
                op('act', lambda e, p_=p_, sz=sz, sq=sq: e.activation(out=sq[:, 0:sz], in_=p_[:, 0:sz], func=AF.Square), reads=[pk], writes=[sqk])
                op('dve', lambda e, p_=p_, sz=sz, raw=raw: e.tensor_copy(out=raw[:, 0:sz], in_=p_[:, 0:sz]), reads=[pk, sqk], writes=[rawk])
                flush(c)
                c['pend'] = (i, sz, o, which, sf, sfk, bi_ == len(banks) - 1, col, t0, n)

        def epi_inproj(kind, *a):
            if kind == 'end':
                return
            if kind == 'alloc':
                ps = a[0]
                c = {}
                c['sf'] = [ps.enter_context(sbuf_u("ipf%d" % i, [128, 1024], BF16)) for i in range(3)]
                c['st'] = [ps.enter_context(sbuf_u("ipt%d" % i, [128, 512], F32)) for i in range(3)]
                c['stb'] = [ps.enter_context(sbuf_u("iptb%d" % i, [128, 512], BF16)) for i in range(3)]
                c['st2'] = [ps.enter_context(sbuf_u("ipt2_%d" % i, [128, 512], F32)) for i in range(3)]
                c['sq'] = [ps.enter_context(sbuf_u("ipq%d" % i, [128, 1024], F32)) for i in range(2)]
                c['sq2'] = [ps.enter_context(sbuf_u("ipq2_%d" % i, [128, 1024], F32)) for i in range(2)]
                c['qi'] = 0
                c['i'] = 0
                return c
            if kind == 'F' and OFF_HQ <= a[2] < OFF_HQ + 512:
                c, banks, col, t0, n = a
                i = c['qi'] % 2
                c['qi'] += 1
                sq, sq2 = c['sq'][i], c['sq2'][i]
                sqk, sq2k = "ipq%d" % i, "ipq2_%d" % i
                for (p_, pk, o, sz) in banks:
                    op('act', lambda e, p_=p_, o=o, sz=sz: e.activation(out=sq[:, o:o + sz], in_=p_[:, 0:sz], func=AF.Exp, scale=-1.0), reads=[pk], writes=[sqk])
                    op('act', lambda e, o=o, sz=sz: e.activation(out=sq[:, o:o + sz], in_=sq[:, o:o + sz], func=AF.Ln, bias=onet[:, 0:1]), reads=[sqk, 'onet'], writes=[sqk])
                    op('act', lambda e, o=o, sz=sz: e.activation(out=sq[:, o:o + sz], in_=sq[:, o:o + sz], func=AF.Exp, scale=-1.0), reads=[sqk], writes=[sqk])
                    op('dve', lambda e, p_=p_, o=o, sz=sz: e.tensor_tensor(out=sq2[:, o:o + sz], in0=p_[:, 0:sz], in1=sq[:, o:o + sz], op=ALU.mult),
                       reads=[pk, sqk], writes=[sq2k])
                dma('pool', U_qs[col - OFF_HQ:col - OFF_HQ + 128, t0:t0 + n], sq2[:, 0:n], reads=[sq2k], writes=['U_qs'])
            elif kind == 'F':
                c, banks, col, t0, n = a
                i = c['i'] % 3
                c['i'] += 1
                sf = c['sf'][i]
                sk = "ipf%d" % i
                for bi_, (p_, pk, o, sz) in enumerate(banks):
                    if bi_ % 2 == 0:
                        op('act', lambda e, p_=p_, o=o, sz=sz: e.activation(out=sf[:, o:o + sz], in_=p_[:, 0:sz], func=AF.Copy),
                           reads=[pk], writes=[sk])
                    else:
                        op('dve', lambda e, p_=p_, o=o, sz=sz: e.tensor_copy(out=sf[:, o:o + sz], in_=p_[:, 0:sz]),
                           reads=[pk], writes=[sk])
                dma('pool', U_F[col:col + 128, t0:t0 + n], sf[:, 0:n], reads=[sk], writes=['U_F'])
            else:
                c, (p_, pk), c0, tt, _ = a
                i = c['i'] % 3
                c['i'] += 1
                if c0 < 1024:
                    s_ = c['st'][i]
                    s2 = c['st2'][i]
                    sk = "ipt%d" % i
                    s2k = "ipt2_%d" % i
                    lbs = lbt[:, c0:c0 + 512]
                    oms = omt[:, c0:c0 + 512]
                    op('act', lambda e: e.activation(out=s_[:], in_=p_[:], func=AF.Exp, scale=-1.0), reads=[pk], writes=[sk])
                    op('act', lambda e: e.activation(out=s_[:], in_=s_[:], func=AF.Ln, bias=onet[:, 0:1]), reads=[sk, 'onet'], writes=[sk])
                    op('act', lambda e: e.activation(out=s_[:], in_=s_[:], func=AF.Exp, scale=-1.0), reads=[sk], writes=[sk])
                    op('dve', lambda e: e.tensor_tensor(out=s2[:], in0=s_[:], in1=oms, op=ALU.mult), reads=[sk, 'omt'], writes=[s2k])
                    op('dve', lambda e: e.scalar_tensor_tensor(out=s_[:], in0=s2[:], scalar=1e-30, in1=lbs, op0=ALU.max, op1=ALU.add),
                       reads=[s2k, 'lbt'], writes=[sk])
                    op('act', lambda e: e.activation(out=s_[:], in_=s_[:], func=AF.Ln), reads=[sk], writes=[sk])
                    op('dve', lambda e: e.tensor_tensor(out=s2[:], in0=oms, in1=s2[:], op=ALU.subtract), reads=[s2k, 'omt'], writes=[s2k])
                    dma('pool', U_f[tt:tt + 128, c0:c0 + 512], s_[:], reads=[sk], writes=['U_f'])
                    dma('pool', U_k[tt:tt + 128, c0:c0 + 512], s2[:], reads=[s2k], writes=['U_k'])
                else:
                    s_ = c['stb'][i]
                    sk = "iptb%d" % i
                    op('dve', lambda e: e.tensor_copy(out=s_[:], in_=p_[:]), reads=[pk], writes=[sk])
                    if c0 == OFF_HI:
                        dma('pool', U_i[tt:tt + 128, :], s_[:], reads=[sk], writes=['U_i'])
                    else:
                        dma('pool', U_nv[tt:tt + 128, c0 - OFF_NV:c0 - OFF_NV + 512], s_[:], reads=[sk], writes=['U_nv'])


        def block_norm_store(ps_ctx, src_ap, n, wcol, dst_rows, t0, extra_mul=None, keys=()):
            c = ps_ctx
            i = c['i'] % 2
            c['i'] += 1
            sq, rs, ob, pn = c['sq'][i], c['rs'][i], c['ob'][i], c['pn'][i]
            sqk, rsk, obk, pnk = "bn_sq%d" % i, "bn_rs%d" % i, "bn_ob%d" % i, "bn_pn%d" % i
            op('act', lambda e: e.activation(out=sq[:, 0:n], in_=src_ap, func=AF.Square), reads=list(keys), writes=[sqk])
            op('pe', lambda e: e.matmul(pn[:, 0:n], lhsT=onesb, rhs=sq[:, 0:n], start=True, stop=True), reads=[sqk, 'ctb'], writes=[pnk])
            op('act', lambda e: e.activation(out=rs[:, 0:n], in_=pn[:, 0:n], func=AF.Ln, bias=epst[:, 0:1], scale=1.0 / 128),
               reads=[pnk, 'epst'], writes=[rsk])
            op('act', lambda e: e.activation(out=rs[:, 0:n], in_=rs[:, 0:n], func=AF.Exp, scale=-0.5), reads=[rsk], writes=[rsk])
            if extra_mul is None:
                op('dve', lambda e: e.scalar_tensor_tensor(out=ob[:, 0:n], in0=src_ap, scalar=vt[:, wcol:wcol + 1], in1=rs[:, 0:n],
                                                         op0=ALU.mult, op1=ALU.mult), reads=list(keys) + [rsk, 'vt'], writes=[obk])
            else:
                op('dve', lambda e: e.scalar_tensor_tensor(out=rs[:, 0:n], in0=src_ap, scalar=vt[:, wcol:wcol + 1], in1=rs[:, 0:n],
                                                         op0=ALU.mult, op1=ALU.mult), reads=list(keys) + [rsk, 'vt'], writes=[rsk])
                emul, ekeys = extra_mul
                op('dve', lambda e: e.tensor_tensor(out=ob[:, 0:n], in0=rs[:, 0:n], in1=emul, op=ALU.mult),
                   reads=[rsk] + list(ekeys), writes=[obk])
            dma('sp', mixF[dst_rows:dst_rows + 128, t0:t0 + n], ob[:, 0:n], reads=[obk], writes=['mixF'])

        def bn_alloc(ps):
            c = {'i': 0}
            c['sq'] = [ps.enter_context(sbuf_u("bnsq%d" % i, [128, 512], BF16)) for i in range(2)]
            c['rs'] = [ps.enter_context(sbuf_u("bnrs%d" % i, [128, 512], F32)) for i in range(2)]
            c['ob'] = [ps.enter_context(sbuf_u("bnob%d" % i, [128, 512], BF16)) for i in range(2)]
            c['pn'] = [ps.enter_context(psum_u("bnpn%d" % i, [128, 512], F32)) for i in range(2)]
            return c

        def phase_hgrn(l):
            NSC = 4
            W_ = NSC * 128
            NQ = NSC * 64
            with contextlib.ExitStack() as ps:
                bn = bn_alloc(ps)
                Obs = [ps.enter_context(sbuf_u("hgO%d" % i, [128, T], F32)) for i in range(2)]
                St = [ps.enter_context(sbuf_u("hgS%d" % i, [128, 128], F32)) for i in range(2)]
                gsb = [ps.enter_context(sbuf_u("hggs%d" % i, [128, 512], BF16)) for i in range(2)]
                gsf = [ps.enter_context(sbuf_u("hggf%d" % i, [128, 512], F32)) for i in range(2)]
                NBUF = 2
                names = [('A', [128, W_], F32), ('B', [128, W_], F32), ('C', [128, W_], F32), ('v', [128, W_], BF16), ('q', [128, NQ], BF16),
                         ('qe', [128, NQ], F32), ('qs', [128, NQ], F32), ('E', [128, NSC * 66], F32), ('En', [128, W_], F32), ('kt', [128, W_], BF16),
                         ('qt', [128, NQ], BF16), ('ktF', [128, NQ], BF16), ('amf', [64, NQ], F32), ('am', [64, NQ], BF16)]
                BUF = [{nm: ps.enter_context(sbuf_u("hg%s%d" % (nm, i), shp, dt)) for (nm, shp, dt) in names} for i in range(NBUF)]
                Sp = [ps.enter_context(sbuf_u("hgSp%d" % i, [128, 128], BF16)) for i in range(2)]
                pA = ps.enter_context(psum_u("hgpA", [128, 512], F32))
                pBB = ps.enter_context(psum_u("hgpBB", [128, 512], F32))
                pT = ps.enter_context(psum_u("hgpT", [128, 1024], BF16))
                pAtt = ps.enter_context(psum_u("hgpAtt", [128, 512], F32))
                pKv = ps.enter_context(psum_u("hgpKv", [128, 512], F32))
                pO = ps.enter_context(psum_u("hgpO", [128, 512], F32))
                nlat = SEQ // (NSC * 64)
                items = []
                for hd in range(4):
                    for dr in range(2):
                        scs = [0] + [CTX + i * NSC * 64 for i in (range(nlat) if dr == 0 else reversed(range(nlat)))]
                        for j, t0 in enumerate(scs):
                            items.append((hd, dr, j, t0, len(scs)))
                n = NSC * 64
                state = {'sidx': 0, 'spi': 0}

                def bufs(i):
                    b2 = i % NBUF
                    Bf = BUF[b2]
                    return [Bf[nm] for (nm, _, _) in names], ["hg%s%d" % (nm, b2) for (nm, _, _) in names]

                def prep(i, part):
                    hd, dr, j, t0, _ = items[i]
                    (A_, B_, C_, v_, q_, qe_, qs_, E_, En_, kt_, qt_, ktF_, amf_, am_), (Ak, Bk, Ck, vk, qk, qek, qsk, Ek, Enk, ktk, qtk, ktFk, amfk, amk) = bufs(i)
                    if part != 1:
                        return
                    fcol = dr * 512 + hd * 128
                    dma('sp', A_[0:64, :].rearrange("p (c d) -> p c d", c=NSC),
                        U_f[t0:t0 + n, fcol:fcol + 128].rearrange("(c p) d -> p c d", p=64), writes=[Ak + "_0"])
                    for half in range(2):
                        dma('sp', C_[half * 64:(half + 1) * 64, :].rearrange("p (c d) -> p c d", c=NSC),
                            U_k[t0:t0 + n, fcol:fcol + 128].rearrange("(c p) d -> p c d", p=64), writes=[Ck + "_%d" % half])
                        dma('sp', v_[half * 64:(half + 1) * 64, :].rearrange("p (c d) -> p c d", c=NSC),
                            U_i[t0:t0 + n, hd * 128:hd * 128 + 128].rearrange("(c p) d -> p c d", p=64), writes=[vk + "_%d" % half])
                    dma('sp', qs_[:], U_qs[hd * 128:hd * 128 + 128, t0:t0 + n], writes=[qsk])

                def partAB(i, stage_):
                    hd, dr, j, t0, nsc_chain = items[i]
                    Ob = Obs[hd % 2]
                    (A_, B_, C_, v_, q_, qe_, qs_, E_, En_, kt_, qt_, ktF_, amf_, am_), (Ak, Bk, Ck, vk, qk, qek, qsk, Ek, Enk, ktk, qtk, ktFk, amfk, amk) = bufs(i)
                    Aks = [Ak + "_0"]
                    Cks = [Ck + "_0", Ck + "_1"]
                    vks = [vk + "_0", vk + "_1"]
                    mext = ct[0:64, (C_MEXT_F if dr == 0 else C_MEXT_B):(C_MEXT_F if dr == 0 else C_MEXT_B) + 66]
                    mcc = ct[0:64, (C_MCC_F if dr == 0 else C_MCC_B):(C_MCC_F if dr == 0 else C_MCC_B) + 128]
                    mkr = ct[0:64, (C_MKR_F if dr == 0 else C_MKR_B):(C_MKR_F if dr == 0 else C_MKR_B) + NQ]
                    def stA1():
                        for c in range(NSC):
                            op('pe', lambda e, c=c: e.matmul(pA[:, c * 66:(c + 1) * 66], lhsT=A_[0:64, c * 128:(c + 1) * 128], rhs=mext, start=True, stop=True),
                               reads=Aks + ['ct'], writes=['hgpA'], inc=(c == NSC - 1))
                        for c in range(NSC):
                            op('pe', lambda e, c=c: e.matmul(pBB[:, c * 128:(c + 1) * 128], lhsT=mcc, rhs=A_[0:64, c * 128:(c + 1) * 128], start=True, stop=True),
                               reads=Aks + ['ct'], writes=['hgpBB'], inc=(c == NSC - 1))

                    def stA1act():
                        op('act', lambda e: e.activation(out=E_[:], in_=pA[:, 0:NSC * 66], func=AF.Exp), reads=['hgpA'], writes=[Ek])
                        op('act', lambda e: e.activation(out=En_[:], in_=pBB[:, :], func=AF.Exp, scale=-1.0), reads=['hgpBB'], writes=[Enk])

                    def stA2():
                        op('dve', lambda e: e.tensor_tensor(out=kt_[:], in0=C_[:], in1=En_[:], op=ALU.mult), reads=Cks + [Enk], writes=[ktk])
                        op('dve', lambda e: e.tensor_tensor(out=qt_[:].rearrange("p (c w) -> p c w", c=NSC), in0=qs_[:].rearrange("p (c w) -> p c w", c=NSC),
                                                           in1=E_[:].rearrange("p (c w) -> p c w", c=NSC)[:, :, 0:64], op=ALU.mult), reads=[qsk, Ek], writes=[qtk])
                        for c in range(NSC):
                            op('pe', lambda e, c=c: e.transpose(out=pT[:, c * 64:(c + 1) * 64], in_=kt_[0:64, c * 128:(c + 1) * 128], identity=identb[0:64, 0:64]),
                               reads=[ktk, 'ctb'], writes=['hgpT'], inc=(c == NSC - 1))
                        op('act', lambda e: e.activation(out=ktF_[:], in_=pT[:, 0:NQ], func=AF.Copy), reads=['hgpT'], writes=[ktFk])

                    def stA3():
                        for c in range(NSC):
                            op('pe', lambda e, c=c: e.matmul(pAtt[0:64, c * 64:(c + 1) * 64], lhsT=ktF_[:, c * 64:(c + 1) * 64], rhs=qt_[:, c * 64:(c + 1) * 64],
                                                            start=True, stop=True), reads=[ktFk, qtk], writes=['hgpAtt'], inc=(c == NSC - 1))
                        op('dve', lambda e: e.tensor_scalar(out=amf_[:], in0=pAtt[0:64, 0:NQ], scalar1=1e30, scalar2=-1e30, op0=ALU.min, op1=ALU.max),
                           reads=['hgpAtt'], writes=[amfk])
                        op('dve', lambda e: e.tensor_tensor(out=am_[:], in0=amf_[:], in1=mkr, op=ALU.mult), reads=[amfk, 'ct'], writes=[amk])
                        for c in range(NSC):
                            op('pe', lambda e, c=c: e.matmul(pKv[:, c * 128:(c + 1) * 128], lhsT=kt_[64:128, c * 128:(c + 1) * 128], rhs=v_[64:128, c * 128:(c + 1) * 128],
                                                            start=True, stop=True), reads=[ktk] + vks, writes=['hgpKv'], inc=(c == NSC - 1))

                    def stB():
                        if j == 0:
                            state['sidx'] = 0
                            op('dve', lambda e: e.memset(St[0][:], 0.0), writes=['hgS0'])
                        order = list(range(NSC)) if dr == 0 else list(reversed(range(NSC)))
                        for c in order:
                            sidx = state['sidx']
                            Sold, Snew = St[sidx % 2], St[(sidx + 1) % 2]
                            Soldk, Snewk = "hgS%d" % (sidx % 2), "hgS%d" % ((sidx + 1) % 2)
                            state['sidx'] += 1
                            Sp_ = Sp[state['spi'] % 2]
                            Spk = "hgSp%d" % (state['spi'] % 2)
                            state['spi'] += 1
                            op('act', lambda e, c=c, Sp_=Sp_, Sold=Sold: e.activation(out=Sp_[:], in_=Sold[:], func=AF.Identity, scale=E_[:, c * 66 + 65:c * 66 + 66]),
                               reads=[Soldk, Ek], writes=[Spk])
                            op('pe', lambda e, c=c, Sp_=Sp_: e.matmul(pO[:, c * 64:(c + 1) * 64], lhsT=Sp_[:], rhs=qt_[:, c * 64:(c + 1) * 64], start=True, stop=False),
                               reads=[Spk, qtk], writes=['hgpO'], inc=False)
                            op('pe', lambda e, c=c: e.matmul(pO[:, c * 64:(c + 1) * 64], lhsT=v_[0:64, c * 128:(c + 1) * 128], rhs=am_[:, c * 64:(c + 1) * 64], start=False, stop=True),
                               reads=vks + [amk], writes=['hgpO'])
                            op('dve', lambda e, c=c, Sold=Sold, Snew=Snew: e.scalar_tensor_tensor(
                                out=Snew[:], in0=Sold[:], scalar=E_[:, c * 66 + 64:c * 66 + 65], in1=pKv[:, c * 128:(c + 1) * 128], op0=ALU.mult, op1=ALU.add),
                               reads=[Soldk, Ek, 'hgpKv'], writes=[Snewk])
                        ok_ = "hgO%d_%d" % (hd % 2, t0 // 256)
                        if dr == 0:
                            op('act', lambda e: e.activation(out=Ob[:, t0:t0 + n], in_=pO[:, 0:n], func=AF.Copy), reads=['hgpO'], writes=[ok_])
                        else:
                            op('dve', lambda e: e.tensor_tensor(out=Ob[:, t0:t0 + n], in0=pO[:, 0:n], in1=Ob[:, t0:t0 + n], op=ALU.add),
                               reads=['hgpO', ok_], writes=[ok_])

                    {0: stA1, 1: stA2, 2: stA3, 3: stB, 4: stA1act}[stage_]()

                def readout(hd):
                    Ob = Obs[hd % 2]
                    for bi_, (t0, nb) in enumerate([(0, 256)] + [(256 + i * 512, 512) for i in range(8)]):
                        b2 = bi_ % 2
                        gk_, gfk_ = "hggs%d" % b2, "hggf%d" % b2
                        dma('sp', gsb[b2][:, 0:nb], U_F[OFF_HGG + hd * 128:OFF_HGG + hd * 128 + 128, t0:t0 + nb], writes=[gk_])
                        op('act', lambda e, b2=b2, nb=nb: e.activation(out=gsf[b2][:, 0:nb], in_=gsb[b2][:, 0:nb], func=AF.Exp, scale=-1.0), reads=[gk_], writes=[gfk_])
                        op('act', lambda e, b2=b2, nb=nb: e.activation(out=gsf[b2][:, 0:nb], in_=gsf[b2][:, 0:nb], func=AF.Ln, bias=onet[:, 0:1]), reads=[gfk_, 'onet'], writes=[gfk_])
                        op('act', lambda e, b2=b2, nb=nb: e.activation(out=gsf[b2][:, 0:nb], in_=gsf[b2][:, 0:nb], func=AF.Exp, scale=-1.0), reads=[gfk_], writes=[gfk_])
                        op('dve', lambda e, b2=b2, nb=nb: e.tensor_tensor(out=gsf[b2][:, 0:nb], in0=gsf[b2][:, 0:nb], in1=gsb[b2][:, 0:nb], op=ALU.mult), reads=[gfk_, gk_], writes=[gfk_])
                        okeys = ["hgO%d_%d" % (hd % 2, jj) for jj in range(t0 // 256, (t0 + nb) // 256)]
                        block_norm_store(bn, Ob[:, t0:t0 + nb], nb, V_HGN + l, hd * 128, t0, extra_mul=(gsf[b2][:, 0:nb], [gfk_]), keys=okeys)

                prep(0, 1)
                prep(0, 2)
                partAB(0, 0)
                for i in range(len(items)):
                    nxt = i + 1 < len(items)
                    partAB(i, 4)
                    partAB(i, 1)
                    if nxt:
                        prep(i + 1, 1)
                    partAB(i, 2)
                    if nxt:
                        prep(i + 1, 2)
                        partAB(i + 1, 0)
                    partAB(i, 3)
                    hd, dr, j, t0, nsc_chain = items[i]
                    if dr == 1 and j == nsc_chain - 1:
                        readout(hd)
            S.barrier()

        def phase_na(l, skip_ctx_out=False, cast_tag=None):
            with contextlib.ExitStack() as ps:
                bn = bn_alloc(ps)
                qn = ps.enter_context(sbuf_u("naq", [128, T], BF16))
                kn = ps.enter_context(sbuf_u("nak", [128, T], BF16))
                raw = [ps.enter_context(sbuf_u("naraw%d" % i, [128, 512], BF16)) for i in range(2)]
                nsq = [ps.enter_context(sbuf_u("nansq%d" % i, [128, 512], BF16)) for i in range(2)]
                nrs = [ps.enter_context(sbuf_u("nanrs%d" % i, [128, 512], F32)) for i in range(2)]
                v0 = ps.enter_context(sbuf_u("nav0", [128, NT * 128], BF16))
                v1 = ps.enter_context(sbuf_u("nav1", [128, 31 * 128], BF16))
                nbf = ps.enter_context(sbuf_u("nanbf", [128, 14 * 64], F32))
                nbb = ps.enter_context(sbuf_u("nanbb", [128, 14 * 64], BF16))
                wsc = ps.enter_context(sbuf_u("nawsc", [128, 2], F32))
                PT = [ps.enter_context(sbuf_u("naPT%d" % i, [128, 512], BF16)) for i in range(2)]
                rd = [ps.enter_context(sbuf_u("nard%d" % i, [128, 256], F32)) for i in range(2)]
                ob = [ps.enter_context(sbuf_u("naob%d" % i, [128, 512], F32)) for i in range(2)]
                pS = [ps.enter_context(psum_u("napS%d" % i, [128, 512], F32)) for i in range(2)]
                pO = [ps.enter_context(psum_u("napO%d" % i, [128, 512], F32)) for i in range(2)]
                pD = [ps.enter_context(psum_u("napD%d" % i, [128, 512], F32)) for i in range(2)]
                pN = bn['pn'][0]
                op('dve', lambda e: e.tensor_scalar(out=wsc[:, 0:1], in0=vt[:, V_QN + l:V_QN + l + 1], scalar1=float(HD ** -0.5), scalar2=None, op0=ALU.mult),
                   reads=['vt'], writes=['nawsc'])
                op('dve', lambda e: e.tensor_copy(out=wsc[:, 1:2], in_=vt[:, V_KN + l:V_KN + l + 1]), reads=['vt'], writes=['nawsc'])
                blocks = [(0, 256)] + [(256 + i * 512, 512) for i in range(8)]
                ri = 0
                rowi = 0
                for hd in range(N_NA):
                    if cast_tag is not None and hd % 2 == 0 and hd // 2 < len(cast_plan[cast_tag]):
                        issue_casts(cast_tag, hd // 2)
                    dma('sp', v0[:].rearrange("p (j d) -> p j d", j=NT), U_nv[0:T, hd * 128:hd * 128 + 128].rearrange("(j p) d -> p j d", p=128), writes=['nav0'])
                    dma('sp', v1[:].rearrange("p (j d) -> p j d", j=31),
                        U_nv[CTX + 64:CTX + 64 + 31 * 128, hd * 128:hd * 128 + 128].rearrange("(j p) d -> p j d", p=128), writes=['nav1'])
                    dma('sp', nbf[:], natab[l, hd], writes=['nanbf'])
                    op('dve', lambda e: e.tensor_copy(out=nbb[:], in_=nbf[:]), reads=['nanbf'], writes=['nanbb'])
                    dma('sp', qn[:], U_F[OFF_NQ + hd * 128:OFF_NQ + hd * 128 + 128, :], writes=['naq'])
                    dma('sp', kn[:], U_F[OFF_NK + hd * 128:OFF_NK + hd * 128 + 128, :], writes=['nak'])

                    def att_scores(job):
                        nonlocal rowi
                        qlo, nq, tiles = job['qlo'], job['nq'], job['tiles']
                        b2 = rowi % 2
                        rowi += 1
                        job['b2'] = b2
                        pS_ = pS[b2]
                        pSk = "napS%d" % b2
                        nt_ = len(tiles)
                        for i, (klo, vap, bap) in enumerate(tiles):
                            last = (i == nt_ - 1)
                            op('pe', lambda e, i=i, klo=klo, bap=bap: e.matmul(pS_[:, i * nq:(i + 1) * nq], lhsT=kn[:, klo:klo + 128], rhs=qn[:, qlo:qlo + nq],
                                                                              start=True, stop=(bap is None)), reads=['nak', 'naq'], writes=[pSk], inc=(bap is None and last))
                            if bap is not None:
                                op('pe', lambda e, i=i, bap=bap: e.matmul(pS_[:, i * nq:(i + 1) * nq], lhsT=identb, rhs=bap, start=False, stop=True),
                                   reads=['nanbb', 'ctb'], writes=[pSk], inc=last)

                    def att_finish(job):
                        qlo, nq, tiles, obuf, ocol, obk, b2 = job['qlo'], job['nq'], job['tiles'], job['obuf'], job['ocol'], job['obk'], job['b2']
                        pS_, pO_, pD_, PT_, rd_ = pS[b2], pO[b2], pD[b2], PT[b2], rd[b2]
                        pSk, pOk, pDk, PTk, rdk = "napS%d" % b2, "napO%d" % b2, "napD%d" % b2, "naPT%d" % b2, "nard%d" % b2
                        nt_ = len(tiles)
                        W_ = nt_ * nq
                        op('act', lambda e: e.activation(out=PT_[:, 0:W_], in_=pS_[:, 0:W_], func=AF.Exp), reads=[pSk], writes=[PTk])
                        for i, (klo, vap, bap) in enumerate(tiles):
                            op('pe', lambda e, i=i, vap=vap: e.matmul(pO_[:, 0:nq], lhsT=vap, rhs=PT_[:, i * nq:(i + 1) * nq], start=(i == 0), stop=(i == nt_ - 1)),
                               reads=['nav0', 'nav1', PTk], writes=[pOk], inc=(i == nt_ - 1))
                        for i in range(nt_):
                            op('pe', lambda e, i=i: e.matmul(pD_[:, 0:nq], lhsT=onesb, rhs=PT_[:, i * nq:(i + 1) * nq], start=(i == 0), stop=(i == nt_ - 1)),
                               reads=['ctb', PTk], writes=[pDk], inc=(i == nt_ - 1))
                        op('dve', lambda e: e.reciprocal(out=rd_[:, 0:nq], in_=pD_[:, 0:nq]), reads=[pDk], writes=[rdk])
                        op('dve', lambda e: e.tensor_tensor(out=obuf[:, ocol:ocol + nq], in0=pO_[:, 0:nq], in1=rd_[:, 0:nq], op=ALU.mult),
                           reads=[pOk, rdk], writes=[obk])
                        if job.get('norm') is not None:
                            nn, t0n = job['norm']
                            block_norm_store(bn, obuf[:, 0:nn], nn, V_NAON + l * 8 + hd, 512 + hd * 128, t0n, keys=[obk])

                    ctiles = [(0, v0[:, 0:128], None), (128, v0[:, 128:256], None)]
                    jobs = []
                    oi = 0
                    ob_ = ob[oi % 2]
                    obk = "naob%d" % (oi % 2)
                    oi += 1
                    if not skip_ctx_out:
                        jobs.append({'qlo': 0, 'nq': 256, 'tiles': ctiles, 'obuf': ob_, 'ocol': 0, 'obk': obk, 'norm': (256, 0)})
                    for r in range(ROWS):
                        if r % 8 == 0:
                            ob_ = ob[oi % 2]
                            obk = "naob%d" % (oi % 2)
                            oi += 1
                        rs0 = min(max(r - 4, 0), ROWS - 8)
                        par = rs0 % 2
                        tiles = []
                        for i in range(4):
                            kr = rs0 + 2 * i
                            klo = CTX + kr * 64
                            if par == 0:
                                j = 2 + kr // 2
                                vap = v0[:, j * 128:(j + 1) * 128]
                            else:
                                j = (kr - 1) // 2
                                vap = v1[:, j * 128:(j + 1) * 128]
                            pi_ = kr - r + 7
                            tiles.append((klo, vap, nbb[:, pi_ * 64:(pi_ + 1) * 64]))
                        tiles += ctiles
                        jobs.append({'qlo': CTX + r * 64, 'nq': 64, 'tiles': tiles, 'obuf': ob_, 'ocol': (r % 8) * 64, 'obk': obk,
                                     'norm': (512, CTX + (r // 8) * 512) if r % 8 == 7 else None})
                    att_scores(jobs[0])
                    for ji in range(len(jobs)):
                        if ji + 1 < len(jobs):
                            att_scores(jobs[ji + 1])
                        att_finish(jobs[ji])
            S.barrier()

        def phase_cv(l):
            with contextlib.ExitStack() as ps:
                bn = bn_alloc(ps)
                Bt = ps.enter_context(sbuf_u("cvB", [128, T], BF16))
                Ct = ps.enter_context(sbuf_u("cvC", [128, T], BF16))
                Vt = ps.enter_context(sbuf_u("cvV", [128, T], BF16))
                cvv = ps.enter_context(sbuf_u("cvcv", [128, T], F32))
                acc = ps.enter_context(sbuf_u("cvacc", [128, T], F32))
                for g in range(N_CV):
                    for (tile_, off, k_) in ((Bt, OFF_CB, 'cvB'), (Ct, OFF_CC, 'cvC'), (Vt, OFF_CVV, 'cvV')):
                        dma('sp', tile_[:], U_F[off + g * 128:off + g * 128 + 128, :], writes=[k_])
                    wc = V_CVW + l * 12
                    op('dve', lambda e: e.tensor_tensor(out=cvv[:], in0=Ct[:], in1=Vt[:], op=ALU.mult), reads=['cvC', 'cvV'], writes=['cvcv'])
                    op('act', lambda e: e.activation(out=acc[:], in_=cvv[:], func=AF.Identity, scale=vt[:, wc + 4 + g:wc + 4 + g + 1]), reads=['cvcv', 'vt'], writes=['cvacc'])
                    for (lo, hi) in ((0, CTX), (CTX, T)):
                        op('dve', lambda e, lo=lo, hi=hi: e.scalar_tensor_tensor(out=acc[:, lo + 1:hi], in0=cvv[:, lo:hi - 1], scalar=vt[:, wc + g:wc + g + 1],
                                                                               in1=acc[:, lo + 1:hi], op0=ALU.mult, op1=ALU.add), reads=['cvcv', 'cvacc', 'vt'], writes=['cvacc'])
                        op('dve', lambda e, lo=lo, hi=hi: e.scalar_tensor_tensor(out=acc[:, lo:hi - 1], in0=cvv[:, lo + 1:hi], scalar=vt[:, wc + 8 + g:wc + 8 + g + 1],
                                                                               in1=acc[:, lo:hi - 1], op0=ALU.mult, op1=ALU.add), reads=['cvcv', 'cvacc', 'vt'], writes=['cvacc'])
                    op('dve', lambda e: e.tensor_tensor(out=acc[:], in0=acc[:], in1=Bt[:], op=ALU.mult), reads=['cvacc', 'cvB'], writes=['cvacc'])
                    for (t0, n) in [(0, 256)] + [(256 + i * 512, 512) for i in range(8)]:
                        block_norm_store(bn, acc[:, t0:t0 + n], n, V_CVON + l * 4 + g, 1536 + g * 128, t0, keys=['cvacc'])
            S.barrier()

        def make_epi_resid(Xsrc, Xdst, gi, lat_only_out=None):
            def epi(kind, *a):
                if kind == 'end':
                    return
                if kind == 'alloc':
                    ps = a[0]
                    c = {'i': 0}
                    c['xs'] = [ps.enter_context(sbuf_u("erx%d" % i, [128, 512], F32)) for i in range(3)]
                    c['tm'] = [ps.enter_context(sbuf_u("ert%d" % i, [128, 512], F32)) for i in range(3)]
                    return c
                c, (p_, pk), c0, tt, _ = a
                i = c['i'] % 3
                c['i'] += 1
                xs, tm = c['xs'][i], c['tm'][i]
                xk, tk = "erx%d" % i, "ert%d" % i
                which = 1 if tt < CTX else 0
                gb = (gi * 2 + which) * D + c0
                dma('sp', xs[:], Xsrc[tt:tt + 128, c0:c0 + 512], writes=[xk])
                op('dve', lambda e: e.tensor_tensor(out=tm[:], in0=p_[:], in1=grow[:, gb:gb + 512], op=ALU.mult), reads=[pk, 'grow'], writes=[tk])
                op('pool', lambda e: e.tensor_tensor(out=tm[:], in0=tm[:], in1=xs[:], op=ALU.add), reads=[tk, xk], writes=[tk])
                if lat_only_out is not None:
                    if tt >= CTX:
                        dma('pool', lat_only_out[tt - CTX:tt - CTX + 128, c0:c0 + 512], tm[:], reads=[tk], writes=['Xout'])
                else:
                    dma('pool', Xdst[tt:tt + 128, c0:c0 + 512], tm[:], reads=[tk], writes=['Xout'])
            return epi

        def phase_ffn_up(l, skip_ctx=False):
            with contextlib.ExitStack() as ps:
                blocks = ([] if skip_ctx else [(0, 256, 0, CTX)]) + [(CTX + o, sz, CTX, T) for (o, sz) in _split(SEQ, 1020)]
                maxn = max(b[1] for b in blocks) + 2
                ab = [ps.enter_context(sbuf_u("fua%d" % i, [128, 16 * maxn], BF16)) for i in range(2)]
                wg = [ps.enter_context(sbuf_u("fuwg%d" % i, [128, 16 * 512], BF16)) for i in range(2)]
                wvl = [ps.enter_context(sbuf_u("fuwv%d" % i, [128, 16 * 512], BF16)) for i in range(2)]
                accg = [ps.enter_context(sbuf_u("fuag%d" % i, [128, maxn], F32)) for i in range(2)]
                accv = [ps.enter_context(sbuf_u("fuav%d" % i, [128, maxn], F32)) for i in range(2)]
                stg = [ps.enter_context(sbuf_u("fust%d" % i, [128, maxn], BF16)) for i in range(2)]
                pp = [ps.enter_context(psum_u("fup%d" % i, [128, 512], F32)) for i in range(8)]
                Av = hF.rearrange("(kc p) t -> p kc t", p=128)
                Wv = wb_up[l].rearrange("(kc p) n -> p kc n", p=128)
                wkl = wkeys['up', l]
                wi = 0
                pidx = 0
                ei = 0
                fcw = V_FCW + l * 264
                fcb = V_FCB + l * 88
                for bi, (t0, n, lo, hi) in enumerate(blocks):
                    a_ = ab[bi % 2]
                    ak = "fua%d" % (bi % 2)
                    av = a_[:, 0:16 * (n + 2)].rearrange("p (kc t) -> p kc t", kc=16)
                    s0_ = max(t0 - 1, lo)
                    s1_ = min(t0 + n + 1, hi)
                    d0 = s0_ - (t0 - 1)
                    if d0 > 0 or s1_ < t0 + n + 1:
                        op('dve', lambda e: e.memset(a_[:, 0:16 * (n + 2)], 0.0), writes=[ak])
                    dma('sp', av[:, :, d0:d0 + (s1_ - s0_)], Av[:, :, s0_:s1_], writes=[ak])
                    for j in range(11):
                        wg_, wv_ = wg[wi % 2], wvl[wi % 2]
                        wgk, wvk = "fuwg%d" % (wi % 2), "fuwv%d" % (wi % 2)
                        wi += 1
                        wgv = wg_[:].rearrange("p (kc n) -> p kc n", kc=16)
                        wvv = wv_[:].rearrange("p (kc n) -> p kc n", kc=16)
                        dma('sp', wgv, Wv[:, :, j * 512:(j + 1) * 512], reads=wkl, writes=[wgk])
                        dma('sp', wvv, Wv[:, :, D_FF + j * 512:D_FF + (j + 1) * 512], reads=wkl, writes=[wvk])
                        for c4 in range(4):
                            gcx = j * 4 + c4
                            e2 = ei % 2
                            ei += 1
                            ag, avl, st_ = accg[e2], accv[e2], stg[e2]
                            agk, avk, stk = "fuag%d" % e2, "fuav%d" % e2, "fust%d" % e2
                            for (wview, wk_, acc_, acck, chn) in ((wgv, wgk, ag, agk, gcx), (wvv, wvk, avl, avk, 44 + gcx)):
                                for (o, sz) in _split(n, 510):
                                    p_ = pp[pidx % 8]
                                    pk = "fup%d" % (pidx % 8)
                                    pidx += 1
                                    for kc in range(16):
                                        op('pe', lambda e, p_=p_, kc=kc, o=o, sz=sz, wview=wview: e.matmul(
                                            p_[:, 0:sz + 2], lhsT=wview[:, kc, c4 * 128:(c4 + 1) * 128], rhs=av[:, kc, o:o + sz + 2],
                                            start=(kc == 0), stop=(kc == 15)), reads=[wk_, ak], writes=[pk], inc=(kc == 15))
                                    op('act', lambda e, p_=p_, o=o, sz=sz, acc_=acc_, chn=chn: e.activation(
                                        out=acc_[:, o:o + sz], in_=p_[:, 1:sz + 1], func=AF.Identity,
                                        scale=vt[:, fcw + 88 + chn:fcw + 88 + chn + 1], bias=vt[:, fcb + chn:fcb + chn + 1]),
                                       reads=[pk, 'vt'], writes=[acck])
                                    op('dve', lambda e, p_=p_, o=o, sz=sz, acc_=acc_, chn=chn: e.scalar_tensor_tensor(
                                        out=acc_[:, o:o + sz], in0=p_[:, 0:sz], scalar=vt[:, fcw + chn:fcw + chn + 1], in1=acc_[:, o:o + sz],
                                        op0=ALU.mult, op1=ALU.add), reads=[pk, 'vt', acck], writes=[acck])
                                    op('dve', lambda e, p_=p_, o=o, sz=sz, acc_=acc_, chn=chn: e.scalar_tensor_tensor(
                                        out=acc_[:, o:o + sz], in0=p_[:, 2:sz + 2], scalar=vt[:, fcw + 176 + chn:fcw + 176 + chn + 1], in1=acc_[:, o:o + sz],
                                        op0=ALU.mult, op1=ALU.add), reads=[pk, 'vt', acck], writes=[acck])
                            op('act', lambda e, ag=ag: e.activation(out=ag[:, 0:n], in_=ag[:, 0:n], func=AF.Silu), reads=[agk], writes=[agk])
                            op('pool', lambda e, ag=ag, avl=avl, st_=st_: e.tensor_tensor(out=st_[:, 0:n], in0=ag[:, 0:n], in1=avl[:, 0:n], op=ALU.mult),
                               reads=[agk, avk], writes=[stk])
                            dma('pool', actF[gcx * 128:(gcx + 1) * 128, t0:t0 + n], st_[:, 0:n], reads=[stk], writes=['actF'])
            S.barrier()

        tblocks_all = [(0, 256)] + [(256 + i * 1024, 1024) for i in range(4)]
        norm_groups = [(0, 2, 1)] + [(256 + i * 512, 4, 0) for i in range(8)]

        X0 = xin
        tb128 = [(0, 256)] + [(256 + i * 1024, 1024) for i in range(4)]
        tb_down = [(0, 256)] + [(256 + i * 512, 512) for i in range(8)]
        for l in range(DEPTH):
            last = (l == DEPTH - 1)
            phase_ada(l)
            if stage <= 0:
                break
            phase_norm(X0, 0, norm_groups)
            if stage <= 1:
                break
            lb_setup(l)
            gemm(hF, D, wb_in[l], wkeys['in', l], 0, 1536, tblocks_all, 'T', epi_inproj, name="ip")
            gemm(hF, D, wb_in[l], wkeys['in', l], OFF_NV, 1024, tblocks_all, 'T', epi_inproj, name="ip")
            gemm(hF, D, wb_in[l], wkeys['in', l], OFF_NK, 1024, tblocks_all, 'F', epi_inproj_qk, name="ip", npp=7)
            gemm(hF, D, wb_in[l], wkeys['in', l], OFF_HQ, 1024, tblocks_all, 'F', epi_inproj, name="ip")
            gemm(hF, D, wb_in[l], wkeys['in', l], OFF_NQ, 1024, tblocks_all, 'F', epi_inproj_qk, name="ip", npp=7)
            gemm(hF, D, wb_in[l], wkeys['in', l], OFF_CB, 1536, tblocks_all, 'F', epi_inproj, name="ip")
            if stage <= 2:
                break
            if 'hg' in phases:
                phase_hgrn(l)
            if 'na' in phases:
                phase_na(l, skip_ctx_out=last, cast_tag=None)
            issue_casts('mix%d' % l)
            if 'cv' in phases:
                phase_cv(l)
            if stage <= 3:
                break
            gemm(mixF, D, wb_out[l], wkeys['out', l], 0, D, tb128[1:] if last else tb128, 'T', make_epi_resid(X0, X1, 0), name="op")
            if stage <= 4:
                break
            phase_norm(X1, 1, norm_groups[1:] if last else norm_groups)
            phase_ffn_up(l, skip_ctx=last)
            if stage <= 5:
                break
            if last:
                gemm(actF, D_FF, wb_down[l], wkeys['down', l], 0, D, tb_down[1:], 'T', make_epi_resid(X1, None, 1, lat_only_out=y), name="dn", abufs=1)
            else:
                gemm(actF, D_FF, wb_down[l], wkeys['down', l], 0, D, tb_down, 'T', make_epi_resid(X1, X2, 1), name="dn", abufs=1)
            X0 = X2
            if stage <= 6 + l:
                break

        if 'dbg_ada' in dbg:
            dada = nc.dram_tensor("dbg_ada", [128, 192 + 128 + 4 * D], F32, kind="ExternalOutput").ap()
            dma('sp', dada[:, 0:192], adaT[:], reads=['adaT'])
            dma('sp', dada[:, 192:320], modv[:], reads=['modv'])
            dma('sp', dada[:, 320:], grow[:], reads=['grow'])
        S.barrier()
        S.final()
    return nc


def _flay(v):
    v = np.asarray(v, np.float32)
    return np.ascontiguousarray(v.reshape(-1, 128).T)


def _consts():
    c = np.zeros((128, NCONST), np.float32)
    c[:, C_ID:C_ID + 128] = np.eye(128, dtype=np.float32)
    c[:, C_ONES:C_ONES + 128] = 1.0
    tp = np.arange(64)[:, None]
    t = np.arange(64)[None, :]
    Mf = (tp <= t).astype(np.float32)
    Mb = (tp >= t).astype(np.float32)
    for (M, mid, oe, oc, om, occ, omr) in ((Mf, 31, C_MEXT_F, C_MC_F, C_MK_F, C_MCC_F, C_MKR_F), (Mb, 32, C_MEXT_B, C_MC_B, C_MK_B, C_MCC_B, C_MKR_B)):
        Mc = M - M[:, mid:mid + 1]
        c[:64, oe:oe + 64] = Mc
        c[:64, oe + 64] = 1.0
        c[:64, oe + 65] = M[:, mid]
        c[:64, oc:oc + 64] = Mc
        c[:64, om:om + 64] = M
        c[:64, occ:occ + 64] = Mc
        c[:64, occ + 64:occ + 128] = M - 1.0
        for rr in range(4):
            c[:64, omr + rr * 64:omr + (rr + 1) * 64] = M
    return c


def _natab(rpb):
    L, H = rpb.shape[0], rpb.shape[1]
    kc = np.arange(64)[:, None]
    c = np.arange(64)[None, :]
    ws = np.clip(c - 8, 0, 48)
    ok = (kc >= ws) & (kc < ws + 16)
    ic = np.clip(kc - c + 15, 0, 30)
    out = np.full((L, H, 2, 64, 14, 64), -1e30, np.float32)
    for pi in range(14):
        for rr in range(2):
            dr = pi - 7 + rr
            g = rpb[:, :, dr + 7, :][:, :, ic]
            out[:, :, rr, :, pi, :] = np.where(ok[None, None], g, np.float32(-1e30))
    return np.ascontiguousarray(out.reshape(L, H, 128, 14 * 64))


def _vecs(inp, b):
    v = np.zeros((128, NVEC), np.float32)
    for l in range(DEPTH):
        v[:, V_BADA + l * 96:V_BADA + (l + 1) * 96] = _flay(inp['b_ada'][l])
        v[:, V_LN1 + l * 16:V_LN1 + (l + 1) * 16] = _flay(inp['ln1_w'][l])
        v[:, V_LN2 + l * 16:V_LN2 + (l + 1) * 16] = _flay(inp['ln2_w'][l])
        v[:, V_HGN + l] = inp['hg_norm_w'][l]
        v[:, V_QN + l] = inp['na_q_norm_w'][l]
        v[:, V_KN + l] = inp['na_k_norm_w'][l]
        v[:, V_NAON + l * 8:V_NAON + (l + 1) * 8] = _flay(inp['na_out_norm_w'][l])
        for t in range(3):
            v[:, V_CVW + l * 12 + t * 4:V_CVW + l * 12 + (t + 1) * 4] = _flay(inp['cv_w'][l, t])
            v[:, V_FCW + l * 264 + t * 88:V_FCW + l * 264 + (t + 1) * 88] = _flay(inp['ffn_conv_w'][l, t])
        v[:, V_CVON + l * 4:V_CVON + (l + 1) * 4] = _flay(inp['cv_out_norm_w'][l])
        v[:, V_FCB + l * 88:V_FCB + (l + 1) * 88] = _flay(inp['ffn_conv_b'][l])
    cc = np.stack([_flay(inp['c'][b]), _flay(inp['c_ctx'])], axis=-1)
    v[:, V_CT:V_CT + 32] = cc.reshape(128, 32)
    return v


def make_in_maps(inp, cores):
    inp = {k: np.asarray(v) for k, v in inp.items()}
    consts = _consts()
    natab = _natab(inp['na_rpb'].astype(np.float32))
    lbl = np.ascontiguousarray(np.broadcast_to(inp['hg_lb_logits'].astype(np.float32).reshape(1, -1), (64, 2 * DEPTH * 512)))
    maps = []
    for i in cores:
        b = i % 4
        m = {
            'xin': np.ascontiguousarray(np.concatenate([inp['ctx'][b], inp['x'][b]], axis=0).astype(np.float32)),
            'vecs': _vecs(inp, b), 'consts': consts, 'lbl': lbl, 'natab': natab,
            'w_ada': inp['w_ada'], 'w_in': inp['w_in'], 'w_out': inp['w_out'], 'w_up': inp['w_up'], 'w_down': inp['w_down'],
        }
        maps.append(m)
    return maps


def kernel(**inputs):
    nc = build()
    cores = list(range(8))
    maps = make_in_maps(inputs, cores)
    res = run_bass_kernel_spmd(nc, maps, core_ids=cores)
    return np.stack([np.asarray(res.results[b]['y'], np.float32) for b in range(4)], axis=0)
```

```python
import contextlib
import numpy as np
import concourse.bass as bass
import concourse.mybir as mybir
from concourse.bass_utils import run_bass_kernel_spmd

F32 = mybir.dt.float32
BF16 = mybir.dt.bfloat16
AF = mybir.ActivationFunctionType
ALU = mybir.AluOpType

D = 2048
DEPTH = 2
CTX = 256
SEQ = 4096
T = CTX + SEQ
GW = 64
ROWS = SEQ // GW
HD = 128
N_HG, N_NA, N_CV = 4, 8, 4
IN_W = 7168
D_FF = 5632
EPS = 1e-6
CH = 64
NT = T // 128

OFF_FFW, OFF_FBW, OFF_HI, OFF_NK, OFF_NV, OFF_HQ, OFF_HGG, OFF_NQ, OFF_CB, OFF_CC, OFF_CVV = (
    0, 512, 1024, 1536, 2560, 3584, 4096, 4608, 5632, 6144, 6656)

V_BADA = 0
V_LN1 = V_BADA + DEPTH * 96
V_LN2 = V_LN1 + DEPTH * 16
V_HGN = V_LN2 + DEPTH * 16
V_QN = V_HGN + DEPTH
V_KN = V_QN + DEPTH
V_NAON = V_KN + DEPTH
V_CVW = V_NAON + DEPTH * 8
V_CVON = V_CVW + DEPTH * 12
V_FCW = V_CVON + DEPTH * 4
V_FCB = V_FCW + DEPTH * 264
V_CT = V_FCB + DEPTH * 88
NVEC = V_CT + 32

C_ID = 0
C_ONES = 128
C_MEXT_F = 256
C_MEXT_B = C_MEXT_F + 66
C_MC_F = C_MEXT_B + 66
C_MC_B = C_MC_F + 64
C_MK_F = C_MC_B + 64
C_MK_B = C_MK_F + 64
C_MCC_F = C_MK_B + 64
C_MCC_B = C_MCC_F + 128
C_MKR_F = C_MCC_B + 128
C_MKR_B = C_MKR_F + 256
NCONST = C_MKR_B + 256


class Sched:
    def __init__(self, nc, es):
        self.nc = nc
        self.eng = {'pe': nc.tensor, 'act': nc.scalar, 'dve': nc.vector, 'pool': nc.gpsimd, 'sp': nc.sync}
        self.sem = {k: es.enter_context(nc.semaphore("sem_" + k)) for k in ('pe', 'act', 'dve', 'pool')}
        self.count = {k: 0 for k in self.sem}
        self.NDS = 40
        self.dsem = [es.enter_context(nc.semaphore("dsem%d" % i)) for i in range(self.NDS)]
        self.dval = [0] * self.NDS
        self.dk = 0
        self.seen = {k: {} for k in self.eng}
        self.res = {}
        self.ninstr = 0
        self.csem = {}
        self.ccnt = {}
        self.es = es

    def _wait(self, en, tok):
        kind, a, v = tok
        if kind == 'e':
            if a == en and en == 'pe':
                return
            key = ('e', a)
            sem = self.sem[a]
        elif kind == 'c':
            key = ('c', a)
            sem = self.csem[a]
            v = 16 * self.ccnt[a]
        else:
            key = ('d', a)
            sem = self.dsem[a]
        if self.seen[en].get(key, 0) >= v:
            return
        self.eng[en].wait_ge(sem, v)
        self.seen[en][key] = v

    def _deps(self, en, reads, writes):
        deps = []
        for k in reads:
            r = self.res.get(k)
            if r and r['w']:
                deps.append(r['w'])
        for k in writes:
            r = self.res.get(k)
            if r:
                deps.extend(r['r'])
                if r['w']:
                    deps.append(r['w'])
        for tok in deps:
            self._wait(en, tok)

    def _update(self, tok, reads, writes):
        for k in reads:
            r = self.res.setdefault(k, {'w': None, 'r': []})
            if tok[0] == 'e':
                r['r'] = [t for t in r['r'] if not (t[0] == 'e' and t[1] == tok[1])]
            r['r'].append(tok)
        for k in writes:
            self.res[k] = {'w': tok, 'r': []}

    def op(self, en, fn, reads=(), writes=(), inc=True):
        self._deps(en, reads, writes)
        ins = fn(self.eng[en])
        self.ninstr += 1
        if inc:
            self.count[en] += 1
            ins.then_inc(self.sem[en], 1)
            tok = ('e', en, self.count[en])
        else:
            tok = ('e', en, self.count[en] + 1)
        self._update(tok, reads, writes)
        return tok

    def dma(self, en, out, in_, reads=(), writes=(), **kw):
        slot = self.dk % self.NDS
        self.dk += 1
        if self.dval[slot] > 0:
            self._wait(en, ('d', slot, self.dval[slot]))
        self._deps(en, reads, writes)
        self.dval[slot] += 16
        self.eng[en].dma_start(out=out, in_=in_, **kw).then_inc(self.dsem[slot], 16)
        self.ninstr += 1
        tok = ('d', slot, self.dval[slot])
        self._update(tok, reads, writes)
        return tok

    def cast_dma(self, en, group, out, in_):
        if group not in self.csem:
            self.csem[group] = self.es.enter_context(self.nc.semaphore("csem_" + group))
            self.ccnt[group] = 0
        self.ccnt[group] += 1
        self.eng[en].dma_start(out=out, in_=in_, max_dma_last_dim=4096).then_inc(self.csem[group], 16)
        self.ninstr += 1

    def barrier(self):
        for en in self.eng:
            for a in self.sem:
                if a != en and self.count[a] > 0:
                    self._wait(en, ('e', a, self.count[a]))
            for s in range(self.NDS):
                if self.dval[s] > 0:
                    self._wait(en, ('d', s, self.dval[s]))
        self.res = {k: v for k, v in self.res.items() if k.startswith('WB_')}

    def final(self):
        for g in self.csem:
            self._wait('sp', ('c', g, 0))
        for s in range(self.NDS):
            if self.dval[s] > 0:
                self._wait('sp', ('d', s, self.dval[s]))
        for a in self.sem:
            if self.count[a] > 0:
                self._wait('sp', ('e', a, self.count[a]))


def _split(n, m):
    k = (n + m - 1) // m
    base = n // k
    rem = n - base * k
    out = []
    o = 0
    for i in range(k):
        s = base + (1 if i < rem else 0)
        out.append((o, s))
        o += s
    return out


def build(stage=99, dbg=None, phases=('hg', 'na', 'cv')):
    nc = bass.Bass("TRN2", target_bir_lowering=False)
    dbg = dbg or []
    _uid = [0]
    _sb, _pt = nc.sbuf_tensor, nc.psum_tensor

    def sbuf_u(name, shape, dt):
        _uid[0] += 1
        return _sb("%s_u%d" % (name, _uid[0]), shape, dt)

    def psum_u(name, shape, dt):
        _uid[0] += 1
        return _pt("%s_u%d" % (name, _uid[0]), shape, dt)
    outs = {}

    def dram_in(name, shape, dt=F32):
        return nc.dram_tensor(name, list(shape), dt, kind="ExternalInput").ap()

    def dram_tmp(name, shape, dt):
        kind = "ExternalOutput" if name in dbg else "Internal"
        return nc.dram_tensor(name, list(shape), dt, kind=kind).ap()

    xin = dram_in("xin", [T, D])
    vecs = dram_in("vecs", [128, NVEC])
    consts = dram_in("consts", [128, NCONST])
    lbl = dram_in("lbl", [64, 2 * DEPTH * 512])
    natab = dram_in("natab", [DEPTH, N_NA, 128, 14 * 64])
    w_ada = dram_in("w_ada", [DEPTH, D, 6 * D])
    w_in = dram_in("w_in", [DEPTH, D, IN_W])
    w_out = dram_in("w_out", [DEPTH, D, D])
    w_up = dram_in("w_up", [DEPTH, D, 2 * D_FF])
    w_down = dram_in("w_down", [DEPTH, D_FF, D])
    y = nc.dram_tensor("y", [SEQ, D], F32, kind="ExternalOutput").ap()

    wb_in = [dram_tmp("wb_in%d" % l, [D, IN_W], BF16) for l in range(DEPTH)]
    wb_out = [dram_tmp("wb_out%d" % l, [D, D], BF16) for l in range(DEPTH)]
    wb_up = [dram_tmp("wb_up%d" % l, [D, 2 * D_FF], BF16) for l in range(DEPTH)]
    wb_down = [dram_tmp("wb_down%d" % l, [D_FF, D], BF16) for l in range(DEPTH)]
    hF = dram_tmp("hF", [D, T], BF16)
    U_f = dram_tmp("U_f", [T, 1024], F32)
    U_i = dram_tmp("U_i", [T, 512], BF16)
    U_nv = dram_tmp("U_nv", [T + 64, 1024], BF16)
    U_F = dram_tmp("U_F", [IN_W, T], BF16)
    U_k = dram_tmp("U_k", [T, 1024], F32)
    U_qs = dram_tmp("U_qs", [512, T], F32)
    mixF = dram_tmp("mixF", [D, T], BF16)
    actF = dram_tmp("actF", [D_FF, T], BF16)
    X1 = dram_tmp("X1", [T, D], F32)
    X2 = dram_tmp("X2", [T, D], F32)

    es = contextlib.ExitStack()
    with es:
        S = Sched(nc, es)
        op, dma = S.op, S.dma

        vt = es.enter_context(sbuf_u("vt", [128, NVEC], F32))
        ct = es.enter_context(sbuf_u("ct", [128, NCONST], F32))
        ctb = es.enter_context(sbuf_u("ctb", [128, NCONST], BF16))
        adaT = es.enter_context(sbuf_u("adaT", [128, 96 * 2], F32))
        modv = es.enter_context(sbuf_u("modv", [128, 4 * 16 * 2], F32))
        grow = es.enter_context(sbuf_u("grow", [128, 4 * D], F32))
        silc = es.enter_context(sbuf_u("silc", [128, 32], F32))
        epst = es.enter_context(sbuf_u("epst", [128, 1], F32))
        onet = es.enter_context(sbuf_u("onet", [128, 1], F32))
        qkw = es.enter_context(sbuf_u("qkw", [128, 2], F32))
        lbt = es.enter_context(sbuf_u("lbt", [128, 2 * 512], F32))
        omt = es.enter_context(sbuf_u("omt", [128, 2 * 512], F32))
        dma('sp', vt[:], vecs[:, :], writes=['vt'])
        dma('sp', ct[:], consts[:, :], writes=['ct'])
        op('dve', lambda e: e.tensor_copy(out=ctb[:], in_=ct[:]), reads=['ct'], writes=['ctb'])
        op('dve', lambda e: e.memset(epst[:], EPS), writes=['epst'])
        op('dve', lambda e: e.memset(onet[:], 1.0), writes=['onet'])
        op('act', lambda e: e.activation(out=silc[:], in_=vt[:, V_CT:V_CT + 32], func=AF.Silu), reads=['vt'], writes=['silc'])
        ident = ct[:, C_ID:C_ID + 128]
        identb = ctb[:, C_ID:C_ID + 128]
        onesb = ctb[:, C_ONES:C_ONES + 128]
        ones = ct[:, C_ONES:C_ONES + 128]

        def cast_weight(src, dst, rows, key):
            step = 512
            for r0 in range(0, rows, step):
                r1 = min(rows, r0 + step)
                S.cast_dma('pool', key, dst[r0:r1, :], src[r0:r1, :])
            S.res['WB_' + key] = {'w': ('c', key, 0), 'r': []}
            return ['WB_' + key]

        wkeys = {}
        for l in range(DEPTH):
            for nm in ('in', 'out', 'up', 'down'):
                wkeys[nm, l] = ['WB_wb%s%d_' % (nm if nm != 'down' else 'dn', l)]
        cast_plan = {
            'start': [('in', 0), ('out', 0)],
            'mix0': [('up', 0), ('down', 0), ('in', 1), ('out', 1)],
            'mix1': [('up', 1), ('down', 1)],
        }

        def issue_casts(tag, idx=None):
            plan = cast_plan[tag] if idx is None else cast_plan[tag][idx:idx + 1]
            for (nm, l) in plan:
                src = {'in': w_in, 'out': w_out, 'up': w_up, 'down': w_down}[nm][l]
                dst = {'in': wb_in, 'out': wb_out, 'up': wb_up, 'down': wb_down}[nm][l]
                rows = D_FF if nm == 'down' else D
                cast_weight(src, dst, rows, "wb%s%d_" % (nm if nm != 'down' else 'dn', l))

        issue_casts('start')

        def phase_ada(l):
            with contextlib.ExitStack() as ps:
                wsl = [ps.enter_context(sbuf_u("adaw%d" % i, [128, 16 * 512], F32)) for i in range(2)]
                pacc = [ps.enter_context(psum_u("adap%d" % i, [128, 512], F32)) for i in range(2)]
                wv = w_ada[l].rearrange("(kc p) n -> p kc n", p=128)
                for sl in range(24):
                    wt = wsl[sl % 2]
                    wk = "adaw%d" % (sl % 2)
                    dma('sp', wt[:].rearrange("p (kc n) -> p kc n", kc=16), wv[:, :, sl * 512:(sl + 1) * 512], writes=[wk])
                    pa = pacc[sl % 2]
                    pk = "adap%d" % (sl % 2)
                    for c4 in range(4):
                        for kc in range(16):
                            op('pe', lambda e, c4=c4, kc=kc: e.matmul(
                                pa[:, c4 * 2:c4 * 2 + 2], lhsT=wt[:, kc * 512 + c4 * 128: kc * 512 + c4 * 128 + 128],
                                rhs=silc[:, kc * 2:kc * 2 + 2], start=(kc == 0), stop=(kc == 15)),
                               reads=[wk, 'silc'], writes=[pk], inc=(kc == 15))
                    for c4 in range(4):
                        j = sl * 4 + c4
                        op('dve', lambda e, c4=c4, j=j: e.tensor_scalar(
                            out=adaT[:, j * 2:j * 2 + 2], in0=pa[:, c4 * 2:c4 * 2 + 2],
                            scalar1=vt[:, V_BADA + l * 96 + j: V_BADA + l * 96 + j + 1], scalar2=None, op0=ALU.add),
                           reads=[pk, 'vt'], writes=['adaT'])
                for n, (lnoff, shc, scc) in enumerate([(V_LN1, 0, 16), (V_LN2, 48, 64)]):
                    for fc in range(16):
                        a_out = modv[:, (n * 2) * 32 + fc * 2:(n * 2) * 32 + fc * 2 + 2]
                        b_out = modv[:, (n * 2 + 1) * 32 + fc * 2:(n * 2 + 1) * 32 + fc * 2 + 2]
                        op('dve', lambda e, a_out=a_out, fc=fc, scc=scc, lnoff=lnoff: e.tensor_scalar(
                            out=a_out, in0=adaT[:, (scc + fc) * 2:(scc + fc) * 2 + 2], scalar1=1.0,
                            scalar2=vt[:, lnoff + l * 16 + fc:lnoff + l * 16 + fc + 1], op0=ALU.add, op1=ALU.mult),
                           reads=['adaT', 'vt'], writes=['modv'])
                        op('dve', lambda e, b_out=b_out, fc=fc, shc=shc: e.tensor_copy(
                            out=b_out, in_=adaT[:, (shc + fc) * 2:(shc + fc) * 2 + 2]),
                           reads=['adaT'], writes=['modv'])
                dg = ps.enter_context(sbuf_u("dg", [128, 2 * 128], F32))
                pg = [ps.enter_context(psum_u("pg%d" % i, [128, 512], F32)) for i in range(2)]
                cnt = 0
                for gi, gch in enumerate([32, 80]):
                    for which in range(2):
                        for f4 in range(4):
                            pgt = pg[cnt % 2]
                            pgk = "pg%d" % (cnt % 2)
                            for q in range(4):
                                fc = f4 * 4 + q
                                dsl = dg[:, (q % 2) * 128:(q % 2) * 128 + 128]
                                dk_ = "dg%d" % (q % 2)
                                col = (gch + fc) * 2 + which
                                op('dve', lambda e, dsl=dsl, col=col: e.tensor_scalar(
                                    out=dsl, in0=ident, scalar1=adaT[:, col:col + 1], scalar2=None, op0=ALU.mult),
                                   reads=['adaT', 'ct'], writes=[dk_])
                                op('pe', lambda e, dsl=dsl, q=q, pgt=pgt: e.matmul(
                                    pgt[:, q * 128:(q + 1) * 128], lhsT=ones, rhs=dsl, start=True, stop=True),
                                   reads=[dk_, 'ct'], writes=[pgk])
                            base = (gi * 2 + which) * D + f4 * 512
                            op('act', lambda e, base=base, pgt=pgt: e.activation(
                                out=grow[:, base:base + 512], in_=pgt[:], func=AF.Copy),
                               reads=[pgk], writes=['grow'])
                            cnt += 1
            S.barrier()

        def phase_norm(src, n, groups):
            with contextlib.ExitStack() as ps:
                xt = [ps.enter_context(sbuf_u("nx%d" % i, [128, D], F32)) for i in range(2)]
                sq = ps.enter_context(sbuf_u("nsq", [128, D], BF16))
                st = [ps.enter_context(sbuf_u("nst%d" % i, [128, 4], F32)) for i in range(2)]
                hs = [ps.enter_context(sbuf_u("nh%d" % i, [128, 16 * 512], BF16)) for i in range(2)]
                pt = [ps.enter_context(psum_u("npt%d" % i, [128, 512], F32)) for i in range(4)]
                hFv = hF.rearrange("(fc p) t -> p fc t", p=128)
                ti = 0
                pi = 0
                for gi, (t0, ntl, which) in enumerate(groups):
                    hst = hs[gi % 2]
                    hk = "nh%d" % (gi % 2)
                    hview = hst[:].rearrange("p (fc t) -> p fc t", fc=16)
                    for s in range(ntl):
                        x_ = xt[ti % 2]
                        xk = "nx%d" % (ti % 2)
                        s_ = st[ti % 2]
                        sk = "nst%d" % (ti % 2)
                        ti += 1
                        tt = t0 + s * 128
                        dma('sp', x_[:], src[tt:tt + 128, :], writes=[xk])
                        op('act', lambda e, x_=x_, s_=s_: e.activation(out=sq[:], in_=x_[:], func=AF.Square, accum_out=s_[:, 0:1]),
                           reads=[xk], writes=['nsq', sk])
                        op('act', lambda e, s_=s_: e.activation(out=s_[:, 1:2], in_=s_[:, 0:1], func=AF.Ln, bias=epst[:, 0:1], scale=1.0 / D),
                           reads=[sk, 'epst'], writes=[sk])
                        op('act', lambda e, s_=s_: e.activation(out=s_[:, 2:3], in_=s_[:, 1:2], func=AF.Exp, scale=-0.5), reads=[sk], writes=[sk])
                        op('dve', lambda e, x_=x_, s_=s_: e.tensor_scalar(out=x_[:], in0=x_[:], scalar1=s_[:, 2:3], scalar2=None, op0=ALU.mult),
                           reads=[xk, sk], writes=[xk])
                        for f4 in range(4):
                            p_ = pt[pi % 4]
                            pk = "npt%d" % (pi % 4)
                            pi += 1
                            for q in range(4):
                                fc = f4 * 4 + q
                                op('pe', lambda e, p_=p_, q=q, fc=fc, x_=x_: e.transpose(
                                    out=p_[:, q * 128:(q + 1) * 128], in_=x_[:, fc * 128:(fc + 1) * 128], identity=ident),
                                   reads=[xk, 'ct'], writes=[pk], inc=(q == 3))
                            for q in range(4):
                                fc = f4 * 4 + q
                                acol = (n * 2) * 32 + fc * 2 + which
                                bcol = (n * 2 + 1) * 32 + fc * 2 + which
                                eng = 'act' if q % 2 == 0 else 'dve'
                                if eng == 'act':
                                    op('act', lambda e, p_=p_, q=q, fc=fc, s=s, acol=acol, bcol=bcol: e.activation(
                                        out=hview[:, fc, s * 128:(s + 1) * 128], in_=p_[:, q * 128:(q + 1) * 128], func=AF.Identity,
                                        scale=modv[:, acol:acol + 1], bias=modv[:, bcol:bcol + 1]),
                                       reads=[pk, 'modv'], writes=[hk])
                                else:
                                    op('dve', lambda e, p_=p_, q=q, fc=fc, s=s, acol=acol, bcol=bcol: e.tensor_scalar(
                                        out=hview[:, fc, s * 128:(s + 1) * 128], in0=p_[:, q * 128:(q + 1) * 128],
                                        scalar1=modv[:, acol:acol + 1], scalar2=modv[:, bcol:bcol + 1], op0=ALU.mult, op1=ALU.add),
                                       reads=[pk, 'modv'], writes=[hk])
                    dma('pool', hFv[:, :, t0:t0 + ntl * 128], hview[:, :, 0:ntl * 128], reads=[hk], writes=['hF'])
            S.barrier()

        def gemm(A, K, Wb, wkeylist, col0, ncols, tblocks, mode, epi, halo=False, name="g", abufs=2, npp=8):
            KC = K // 128
            with contextlib.ExitStack() as ps:
                maxn = max(n for _, n in tblocks)
                ab = [ps.enter_context(sbuf_u("%sa%d" % (name, i), [128, KC * maxn], BF16)) for i in range(abufs)]
                wsb = [ps.enter_context(sbuf_u("%sw%d" % (name, i), [128, KC * 512], BF16)) for i in range(2)]
                pp = [ps.enter_context(psum_u("%sp%d" % (name, i), [128, 512], F32)) for i in range(npp)]
                ctx = epi('alloc', ps)
                Av = A.rearrange("(kc p) t -> p kc t", p=128)
                Wv = Wb.rearrange("(kc p) n -> p kc n", p=128)
                wi = 0
                pidx = 0
                def load_A(bi):
                    t0, n = tblocks[bi]
                    a_ = ab[bi % abufs]
                    ak = "%sa%d" % (name, bi % abufs)
                    av = a_[:, 0:KC * n].rearrange("p (kc t) -> p kc t", kc=KC)
                    for k0 in range(0, KC, 8):
                        k1 = min(KC, k0 + 8)
                        dma('sp', av[:, k0:k1, :], Av[:, k0:k1, t0:t0 + n], reads=['A_' + name], writes=[ak + "_%d" % k0])

                load_A(0)
                for bi, (t0, n) in enumerate(tblocks):
                    a_ = ab[bi % abufs]
                    ak = "%sa%d" % (name, bi % abufs)
                    if abufs == 1 and bi > 0:
                        load_A(bi)
                    av = a_[:, 0:KC * n].rearrange("p (kc t) -> p kc t", kc=KC)
                    akeys = [ak + "_%d" % k0 for k0 in range(0, KC, 8)]
                    for c0 in range(col0, col0 + ncols, 512):
                        w_ = wsb[wi % 2]
                        wk = "%sw%d" % (name, wi % 2)
                        wi += 1
                        wv = w_[:].rearrange("p (kc n) -> p kc n", kc=KC)
                        for k0 in range(0, KC, 8):
                            k1 = min(KC, k0 + 8)
                            dma('sp', wv[:, k0:k1, :], Wv[:, k0:k1, c0:c0 + 512], reads=wkeylist, writes=[wk + "_%d" % k0])
                        wks = [wk + "_%d" % k0 for k0 in range(0, KC, 8)]
                        if abufs > 1 and c0 == col0 and bi + 1 < len(tblocks):
                            load_A(bi + 1)
                        if mode == 'F':
                            pieces = _split(n, 512)
                            for c4 in range(4):
                                banks = []
                                for (o, sz) in pieces:
                                    p_ = pp[pidx % npp]
                                    pk = "%sp%d" % (name, pidx % npp)
                                    pidx += 1
                                    for kc in range(KC):
                                        op('pe', lambda e, p_=p_, sz=sz, o=o, kc=kc, c4=c4, wv=wv, av=av: e.matmul(
                                            p_[:, 0:sz], lhsT=wv[:, kc, c4 * 128:(c4 + 1) * 128], rhs=av[:, kc, o:o + sz],
                                            start=(kc == 0), stop=(kc == KC - 1)),
                                           reads=[wks[kc // 8], akeys[kc // 8]], writes=[pk], inc=(kc == KC - 1))
                                    banks.append((p_, pk, o, sz))
                                epi('F', ctx, banks, c0 + c4 * 128, t0, n)
                        else:
                            for s in range(n // 128):
                                p_ = pp[pidx % npp]
                                pk = "%sp%d" % (name, pidx % npp)
                                pidx += 1
                                for kc in range(KC):
                                    op('pe', lambda e, p_=p_, kc=kc, s=s, wv=wv, av=av: e.matmul(
                                        p_[:, :], lhsT=av[:, kc, s * 128:(s + 1) * 128], rhs=wv[:, kc, :],
                                        start=(kc == 0), stop=(kc == KC - 1)),
                                       reads=[wks[kc // 8], akeys[kc // 8]], writes=[pk], inc=(kc == KC - 1))
                                epi('T', ctx, (p_, pk), c0, t0 + s * 128, 128)
                epi('end', ctx)
            S.barrier()

        def lb_setup(l):
            op('dve', lambda e: e.tensor_scalar(out=qkw[:, 0:1], in0=vt[:, V_QN + l:V_QN + l + 1], scalar1=float(HD ** -0.5), scalar2=None, op0=ALU.mult),
               reads=['vt'], writes=['qkw'])
            op('dve', lambda e: e.tensor_copy(out=qkw[:, 1:2], in_=vt[:, V_KN + l:V_KN + l + 1]), reads=['vt'], writes=['qkw'])
            _lb_setup(l)

        def _lb_setup(l):
            if l == 0:
                op('dve', lambda e: e.memset(lbt[:], 0.0), writes=['lbt'])
                op('dve', lambda e: e.memset(omt[:], 1.0), writes=['omt'])
            else:
                with contextlib.ExitStack() as ps:
                    lg = ps.enter_context(sbuf_u("lg", [128, 2 * DEPTH * 512], F32))
                    dma('sp', lg[0:64, :], lbl[:, :], writes=['lg'])
                    dma('sp', lg[64:128, :], lbl[:, :], writes=['lg2'])
                    for dr in range(2):
                        op('dve', lambda e, dr=dr: e.tensor_tensor(out=lbt[:, dr * 512:(dr + 1) * 512], in0=lg[:, (dr * 2) * 512:(dr * 2 + 1) * 512],
                                                                 in1=lg[:, (dr * 2 + 1) * 512:(dr * 2 + 2) * 512], op=ALU.subtract), reads=['lg', 'lg2'], writes=['lbt'])
                    op('act', lambda e: e.activation(out=lbt[:], in_=lbt[:], func=AF.Exp), reads=['lbt'], writes=['lbt'])
                    op('dve', lambda e: e.tensor_scalar(out=lbt[:], in0=lbt[:], scalar1=1.0, scalar2=None, op0=ALU.add), reads=['lbt'], writes=['lbt'])
                    op('dve', lambda e: e.reciprocal(out=lbt[:], in_=lbt[:]), reads=['lbt'], writes=['lbt'])
                    op('dve', lambda e: e.tensor_scalar(out=omt[:], in0=lbt[:], scalar1=-1.0, scalar2=1.0, op0=ALU.mult, op1=ALU.add),
                       reads=['lbt'], writes=['omt'])
                    S.barrier()

        def epi_inproj_qk(kind, *a):
            if kind == 'alloc':
                ps = a[0]
                c = {'i': 0, 'pend': None, 'si': 0}
                c['raw'] = [ps.enter_context(sbuf_u("qkraw%d" % i, [128, 512], F32)) for i in range(3)]
                c['sq'] = [ps.enter_context(sbuf_u("qksq%d" % i, [128, 512], BF16)) for i in range(3)]
                c['rs'] = [ps.enter_context(sbuf_u("qkrs%d" % i, [128, 512], F32)) for i in range(3)]
                c['sf'] = [ps.enter_context(sbuf_u("qksf%d" % i, [128, 1024], BF16)) for i in range(2)]
                c['pn'] = ps.enter_context(psum_u("qkpn", [128, 512], F32))
                return c

            def flush(c):
                job = c['pend']
                if job is None:
                    return
                c['pend'] = None
                (i, sz, o, which, sf, sfk, last, col, t0, n) = job
                raw, sq, rs = c['raw'][i], c['sq'][i], c['rs'][i]
                rawk, sqk, rsk = "qkraw%d" % i, "qksq%d" % i, "qkrs%d" % i
                pn = c['pn']
                op('pe', lambda e: e.matmul(pn[:, 0:sz], lhsT=onesb, rhs=sq[:, 0:sz], start=True, stop=True), reads=[sqk, 'ctb'], writes=['qkpn'])
                op('act', lambda e: e.activation(out=rs[:, 0:sz], in_=pn[:, 0:sz], func=AF.Ln, bias=epst[:, 0:1], scale=1.0 / 128), reads=['qkpn', 'epst'], writes=[rsk])
                op('act', lambda e: e.activation(out=rs[:, 0:sz], in_=rs[:, 0:sz], func=AF.Exp, scale=-0.5), reads=[rsk], writes=[rsk])
                op('dve', lambda e: e.scalar_tensor_tensor(out=sf[:, o:o + sz], in0=raw[:, 0:sz], scalar=qkw[:, which:which + 1], in1=rs[:, 0:sz],
                                                         op0=ALU.mult, op1=ALU.mult), reads=[rawk, rsk, 'qkw'], writes=[sfk])
                if last:
                    dma('pool', U_F[col:col + 128, t0:t0 + n], sf[:, 0:n], reads=[sfk], writes=['U_F'])

            if kind == 'end':
                flush(a[0])
                return
            c, banks, col, t0, n = a
            which = 0 if col >= OFF_NQ else 1
            si = c['si'] % 2
            c['si'] += 1
            sf = c['sf'][si]
            sfk = "qksf%d" % si
            for bi_, (p_, pk, o, sz) in enumerate(banks):
                i = c['i'] % 3
                c['i'] += 1
                raw, sq = c['raw'][i], c['sq'][i]
                rawk, sqk = "qkraw%d" % i, "qksq%d" % i
                op('act', lambda e, p_=p_, sz=sz, sq=sq: e.activation(out=sq[:, 0:sz], in_=p_[:, 0:sz], func=AF.Square), reads=[pk], writes=[sqk])
                op('dve', lambda e, p_=p_, sz=sz, raw=raw: e.tensor_copy(out=raw[:, 0:sz], in_=p_[:, 0:sz]), reads=[pk, sqk], writes=[rawk])
                flush(c)
                c['pend'] = (i, sz, o, which, sf, sfk, bi_ == len(banks) - 1, col, t0, n)

        def epi_inproj(kind, *a):
            if kind == 'end':
                return
            if kind == 'alloc':
                ps = a[0]
                c = {}
                c['sf'] = [ps.enter_context(sbuf_u("ipf%d" % i, [128, 1024], BF16)) for i in range(3)]
                c['st'] = [ps.enter_context(sbuf_u("ipt%d" % i, [128, 512], F32)) for i in range(3)]
                c['stb'] = [ps.enter_context(sbuf_u("iptb%d" % i, [128, 512], BF16)) for i in range(3)]
                c['st2'] = [ps.enter_context(sbuf_u("ipt2_%d" % i, [128, 512], F32)) for i in range(3)]
                c['sq'] = [ps.enter_context(sbuf_u("ipq%d" % i, [128, 1024], F32)) for i in range(2)]
                c['sq2'] = [ps.enter_context(sbuf_u("ipq2_%d" % i, [128, 1024], F32)) for i in range(2)]
                c['qi'] = 0
                c['i'] = 0
                return c
            if kind == 'F' and OFF_HQ <= a[2] < OFF_HQ + 512:
                c, banks, col, t0, n = a
                i = c['qi'] % 2
                c['qi'] += 1
                sq, sq2 = c['sq'][i], c['sq2'][i]
                sqk, sq2k = "ipq%d" % i, "ipq2_%d" % i
                for (p_, pk, o, sz) in banks:
                    op('act', lambda e, p_=p_, o=o, sz=sz: e.activation(out=sq[:, o:o + sz], in_=p_[:, 0:sz], func=AF.Exp, scale=-1.0), reads=[pk], writes=[sqk])
                    op('act', lambda e, o=o, sz=sz: e.activation(out=sq[:, o:o + sz], in_=sq[:, o:o + sz], func=AF.Ln, bias=onet[:, 0:1]), reads=[sqk, 'onet'], writes=[sqk])
                    op('act', lambda e, o=o, sz=sz: e.activation(out=sq[:, o:o + sz], in_=sq[:, o:o + sz], func=AF.Exp, scale=-1.0), reads=[sqk], writes=[sqk])
                    op('dve', lambda e, p_=p_, o=o, sz=sz: e.tensor_tensor(out=sq2[:, o:o + sz], in0=p_[:, 0:sz], in1=sq[:, o:o + sz], op=ALU.mult),
                       reads=[pk, sqk], writes=[sq2k])
                dma('pool', U_qs[col - OFF_HQ:col - OFF_HQ + 128, t0:t0 + n], sq2[:, 0:n], reads=[sq2k], writes=['U_qs'])
            elif kind == 'F':
                c, banks, col, t0, n = a
                i = c['i'] % 3
                c['i'] += 1
                sf = c['sf'][i]
                sk = "ipf%d" % i
                for bi_, (p_, pk, o, sz) in enumerate(banks):
                    if bi_ % 2 == 0:
                        op('act', lambda e, p_=p_, o=o, sz=sz: e.activation(out=sf[:, o:o + sz], in_=p_[:, 0:sz], func=AF.Copy),
                           reads=[pk], writes=[sk])
                    else:
                        op('dve', lambda e, p_=p_, o=o, sz=sz: e.tensor_copy(out=sf[:, o:o + sz], in_=p_[:, 0:sz]),
                           reads=[pk], writes=[sk])
                dma('pool', U_F[col:col + 128, t0:t0 + n], sf[:, 0:n], reads=[sk], writes=['U_F'])
            else:
                c, (p_, pk), c0, tt, _ = a
                i = c['i'] % 3
                c['i'] += 1
                if c0 < 1024:
                    s_ = c['st'][i]
                    s2 = c['st2'][i]
                    sk = "ipt%d" % i
                    s2k = "ipt2_%d" % i
                    lbs = lbt[:, c0:c0 + 512]
                    oms = omt[:, c0:c0 + 512]
                    op('act', lambda e: e.activation(out=s_[:], in_=p_[:], func=AF.Exp, scale=-1.0), reads=[pk], writes=[sk])
                    op('act', lambda e: e.activation(out=s_[:], in_=s_[:], func=AF.Ln, bias=onet[:, 0:1]), reads=[sk, 'onet'], writes=[sk])
                    op('act', lambda e: e.activation(out=s_[:], in_=s_[:], func=AF.Exp, scale=-1.0), reads=[sk], writes=[sk])
                    op('dve', lambda e: e.tensor_tensor(out=s2[:], in0=s_[:], in1=oms, op=ALU.mult), reads=[sk, 'omt'], writes=[s2k])
                    op('dve', lambda e: e.scalar_tensor_tensor(out=s_[:], in0=s2[:], scalar=1e-30, in1=lbs, op0=ALU.max, op1=ALU.add),
                       reads=[s2k, 'lbt'], writes=[sk])
                    op('act', lambda e: e.activation(out=s_[:], in_=s_[:], func=AF.Ln), reads=[sk], writes=[sk])
                    op('dve', lambda e: e.tensor_tensor(out=s2[:], in0=oms, in1=s2[:], op=ALU.subtract), reads=[s2k, 'omt'], writes=[s2k])
                    dma('pool', U_f[tt:tt + 128, c0:c0 + 512], s_[:], reads=[sk], writes=['U_f'])
                    dma('pool', U_k[tt:tt + 128, c0:c0 + 512], s2[:], reads=[s2k], writes=['U_k'])
                else:
                    s_ = c['stb'][i]
                    sk = "iptb%d" % i
                    op('dve', lambda e: e.tensor_copy(out=s_[:], in_=p_[:]), reads=[pk], writes=[sk])
                    if c0 == OFF_HI:
                        dma('pool', U_i[tt:tt + 128, :], s_[:], reads=[sk], writes=['U_i'])
                    else:
                        dma('pool', U_nv[tt:tt + 128, c0 - OFF_NV:c0 - OFF_NV + 512], s_[:], reads=[sk], writes=['U_nv'])


        def block_norm_store(ps_ctx, src_ap, n, wcol, dst_rows, t0, extra_mul=None, keys=()):
            c = ps_ctx
            i = c['i'] % 2
            c['i'] += 1
            sq, rs, ob, pn = c['sq'][i], c['rs'][i], c['ob'][i], c['pn'][i]
            sqk, rsk, obk, pnk = "bn_sq%d" % i, "bn_rs%d" % i, "bn_ob%d" % i, "bn_pn%d" % i
            op('act', lambda e: e.activation(out=sq[:, 0:n], in_=src_ap, func=AF.Square), reads=list(keys), writes=[sqk])
            op('pe', lambda e: e.matmul(pn[:, 0:n], lhsT=onesb, rhs=sq[:, 0:n], start=True, stop=True), reads=[sqk, 'ctb'], writes=[pnk])
            op('act', lambda e: e.activation(out=rs[:, 0:n], in_=pn[:, 0:n], func=AF.Ln, bias=epst[:, 0:1], scale=1.0 / 128),
               reads=[pnk, 'epst'], writes=[rsk])
            op('act', lambda e: e.activation(out=rs[:, 0:n], in_=rs[:, 0:n], func=AF.Exp, scale=-0.5), reads=[rsk], writes=[rsk])
            if extra_mul is None:
                op('dve', lambda e: e.scalar_tensor_tensor(out=ob[:, 0:n], in0=src_ap, scalar=vt[:, wcol:wcol + 1], in1=rs[:, 0:n],
                                                         op0=ALU.mult, op1=ALU.mult), reads=list(keys) + [rsk, 'vt'], writes=[obk])
            else:
                op('dve', lambda e: e.scalar_tensor_tensor(out=rs[:, 0:n], in0=src_ap, scalar=vt[:, wcol:wcol + 1], in1=rs[:, 0:n],
                                                         op0=ALU.mult, op1=ALU.mult), reads=list(keys) + [rsk, 'vt'], writes=[rsk])
                emul, ekeys = extra_mul
                op('dve', lambda e: e.tensor_tensor(out=ob[:, 0:n], in0=rs[:, 0:n], in1=emul, op=ALU.mult),
                   reads=[rsk] + list(ekeys), writes=[obk])
            dma('sp', mixF[dst_rows:dst_rows + 128, t0:t0 + n], ob[:, 0:n], reads=[obk], writes=['mixF'])

        def bn_alloc(ps):
            c = {'i': 0}
            c['sq'] = [ps.enter_context(sbuf_u("bnsq%d" % i, [128, 512], BF16)) for i in range(2)]
            c['rs'] = [ps.enter_context(sbuf_u("bnrs%d" % i, [128, 512], F32)) for i in range(2)]
            c['ob'] = [ps.enter_context(sbuf_u("bnob%d" % i, [128, 512], BF16)) for i in range(2)]
            c['pn'] = [ps.enter_context(psum_u("bnpn%d" % i, [128, 512], F32)) for i in range(2)]
            return c

        def phase_hgrn(l):
            NSC = 4
            W_ = NSC * 128
            NQ = NSC * 64
            with contextlib.ExitStack() as ps:
                bn = bn_alloc(ps)
                Obs = [ps.enter_context(sbuf_u("hgO%d" % i, [128, T], F32)) for i in range(2)]
                St = [ps.enter_context(sbuf_u("hgS%d" % i, [128, 128], F32)) for i in range(2)]
                gsb = [ps.enter_context(sbuf_u("hggs%d" % i, [128, 512], BF16)) for i in range(2)]
                gsf = [ps.enter_context(sbuf_u("hggf%d" % i, [128, 512], F32)) for i in range(2)]
                NBUF = 2
                names = [('A', [128, W_], F32), ('B', [128, W_], F32), ('C', [128, W_], F32), ('v', [128, W_], BF16), ('q', [128, NQ], BF16),
                         ('qe', [128, NQ], F32), ('qs', [128, NQ], F32), ('E', [128, NSC * 66], F32), ('En', [128, W_], F32), ('kt', [128, W_], BF16),
                         ('qt', [128, NQ], BF16), ('ktF', [128, NQ], BF16), ('amf', [64, NQ], F32), ('am', [64, NQ], BF16)]
                BUF = [{nm: ps.enter_context(sbuf_u("hg%s%d" % (nm, i), shp, dt)) for (nm, shp, dt) in names} for i in range(NBUF)]
                Sp = [ps.enter_context(sbuf_u("hgSp%d" % i, [128, 128], BF16)) for i in range(2)]
                pA = ps.enter_context(psum_u("hgpA", [128, 512], F32))
                pBB = ps.enter_context(psum_u("hgpBB", [128, 512], F32))
                pT = ps.enter_context(psum_u("hgpT", [128, 1024], BF16))
                pAtt = ps.enter_context(psum_u("hgpAtt", [128, 512], F32))
                pKv = ps.enter_context(psum_u("hgpKv", [128, 512], F32))
                pO = ps.enter_context(psum_u("hgpO", [128, 512], F32))
                nlat = SEQ // (NSC * 64)
                items = []
                for hd in range(4):
                    for dr in range(2):
                        scs = [0] + [CTX + i * NSC * 64 for i in (range(nlat) if dr == 0 else reversed(range(nlat)))]
                        for j, t0 in enumerate(scs):
                            items.append((hd, dr, j, t0, len(scs)))
                n = NSC * 64
                state = {'sidx': 0, 'spi': 0}

                def bufs(i):
                    b2 = i % NBUF
                    Bf = BUF[b2]
                    return [Bf[nm] for (nm, _, _) in names], ["hg%s%d" % (nm, b2) for (nm, _, _) in names]

                def prep(i, part):
                    hd, dr, j, t0, _ = items[i]
                    (A_, B_, C_, v_, q_, qe_, qs_, E_, En_, kt_, qt_, ktF_, amf_, am_), (Ak, Bk, Ck, vk, qk, qek, qsk, Ek, Enk, ktk, qtk, ktFk, amfk, amk) = bufs(i)
                    if part != 1:
                        return
                    fcol = dr * 512 + hd * 128
                    dma('sp', A_[0:64, :].rearrange("p (c d) -> p c d", c=NSC),
                        U_f[t0:t0 + n, fcol:fcol + 128].rearrange("(c p) d -> p c d", p=64), writes=[Ak + "_0"])
                    for half in range(2):
                        dma('sp', C_[half * 64:(half + 1) * 64, :].rearrange("p (c d) -> p c d", c=NSC),
                            U_k[t0:t0 + n, fcol:fcol + 128].rearrange("(c p) d -> p c d", p=64), writes=[Ck + "_%d" % half])
                        dma('sp', v_[half * 64:(half + 1) * 64, :].rearrange("p (c d) -> p c d", c=NSC),
                            U_i[t0:t0 + n, hd * 128:hd * 128 + 128].rearrange("(c p) d -> p c d", p=64), writes=[vk + "_%d" % half])
                    dma('sp', qs_[:], U_qs[hd * 128:hd * 128 + 128, t0:t0 + n], writes=[qsk])

                def partAB(i, stage_):
                    hd, dr, j, t0, nsc_chain = items[i]
                    Ob = Obs[hd % 2]
                    (A_, B_, C_, v_, q_, qe_, qs_, E_, En_, kt_, qt_, ktF_, amf_, am_), (Ak, Bk, Ck, vk, qk, qek, qsk, Ek, Enk, ktk, qtk, ktFk, amfk, amk) = bufs(i)
                    Aks = [Ak + "_0"]
                    Cks = [Ck + "_0", Ck + "_1"]
                    vks = [vk + "_0", vk + "_1"]
                    mext = ct[0:64, (C_MEXT_F if dr == 0 else C_MEXT_B):(C_MEXT_F if dr == 0 else C_MEXT_B) + 66]
                    mcc = ct[0:64, (C_MCC_F if dr == 0 else C_MCC_B):(C_MCC_F if dr == 0 else C_MCC_B) + 128]
                    mkr = ct[0:64, (C_MKR_F if dr == 0 else C_MKR_B):(C_MKR_F if dr == 0 else C_MKR_B) + NQ]
                    def stA1():
                        for c in range(NSC):
                            op('pe', lambda e, c=c: e.matmul(pA[:, c * 66:(c + 1) * 66], lhsT=A_[0:64, c * 128:(c + 1) * 128], rhs=mext, start=True, stop=True),
                               reads=Aks + ['ct'], writes=['hgpA'], inc=(c == NSC - 1))
                        for c in range(NSC):
                            op('pe', lambda e, c=c: e.matmul(pBB[:, c * 128:(c + 1) * 128], lhsT=mcc, rhs=A_[0:64, c * 128:(c + 1) * 128], start=True, stop=True),
                               reads=Aks + ['ct'], writes=['hgpBB'], inc=(c == NSC - 1))

                    def stA1act():
                        op('act', lambda e: e.activation(out=E_[:], in_=pA[:, 0:NSC * 66], func=AF.Exp), reads=['hgpA'], writes=[Ek])
                        op('act', lambda e: e.activation(out=En_[:], in_=pBB[:, :], func=AF.Exp, scale=-1.0), reads=['hgpBB'], writes=[Enk])

                    def stA2():
                        op('dve', lambda e: e.tensor_tensor(out=kt_[:], in0=C_[:], in1=En_[:], op=ALU.mult), reads=Cks + [Enk], writes=[ktk])
                        op('dve', lambda e: e.tensor_tensor(out=qt_[:].rearrange("p (c w) -> p c w", c=NSC), in0=qs_[:].rearrange("p (c w) -> p c w", c=NSC),
                                                           in1=E_[:].rearrange("p (c w) -> p c w", c=NSC)[:, :, 0:64], op=ALU.mult), reads=[qsk, Ek], writes=[qtk])
                        for c in range(NSC):
                            op('pe', lambda e, c=c: e.transpose(out=pT[:, c * 64:(c + 1) * 64], in_=kt_[0:64, c * 128:(c + 1) * 128], identity=identb[0:64, 0:64]),
                               reads=[ktk, 'ctb'], writes=['hgpT'], inc=(c == NSC - 1))
                        op('act', lambda e: e.activation(out=ktF_[:], in_=pT[:, 0:NQ], func=AF.Copy), reads=['hgpT'], writes=[ktFk])

                    def stA3():
                        for c in range(NSC):
                            op('pe', lambda e, c=c: e.matmul(pAtt[0:64, c * 64:(c + 1) * 64], lhsT=ktF_[:, c * 64:(c + 1) * 64], rhs=qt_[:, c * 64:(c + 1) * 64],
                                                            start=True, stop=True), reads=[ktFk, qtk], writes=['hgpAtt'], inc=(c == NSC - 1))
                        op('dve', lambda e: e.tensor_scalar(out=amf_[:], in0=pAtt[0:64, 0:NQ], scalar1=1e30, scalar2=-1e30, op0=ALU.min, op1=ALU.max),
                           reads=['hgpAtt'], writes=[amfk])
                        op('dve', lambda e: e.tensor_tensor(out=am_[:], in0=amf_[:], in1=mkr, op=ALU.mult), reads=[amfk, 'ct'], writes=[amk])
                        for c in range(NSC):
                            op('pe', lambda e, c=c: e.matmul(pKv[:, c * 128:(c + 1) * 128], lhsT=kt_[64:128, c * 128:(c + 1) * 128], rhs=v_[64:128, c * 128:(c + 1) * 128],
                                                            start=True, stop=True), reads=[ktk] + vks, writes=['hgpKv'], inc=(c == NSC - 1))

                    def stB():
                        if j == 0:
                            state['sidx'] = 0
                            op('dve', lambda e: e.memset(St[0][:], 0.0), writes=['hgS0'])
                        order = list(range(NSC)) if dr == 0 else list(reversed(range(NSC)))
                        for c in order:
                            sidx = state['sidx']
                            Sold, Snew = St[sidx % 2], St[(sidx + 1) % 2]
                            Soldk, Snewk = "hgS%d" % (sidx % 2), "hgS%d" % ((sidx + 1) % 2)
                            state['sidx'] += 1
                            Sp_ = Sp[state['spi'] % 2]
                            Spk = "hgSp%d" % (state['spi'] % 2)
                            state['spi'] += 1
                            op('act', lambda e, c=c, Sp_=Sp_, Sold=Sold: e.activation(out=Sp_[:], in_=Sold[:], func=AF.Identity, scale=E_[:, c * 66 + 65:c * 66 + 66]),
                               reads=[Soldk, Ek], writes=[Spk])
                            op('pe', lambda e, c=c, Sp_=Sp_: e.matmul(pO[:, c * 64:(c + 1) * 64], lhsT=Sp_[:], rhs=qt_[:, c * 64:(c + 1) * 64], start=True, stop=False),
                               reads=[Spk, qtk], writes=['hgpO'], inc=False)
                            op('pe', lambda e, c=c: e.matmul(pO[:, c * 64:(c + 1) * 64], lhsT=v_[0:64, c * 128:(c + 1) * 128], rhs=am_[:, c * 64:(c + 1) * 64], start=False, stop=True),
                               reads=vks + [amk], writes=['hgpO'])
                            op('dve', lambda e, c=c, Sold=Sold, Snew=Snew: e.scalar_tensor_tensor(
                                out=Snew[:], in0=Sold[:], scalar=E_[:, c * 66 + 64:c * 66 + 65], in1=pKv[:, c * 128:(c + 1) * 128], op0=ALU.mult, op1=ALU.add),
                               reads=[Soldk, Ek, 'hgpKv'], writes=[Snewk])
                        ok_ = "hgO%d_%d" % (hd % 2, t0 // 256)
                        if dr == 0:
                            op('act', lambda e: e.activation(out=Ob[:, t0:t0 + n], in_=pO[:, 0:n], func=AF.Copy), reads=['hgpO'], writes=[ok_])
                        else:
                            op('dve', lambda e: e.tensor_tensor(out=Ob[:, t0:t0 + n], in0=pO[:, 0:n], in1=Ob[:, t0:t0 + n], op=ALU.add),
                               reads=['hgpO', ok_], writes=[ok_])

                    {0: stA1, 1: stA2, 2: stA3, 3: stB, 4: stA1act}[stage_]()

                def readout(hd):
                    Ob = Obs[hd % 2]
                    for bi_, (t0, nb) in enumerate([(0, 256)] + [(256 + i * 512, 512) for i in range(8)]):
                        b2 = bi_ % 2
                        gk_, gfk_ = "hggs%d" % b2, "hggf%d" % b2
                        dma('sp', gsb[b2][:, 0:nb], U_F[OFF_HGG + hd * 128:OFF_HGG + hd * 128 + 128, t0:t0 + nb], writes=[gk_])
                        op('act', lambda e, b2=b2, nb=nb: e.activation(out=gsf[b2][:, 0:nb], in_=gsb[b2][:, 0:nb], func=AF.Exp, scale=-1.0), reads=[gk_], writes=[gfk_])
                        op('act', lambda e, b2=b2, nb=nb: e.activation(out=gsf[b2][:, 0:nb], in_=gsf[b2][:, 0:nb], func=AF.Ln, bias=onet[:, 0:1]), reads=[gfk_, 'onet'], writes=[gfk_])
                        op('act', lambda e, b2=b2, nb=nb: e.activation(out=gsf[b2][:, 0:nb], in_=gsf[b2][:, 0:nb], func=AF.Exp, scale=-1.0), reads=[gfk_], writes=[gfk_])
                        op('dve', lambda e, b2=b2, nb=nb: e.tensor_tensor(out=gsf[b2][:, 0:nb], in0=gsf[b2][:, 0:nb], in1=gsb[b2][:, 0:nb], op=ALU.mult), reads=[gfk_, gk_], writes=[gfk_])
                        okeys = ["hgO%d_%d" % (hd % 2, jj) for jj in range(t0 // 256, (t0 + nb) // 256)]
                        block_norm_store(bn, Ob[:, t0:t0 + nb], nb, V_HGN + l, hd * 128, t0, extra_mul=(gsf[b2][:, 0:nb], [gfk_]), keys=okeys)

                prep(0, 1)
                prep(0, 2)
                partAB(0, 0)
                for i in range(len(items)):
                    nxt = i + 1 < len(items)
                    partAB(i, 4)
                    partAB(i, 1)
                    if nxt:
                        prep(i + 1, 1)
                    partAB(i, 2)
                    if nxt:
                        prep(i + 1, 2)
                        partAB(i + 1, 0)
                    partAB(i, 3)
                    hd, dr, j, t0, nsc_chain = items[i]
                    if dr == 1 and j == nsc_chain - 1:
                        readout(hd)
            S.barrier()

        def phase_na(l, skip_ctx_out=False, cast_tag=None):
            with contextlib.ExitStack() as ps:
                bn = bn_alloc(ps)
                qn = ps.enter_context(sbuf_u("naq", [128, T], BF16))
                kn = ps.enter_context(sbuf_u("nak", [128, T], BF16))
                raw = [ps.enter_context(sbuf_u("naraw%d" % i, [128, 512], BF16)) for i in range(2)]
                nsq = [ps.enter_context(sbuf_u("nansq%d" % i, [128, 512], BF16)) for i in range(2)]
                nrs = [ps.enter_context(sbuf_u("nanrs%d" % i, [128, 512], F32)) for i in range(2)]
                v0 = ps.enter_context(sbuf_u("nav0", [128, NT * 128], BF16))
                v1 = ps.enter_context(sbuf_u("nav1", [128, 31 * 128], BF16))
                nbf = ps.enter_context(sbuf_u("nanbf", [128, 14 * 64], F32))
                nbb = ps.enter_context(sbuf_u("nanbb", [128, 14 * 64], BF16))
                wsc = ps.enter_context(sbuf_u("nawsc", [128, 2], F32))
                PT = [ps.enter_context(sbuf_u("naPT%d" % i, [128, 512], BF16)) for i in range(2)]
                rd = [ps.enter_context(sbuf_u("nard%d" % i, [128, 256], F32)) for i in range(2)]
                ob = [ps.enter_context(sbuf_u("naob%d" % i, [128, 512], F32)) for i in range(2)]
                pS = [ps.enter_context(psum_u("napS%d" % i, [128, 512], F32)) for i in range(2)]
                pO = [ps.enter_context(psum_u("napO%d" % i, [128, 512], F32)) for i in range(2)]
                pD = [ps.enter_context(psum_u("napD%d" % i, [128, 512], F32)) for i in range(2)]
                pN = bn['pn'][0]
                op('dve', lambda e: e.tensor_scalar(out=wsc[:, 0:1], in0=vt[:, V_QN + l:V_QN + l + 1], scalar1=float(HD ** -0.5), scalar2=None, op0=ALU.mult),
                   reads=['vt'], writes=['nawsc'])
                op('dve', lambda e: e.tensor_copy(out=wsc[:, 1:2], in_=vt[:, V_KN + l:V_KN + l + 1]), reads=['vt'], writes=['nawsc'])
                blocks = [(0, 256)] + [(256 + i * 512, 512) for i in range(8)]
                ri = 0
                rowi = 0
                for hd in range(N_NA):
                    if cast_tag is not None and hd % 2 == 0 and hd // 2 < len(cast_plan[cast_tag]):
                        issue_casts(cast_tag, hd // 2)
                    dma('sp', v0[:].rearrange("p (j d) -> p j d", j=NT), U_nv[0:T, hd * 128:hd * 128 + 128].rearrange("(j p) d -> p j d", p=128), writes=['nav0'])
                    dma('sp', v1[:].rearrange("p (j d) -> p j d", j=31),
                        U_nv[CTX + 64:CTX + 64 + 31 * 128, hd * 128:hd * 128 + 128].rearrange("(j p) d -> p j d", p=128), writes=['nav1'])
                    dma('sp', nbf[:], natab[l, hd], writes=['nanbf'])
                    op('dve', lambda e: e.tensor_copy(out=nbb[:], in_=nbf[:]), reads=['nanbf'], writes=['nanbb'])
                    dma('sp', qn[:], U_F[OFF_NQ + hd * 128:OFF_NQ + hd * 128 + 128, :], writes=['naq'])
                    dma('sp', kn[:], U_F[OFF_NK + hd * 128:OFF_NK + hd * 128 + 128, :], writes=['nak'])

                    def att_scores(job):
                        nonlocal rowi
                        qlo, nq, tiles = job['qlo'], job['nq'], job['tiles']
                        b2 = rowi % 2
                        rowi += 1
                        job['b2'] = b2
                        pS_ = pS[b2]
                        pSk = "napS%d" % b2
                        nt_ = len(tiles)
                        for i, (klo, vap, bap) in enumerate(tiles):
                            last = (i == nt_ - 1)
                            op('pe', lambda e, i=i, klo=klo, bap=bap: e.matmul(pS_[:, i * nq:(i + 1) * nq], lhsT=kn[:, klo:klo + 128], rhs=qn[:, qlo:qlo + nq],
                                                                              start=True, stop=(bap is None)), reads=['nak', 'naq'], writes=[pSk], inc=(bap is None and last))
                            if bap is not None:
                                op('pe', lambda e, i=i, bap=bap: e.matmul(pS_[:, i * nq:(i + 1) * nq], lhsT=identb, rhs=bap, start=False, stop=True),
                                   reads=['nanbb', 'ctb'], writes=[pSk], inc=last)

                    def att_finish(job):
                        qlo, nq, tiles, obuf, ocol, obk, b2 = job['qlo'], job['nq'], job['tiles'], job['obuf'], job['ocol'], job['obk'], job['b2']
                        pS_, pO_, pD_, PT_, rd_ = pS[b2], pO[b2], pD[b2], PT[b2], rd[b2]
                        pSk, pOk, pDk, PTk, rdk = "napS%d" % b2, "napO%d" % b2, "napD%d" % b2, "naPT%d" % b2, "nard%d" % b2
                        nt_ = len(tiles)
                        W_ = nt_ * nq
                        op('act', lambda e: e.activation(out=PT_[:, 0:W_], in_=pS_[:, 0:W_], func=AF.Exp), reads=[pSk], writes=[PTk])
                        for i, (klo, vap, bap) in enumerate(tiles):
                            op('pe', lambda e, i=i, vap=vap: e.matmul(pO_[:, 0:nq], lhsT=vap, rhs=PT_[:, i * nq:(i + 1) * nq], start=(i == 0), stop=(i == nt_ - 1)),
                               reads=['nav0', 'nav1', PTk], writes=[pOk], inc=(i == nt_ - 1))
                        for i in range(nt_):
                            op('pe', lambda e, i=i: e.matmul(pD_[:, 0:nq], lhsT=onesb, rhs=PT_[:, i * nq:(i + 1) * nq], start=(i == 0), stop=(i == nt_ - 1)),
                               reads=['ctb', PTk], writes=[pDk], inc=(i == nt_ - 1))
                        op('dve', lambda e: e.reciprocal(out=rd_[:, 0:nq], in_=pD_[:, 0:nq]), reads=[pDk], writes=[rdk])
                        op('dve', lambda e: e.tensor_tensor(out=obuf[:, ocol:ocol + nq], in0=pO_[:, 0:nq], in1=rd_[:, 0:nq], op=ALU.mult),
                           reads=[pOk, rdk], writes=[obk])
                        if job.get('norm') is not None:
                            nn, t0n = job['norm']
                            block_norm_store(bn, obuf[:, 0:nn], nn, V_NAON + l * 8 + hd, 512 + hd * 128, t0n, keys=[obk])

                    ctiles = [(0, v0[:, 0:128], None), (128, v0[:, 128:256], None)]
                    jobs = []
                    oi = 0
                    ob_ = ob[oi % 2]
                    obk = "naob%d" % (oi % 2)
                    oi += 1
                    if not skip_ctx_out:
                        jobs.append({'qlo': 0, 'nq': 256, 'tiles': ctiles, 'obuf': ob_, 'ocol': 0, 'obk': obk, 'norm': (256, 0)})
                    for r in range(ROWS):
                        if r % 8 == 0:
                            ob_ = ob[oi % 2]
                            obk = "naob%d" % (oi % 2)
                            oi += 1
                        rs0 = min(max(r - 4, 0), ROWS - 8)
                        par = rs0 % 2
                        tiles = []
                        for i in range(4):
                            kr = rs0 + 2 * i
                            klo = CTX + kr * 64
                            if par == 0:
                                j = 2 + kr // 2
                                vap = v0[:, j * 128:(j + 1) * 128]
                            else:
                                j = (kr - 1) // 2
                                vap = v1[:, j * 128:(j + 1) * 128]
                            pi_ = kr - r + 7
                            tiles.append((klo, vap, nbb[:, pi_ * 64:(pi_ + 1) * 64]))
                        tiles += ctiles
                        jobs.append({'qlo': CTX + r * 64, 'nq': 64, 'tiles': tiles, 'obuf': ob_, 'ocol': (r % 8) * 64, 'obk': obk,
                                     'norm': (512, CTX + (r // 8) * 512) if r % 8 == 7 else None})
                    att_scores(jobs[0])
                    for ji in range(len(jobs)):
                        if ji + 1 < len(jobs):
                            att_scores(jobs[ji + 1])
                        att_finish(jobs[ji])
            S.barrier()

        def phase_cv(l):
            with contextlib.ExitStack() as ps:
                bn = bn_alloc(ps)
                Bt = ps.enter_context(sbuf_u("cvB", [128, T], BF16))
                Ct = ps.enter_context(sbuf_u("cvC", [128, T], BF16))
                Vt = ps.enter_context(sbuf_u("cvV", [128, T], BF16))
                cvv = ps.enter_context(sbuf_u("cvcv", [128, T], F32))
                acc = ps.enter_context(sbuf_u("cvacc", [128, T], F32))
                for g in range(N_CV):
                    for (tile_, off, k_) in ((Bt, OFF_CB, 'cvB'), (Ct, OFF_CC, 'cvC'), (Vt, OFF_CVV, 'cvV')):
                        dma('sp', tile_[:], U_F[off + g * 128:off + g * 128 + 128, :], writes=[k_])
                    wc = V_CVW + l * 12
                    op('dve', lambda e: e.tensor_tensor(out=cvv[:], in0=Ct[:], in1=Vt[:], op=ALU.mult), reads=['cvC', 'cvV'], writes=['cvcv'])
                    op('act', lambda e: e.activation(out=acc[:], in_=cvv[:], func=AF.Identity, scale=vt[:, wc + 4 + g:wc + 4 + g + 1]), reads=['cvcv', 'vt'], writes=['cvacc'])
                    for (lo, hi) in ((0, CTX), (CTX, T)):
                        op('dve', lambda e, lo=lo, hi=hi: e.scalar_tensor_tensor(out=acc[:, lo + 1:hi], in0=cvv[:, lo:hi - 1], scalar=vt[:, wc + g:wc + g + 1],
                                                                               in1=acc[:, lo + 1:hi], op0=ALU.mult, op1=ALU.add), reads=['cvcv', 'cvacc', 'vt'], writes=['cvacc'])
                        op('dve', lambda e, lo=lo, hi=hi: e.scalar_tensor_tensor(out=acc[:, lo:hi - 1], in0=cvv[:, lo + 1:hi], scalar=vt[:, wc + 8 + g:wc + 8 + g + 1],
                                                                               in1=acc[:, lo:hi - 1], op0=ALU.mult, op1=ALU.add), reads=['cvcv', 'cvacc', 'vt'], writes=['cvacc'])
                    op('dve', lambda e: e.tensor_tensor(out=acc[:], in0=acc[:], in1=Bt[:], op=ALU.mult), reads=['cvacc', 'cvB'], writes=['cvacc'])
                    for (t0, n) in [(0, 256)] + [(256 + i * 512, 512) for i in range(8)]:
                        block_norm_store(bn, acc[:, t0:t0 + n], n, V_CVON + l * 4 + g, 1536 + g * 128, t0, keys=['cvacc'])
            S.barrier()

        def make_epi_resid(Xsrc, Xdst, gi, lat_only_out=None):
            def epi(kind, *a):
                if kind == 'end':
                    return
                if kind == 'alloc':
                    ps = a[0]
                    c = {'i': 0}
                    c['xs'] = [ps.enter_context(sbuf_u("erx%d" % i, [128, 512], F32)) for i in range(3)]
                    c['tm'] = [ps.enter_context(sbuf_u("ert%d" % i, [128, 512], F32)) for i in range(3)]
                    return c
                c, (p_, pk), c0, tt, _ = a
                i = c['i'] % 3
                c['i'] += 1
                xs, tm = c['xs'][i], c['tm'][i]
                xk, tk = "erx%d" % i, "ert%d" % i
                which = 1 if tt < CTX else 0
                gb = (gi * 2 + which) * D + c0
                dma('sp', xs[:], Xsrc[tt:tt + 128, c0:c0 + 512], writes=[xk])
                op('dve', lambda e: e.tensor_tensor(out=tm[:], in0=p_[:], in1=grow[:, gb:gb + 512], op=ALU.mult), reads=[pk, 'grow'], writes=[tk])
                op('pool', lambda e: e.tensor_tensor(out=tm[:], in0=tm[:], in1=xs[:], op=ALU.add), reads=[tk, xk], writes=[tk])
                if lat_only_out is not None:
                    if tt >= CTX:
                        dma('pool', lat_only_out[tt - CTX:tt - CTX + 128, c0:c0 + 512], tm[:], reads=[tk], writes=['Xout'])
                else:
                    dma('pool', Xdst[tt:tt + 128, c0:c0 + 512], tm[:], reads=[tk], writes=['Xout'])
            return epi

        def phase_ffn_up(l, skip_ctx=False):
            with contextlib.ExitStack() as ps:
                blocks = ([] if skip_ctx else [(0, 256, 0, CTX)]) + [(CTX + o, sz, CTX, T) for (o, sz) in _split(SEQ, 1020)]
                maxn = max(b[1] for b in blocks) + 2
                ab = [ps.enter_context(sbuf_u("fua%d" % i, [128, 16 * maxn], BF16)) for i in range(2)]
                wg = [ps.enter_context(sbuf_u("fuwg%d" % i, [128, 16 * 512], BF16)) for i in range(2)]
                wvl = [ps.enter_context(sbuf_u("fuwv%d" % i, [128, 16 * 512], BF16)) for i in range(2)]
                accg = [ps.enter_context(sbuf_u("fuag%d" % i, [128, maxn], F32)) for i in range(2)]
                accv = [ps.enter_context(sbuf_u("fuav%d" % i, [128, maxn], F32)) for i in range(2)]
                stg = [ps.enter_context(sbuf_u("fust%d" % i, [128, maxn], BF16)) for i in range(2)]
                pp = [ps.enter_context(psum_u("fup%d" % i, [128, 512], F32)) for i in range(8)]
                Av = hF.rearrange("(kc p) t -> p kc t", p=128)
                Wv = wb_up[l].rearrange("(kc p) n -> p kc n", p=128)
                wkl = wkeys['up', l]
                wi = 0
                pidx = 0
                ei = 0
                fcw = V_FCW + l * 264
                fcb = V_FCB + l * 88
                for bi, (t0, n, lo, hi) in enumerate(blocks):
                    a_ = ab[bi % 2]
                    ak = "fua%d" % (bi % 2)
                    av = a_[:, 0:16 * (n + 2)].rearrange("p (kc t) -> p kc t", kc=16)
                    s0_ = max(t0 - 1, lo)
                    s1_ = min(t0 + n + 1, hi)
                    d0 = s0_ - (t0 - 1)
                    if d0 > 0 or s1_ < t0 + n + 1:
                        op('dve', lambda e: e.memset(a_[:, 0:16 * (n + 2)], 0.0), writes=[ak])
                    dma('sp', av[:, :, d0:d0 + (s1_ - s0_)], Av[:, :, s0_:s1_], writes=[ak])
                    for j in range(11):
                        wg_, wv_ = wg[wi % 2], wvl[wi % 2]
                        wgk, wvk = "fuwg%d" % (wi % 2), "fuwv%d" % (wi % 2)
                        wi += 1
                        wgv = wg_[:].rearrange("p (kc n) -> p kc n", kc=16)
                        wvv = wv_[:].rearrange("p (kc n) -> p kc n", kc=16)
                        dma('sp', wgv, Wv[:, :, j * 512:(j + 1) * 512], reads=wkl, writes=[wgk])
                        dma('sp', wvv, Wv[:, :, D_FF + j * 512:D_FF + (j + 1) * 512], reads=wkl, writes=[wvk])
                        for c4 in range(4):
                            gcx = j * 4 + c4
                            e2 = ei % 2
                            ei += 1
                            ag, avl, st_ = accg[e2], accv[e2], stg[e2]
                            agk, avk, stk = "fuag%d" % e2, "fuav%d" % e2, "fust%d" % e2
                            for (wview, wk_, acc_, acck, chn) in ((wgv, wgk, ag, agk, gcx), (wvv, wvk, avl, avk, 44 + gcx)):
                                for (o, sz) in _split(n, 510):
                                    p_ = pp[pidx % 8]
                                    pk = "fup%d" % (pidx % 8)
                                    pidx += 1
                                    for kc in range(16):
                                        op('pe', lambda e, p_=p_, kc=kc, o=o, sz=sz, wview=wview: e.matmul(
                                            p_[:, 0:sz + 2], lhsT=wview[:, kc, c4 * 128:(c4 + 1) * 128], rhs=av[:, kc, o:o + sz + 2],
                                            start=(kc == 0), stop=(kc == 15)), reads=[wk_, ak], writes=[pk], inc=(kc == 15))
                                    op('act', lambda e, p_=p_, o=o, sz=sz, acc_=acc_, chn=chn: e.activation(
                                        out=acc_[:, o:o + sz], in_=p_[:, 1:sz + 1], func=AF.Identity,
                                        scale=vt[:, fcw + 88 + chn:fcw + 88 + chn + 1], bias=vt[:, fcb + chn:fcb + chn + 1]),
                                       reads=[pk, 'vt'], writes=[acck])
                                    op('dve', lambda e, p_=p_, o=o, sz=sz, acc_=acc_, chn=chn: e.scalar_tensor_tensor(
                                        out=acc_[:, o:o + sz], in0=p_[:, 0:sz], scalar=vt[:, fcw + chn:fcw + chn + 1], in1=acc_[:, o:o + sz],
                                        op0=ALU.mult, op1=ALU.add), reads=[pk, 'vt', acck], writes=[acck])
                                    op('dve', lambda e, p_=p_, o=o, sz=sz, acc_=acc_, chn=chn: e.scalar_tensor_tensor(
                                        out=acc_[:, o:o + sz], in0=p_[:, 2:sz + 2], scalar=vt[:, fcw + 176 + chn:fcw + 176 + chn + 1], in1=acc_[:, o:o + sz],
                                        op0=ALU.mult, op1=ALU.add), reads=[pk, 'vt', acck], writes=[acck])
                            op('act', lambda e, ag=ag: e.activation(out=ag[:, 0:n], in_=ag[:, 0:n], func=AF.Silu), reads=[agk], writes=[agk])
                            op('pool', lambda e, ag=ag, avl=avl, st_=st_: e.tensor_tensor(out=st_[:, 0:n], in0=ag[:, 0:n], in1=avl[:, 0:n], op=ALU.mult),
                               reads=[agk, avk], writes=[stk])
                            dma('pool', actF[gcx * 128:(gcx + 1) * 128, t0:t0 + n], st_[:, 0:n], reads=[stk], writes=['actF'])
            S.barrier()

        tblocks_all = [(0, 256)] + [(256 + i * 1024, 1024) for i in range(4)]
        norm_groups = [(0, 2, 1)] + [(256 + i * 512, 4, 0) for i in range(8)]

        X0 = xin
        tb128 = [(0, 256)] + [(256 + i * 1024, 1024) for i in range(4)]
        tb_down = [(0, 256)] + [(256 + i * 512, 512) for i in range(8)]
        for l in range(DEPTH):
            last = (l == DEPTH - 1)
            phase_ada(l)
            if stage <= 0:
                break
            phase_norm(X0, 0, norm_groups)
            if stage <= 1:
                break
            lb_setup(l)
            gemm(hF, D, wb_in[l], wkeys['in', l], 0, 1536, tblocks_all, 'T', epi_inproj, name="ip")
            gemm(hF, D, wb_in[l], wkeys['in', l], OFF_NV, 1024, tblocks_all, 'T', epi_inproj, name="ip")
            gemm(hF, D, wb_in[l], wkeys['in', l], OFF_NK, 1024, tblocks_all, 'F', epi_inproj_qk, name="ip", npp=7)
            gemm(hF, D, wb_in[l], wkeys['in', l], OFF_HQ, 1024, tblocks_all, 'F', epi_inproj, name="ip")
            gemm(hF, D, wb_in[l], wkeys['in', l], OFF_NQ, 1024, tblocks_all, 'F', epi_inproj_qk, name="ip", npp=7)
            gemm(hF, D, wb_in[l], wkeys['in', l], OFF_CB, 1536, tblocks_all, 'F', epi_inproj, name="ip")
            if stage <= 2:
                break
            if 'hg' in phases:
                phase_hgrn(l)
            if 'na' in phases:
                phase_na(l, skip_ctx_out=last, cast_tag='mix%d' % l)
            if 'cv' in phases:
                phase_cv(l)
            if stage <= 3:
                break
            gemm(mixF, D, wb_out[l], wkeys['out', l], 0, D, tb128[1:] if last else tb128, 'T', make_epi_resid(X0, X1, 0), name="op")
            if stage <= 4:
                break
            phase_norm(X1, 1, norm_groups[1:] if last else norm_groups)
            phase_ffn_up(l, skip_ctx=last)
            if stage <= 5:
                break
            if last:
                gemm(actF, D_FF, wb_down[l], wkeys['down', l], 0, D, tb_down[1:], 'T', make_epi_resid(X1, None, 1, lat_only_out=y), name="dn", abufs=1)
            else:
                gemm(actF, D_FF, wb_down[l], wkeys['down', l], 0, D, tb_down, 'T', make_epi_resid(X1, X2, 1), name="dn", abufs=1)
            X0 = X2
            if stage <= 6 + l:
                break

        if 'dbg_ada' in dbg:
            dada = nc.dram_tensor("dbg_ada", [128, 192 + 128 + 4 * D], F32, kind="ExternalOutput").ap()
            dma('sp', dada[:, 0:192], adaT[:], reads=['adaT'])
            dma('sp', dada[:, 192:320], modv[:], reads=['modv'])
            dma('sp', dada[:, 320:], grow[:], reads=['grow'])
        S.barrier()
        S.final()
    return nc


def _flay(v):
    v = np.asarray(v, np.float32)
    return np.ascontiguousarray(v.reshape(-1, 128).T)


def _consts():
    c = np.zeros((128, NCONST), np.float32)
    c[:, C_ID:C_ID + 128] = np.eye(128, dtype=np.float32)
    c[:, C_ONES:C_ONES + 128] = 1.0
    tp = np.arange(64)[:, None]
    t = np.arange(64)[None, :]
    Mf = (tp <= t).astype(np.float32)
    Mb = (tp >= t).astype(np.float32)
    for (M, mid, oe, oc, om, occ, omr) in ((Mf, 31, C_MEXT_F, C_MC_F, C_MK_F, C_MCC_F, C_MKR_F), (Mb, 32, C_MEXT_B, C_MC_B, C_MK_B, C_MCC_B, C_MKR_B)):
        Mc = M - M[:, mid:mid + 1]
        c[:64, oe:oe + 64] = Mc
        c[:64, oe + 64] = 1.0
        c[:64, oe + 65] = M[:, mid]
        c[:64, oc:oc + 64] = Mc
        c[:64, om:om + 64] = M
        c[:64, occ:occ + 64] = Mc
        c[:64, occ + 64:occ + 128] = M - 1.0
        for rr in range(4):
            c[:64, omr + rr * 64:omr + (rr + 1) * 64] = M
    return c


def _natab(rpb):
    L, H = rpb.shape[0], rpb.shape[1]
    kc = np.arange(64)[:, None]
    c = np.arange(64)[None, :]
    ws = np.clip(c - 8, 0, 48)
    ok = (kc >= ws) & (kc < ws + 16)
    ic = np.clip(kc - c + 15, 0, 30)
    out = np.full((L, H, 2, 64, 14, 64), -1e30, np.float32)
    for pi in range(14):
        for rr in range(2):
            dr = pi - 7 + rr
            g = rpb[:, :, dr + 7, :][:, :, ic]
            out[:, :, rr, :, pi, :] = np.where(ok[None, None], g, np.float32(-1e30))
    return np.ascontiguousarray(out.reshape(L, H, 128, 14 * 64))


def _vecs(inp, b):
    v = np.zeros((128, NVEC), np.float32)
    for l in range(DEPTH):
        v[:, V_BADA + l * 96:V_BADA + (l + 1) * 96] = _flay(inp['b_ada'][l])
        v[:, V_LN1 + l * 16:V_LN1 + (l + 1) * 16] = _flay(inp['ln1_w'][l])
        v[:, V_LN2 + l * 16:V_LN2 + (l + 1) * 16] = _flay(inp['ln2_w'][l])
        v[:, V_HGN + l] = inp['hg_norm_w'][l]
        v[:, V_QN + l] = inp['na_q_norm_w'][l]
        v[:, V_KN + l] = inp['na_k_norm_w'][l]
        v[:, V_NAON + l * 8:V_NAON + (l + 1) * 8] = _flay(inp['na_out_norm_w'][l])
        for t in range(3):
            v[:, V_CVW + l * 12 + t * 4:V_CVW + l * 12 + (t + 1) * 4] = _flay(inp['cv_w'][l, t])
            v[:, V_FCW + l * 264 + t * 88:V_FCW + l * 264 + (t + 1) * 88] = _flay(inp['ffn_conv_w'][l, t])
        v[:, V_CVON + l * 4:V_CVON + (l + 1) * 4] = _flay(inp['cv_out_norm_w'][l])
        v[:, V_FCB + l * 88:V_FCB + (l + 1) * 88] = _flay(inp['ffn_conv_b'][l])
    cc = np.stack([_flay(inp['c'][b]), _flay(inp['c_ctx'])], axis=-1)
    v[:, V_CT:V_CT + 32] = cc.reshape(128, 32)
    return v


def make_in_maps(inp, cores):
    inp = {k: np.asarray(v) for k, v in inp.items()}
    consts = _consts()
    natab = _natab(inp['na_rpb'].astype(np.float32))
    lbl = np.ascontiguousarray(np.broadcast_to(inp['hg_lb_logits'].astype(np.float32).reshape(1, -1), (64, 2 * DEPTH * 512)))
    maps = []
    for i in cores:
        b = i % 4
        m = {
            'xin': np.ascontiguousarray(np.concatenate([inp['ctx'][b], inp['x'][b]], axis=0).astype(np.float32)),
            'vecs': _vecs(inp, b), 'consts': consts, 'lbl': lbl, 'natab': natab,
            'w_ada': inp['w_ada'], 'w_in': inp['w_in'], 'w_out': inp['w_out'], 'w_up': inp['w_up'], 'w_down': inp['w_down'],
        }
        maps.append(m)
    return maps


def kernel(**inputs):
    nc = build()
    cores = list(range(8))
    maps = make_in_maps(inputs, cores)
    res = run_bass_kernel_spmd(nc, maps, core_ids=cores)
    return np.stack([np.asarray(res.results[b]['y'], np.float32) for b in range(4)], axis=0)
```
